# Optimizing a Trainium2 kernel written in Bass

```python
import math
import jax, jax.numpy as jnp
from jax import lax
import numpy as np

D_MODEL = 1024
BATCH = 8
SEQ = 2048
DEPTH = 2
DEC_BATCH = 4
DEC_SEQ = 8192
PAST_LEN = 128

N_MIXERS = 2
N_ATTN_LAYERS = (DEPTH + 1) // 2
N_RWKV_LAYERS = DEPTH // 2
RMS_EPS = 1e-6
DA_HEADS = 8
DA_QK_DIM = D_MODEL // (2 * DA_HEADS)
DA_V_DIM = 2 * DA_QK_DIM
ROPE_THETA = 500000.0
ROPE_DIM = DA_QK_DIM // 4
Q_BLOCK = 128
SUBLN_EPS = 1e-5
RW_HEAD = 64
RW_HEADS = D_MODEL // RW_HEAD
DECAY_LORA = max(32, int(round(1.8 * D_MODEL ** 0.5 / 32)) * 32)
AAA_LORA = max(32, int(round(1.8 * D_MODEL ** 0.5 / 32)) * 32)
GATE_LORA = max(32, int(round(0.6 * D_MODEL ** 0.8 / 32)) * 32)
GN_EPS = 64e-5
FFN_HIDDEN = -(-8 * D_MODEL // (3 * 256)) * 256

kernel_name = "diffattn_rwkv7_bidir_hybrid"


def rmsnorm(x, w, eps=RMS_EPS):
    xf = x.astype(jnp.float32)
    y = xf * lax.rsqrt(jnp.mean(xf * xf, axis=-1, keepdims=True) + eps)
    return (y * w.astype(jnp.float32)).astype(x.dtype)


def rotary_tables(seq_len):
    inv_freq = ROPE_THETA ** (-jnp.arange(0, ROPE_DIM, 2, dtype=jnp.float32) / ROPE_DIM)
    ang = jnp.arange(seq_len, dtype=jnp.float32)[:, None] * inv_freq[None, :]
    return jnp.cos(ang), jnp.sin(ang)


def rotary_partial(t, cos, sin):
    half = ROPE_DIM // 2
    c = cos[None, :, None, None, :]
    s = sin[None, :, None, None, :]
    t1 = t[..., :half].astype(jnp.float32)
    t2 = t[..., half:ROPE_DIM].astype(jnp.float32)
    rot = jnp.concatenate([t1 * c - t2 * s, t2 * c + t1 * s], axis=-1).astype(t.dtype)
    return jnp.concatenate([rot, t[..., ROPE_DIM:]], axis=-1)


def diff_attention(h, w_qkv, lq1, lk1, lq2, lk2, subln_w, w_o, lambda_init):
    B, S, _ = h.shape
    q, k, v = jnp.split(h @ w_qkv, 3, axis=-1)
    q = q.reshape(B, S, DA_HEADS, 2, DA_QK_DIM)
    k = k.reshape(B, S, DA_HEADS, 2, DA_QK_DIM)
    v = v.reshape(B, S, DA_HEADS, DA_V_DIM).astype(jnp.float32)
    cos, sin = rotary_tables(S)
    q = rotary_partial(q, cos, sin)
    k = rotary_partial(k, cos, sin)
    f32 = jnp.float32
    lam = (jnp.exp(jnp.sum(lq1.astype(f32) * lk1.astype(f32)))
           - jnp.exp(jnp.sum(lq2.astype(f32) * lk2.astype(f32))) + lambda_init)
    scale = DA_QK_DIM ** -0.5
    nb = S // Q_BLOCK
    qb = jnp.moveaxis(q.reshape(B, nb, Q_BLOCK, DA_HEADS, 2, DA_QK_DIM), 1, 0)

    def block(qi):
        s = jnp.einsum('bqhcd,bkhcd->bhcqk', qi, k).astype(f32) * scale
        p = jax.nn.softmax(s, axis=-1)
        pd = p[:, :, 0] - lam * p[:, :, 1]
        return jnp.einsum('bhqk,bkhd->bqhd', pd, v)

    o = jnp.moveaxis(lax.map(block, qb), 0, 1).reshape(B, S, DA_HEADS, DA_V_DIM)
    o = o * lax.rsqrt(jnp.mean(o * o, axis=-1, keepdims=True) + SUBLN_EPS)
    o = o * subln_w.astype(f32) * (1.0 - lambda_init)
    return o.reshape(B, S, DA_HEADS * DA_V_DIM).astype(h.dtype) @ w_o


def _dirs_time_major(t):
    t = jnp.stack([t[0], t[1][:, ::-1]])
    return jnp.moveaxis(t, 2, 0)


def _wkv7_step(state, inp):
    r_t, w_t, k_t, v_t, kk_t, a_t = inp
    sa = jnp.einsum('zbhij,zbhj->zbhi', state, -kk_t)
    state = (state * w_t[..., None, :] + sa[..., None] * (kk_t * a_t)[..., None, :]
             + v_t[..., None] * k_t[..., None, :])
    return state, jnp.einsum('zbhij,zbhj->zbhi', state, r_t)


def rwkv7_bidir(h, mu, w_rkv, w0, w1, w2, a0, a1, a2, g1, g2, k_k, k_a, r_k, ln_w, ln_b, w_out):
    B, T, D = h.shape
    f32 = jnp.float32
    zero = jnp.zeros_like(h[:, :1])
    prev = jnp.concatenate([zero, h[:, :-1]], axis=1)
    nxt = jnp.concatenate([h[:, 1:], zero], axis=1)
    xx = 0.5 * (prev + nxt) - h
    xr, xk, xv, xw, xa, xg = [h + xx * mu[i] for i in range(6)]
    r, k, v = jnp.einsum('zbtd,zde->zbte', jnp.stack([xr, xk, xv]), w_rkv)
    wl = jnp.einsum('zbtl,zld->zbtd', jnp.tanh(jnp.einsum('btd,zdl->zbtl', xw, w1)), w2)
    wlog = -jax.nn.softplus(-(w0[:, None, None, :] + wl).astype(f32)) - 0.5
    decay = jnp.exp(-jnp.exp(wlog))
    al = jnp.einsum('zbtl,zld->zbtd', jnp.einsum('btd,zdl->zbtl', xa, a1), a2)
    a = jax.nn.sigmoid((a0[:, None, None, :] + al).astype(f32))
    g = jax.nn.sigmoid(xg @ g1) @ g2
    hs = (B, T, RW_HEADS, RW_HEAD)
    r_h = r.astype(f32).reshape(hs)
    v_h = v.astype(f32).reshape(hs)
    k32 = k.astype(f32)
    kk = (k32 * k_k.astype(f32)).reshape(hs)
    kk = kk / jnp.maximum(jnp.sqrt(jnp.sum(kk * kk, axis=-1, keepdims=True)), 1e-12)
    kd = (k32[None] * (1.0 + (a - 1.0) * k_a.astype(f32))).reshape((2,) + hs)
    a_h = a.reshape((2,) + hs)
    dec_h = decay.reshape((2,) + hs)
    xs = (_dirs_time_major(jnp.stack([r_h, r_h])), _dirs_time_major(dec_h),
          _dirs_time_major(kd), _dirs_time_major(jnp.stack([v_h, v_h])),
          _dirs_time_major(jnp.stack([kk, kk])), _dirs_time_major(a_h))
    s0 = jnp.zeros((2, B, RW_HEADS, RW_HEAD, RW_HEAD), f32)
    _, ys = lax.scan(_wkv7_step, s0, xs)
    ys = jnp.moveaxis(ys, 0, 2)
    y = ys[0] + ys[1][:, ::-1]
    mean = jnp.mean(y, axis=-1, keepdims=True)
    var = jnp.mean((y - mean) ** 2, axis=-1, keepdims=True)
    yn = ((y - mean) * lax.rsqrt(var + GN_EPS)).reshape(B, T, D)
    yn = yn * ln_w.astype(f32) + ln_b.astype(f32)
    rk = r_k.astype(f32).reshape(RW_HEADS, RW_HEAD)
    coef = jnp.sum(jnp.sum(r_h[None] * kd * rk, axis=-1, keepdims=True), axis=0)
    yb = yn + (coef * v_h).reshape(B, T, D)
    return (yb * g.astype(f32)).astype(h.dtype) @ w_out


def swiglu(h, w_in, w_out):
    gate, up = jnp.split(h @ w_in, 2, axis=-1)
    return (jax.nn.silu(gate) * up) @ w_out


def trunk(x, p):
    for i in range(DEPTH):
        j = i // N_MIXERS
        hn = rmsnorm(x, p['norm_mix'][i])
        if i % N_MIXERS == 0:
            lambda_init = 0.8 - 0.6 * math.exp(-0.3 * i)
            x = x + diff_attention(hn, p['attn_w_qkv'][j], p['attn_lambda_q1'][j], p['attn_lambda_k1'][j],
                                   p['attn_lambda_q2'][j], p['attn_lambda_k2'][j], p['attn_subln_w'][j],
                                   p['attn_w_o'][j], lambda_init)
        else:
            x = x + rwkv7_bidir(hn, p['rwkv_mu'][j], p['rwkv_w_rkv'][j], p['rwkv_w0'][j], p['rwkv_w1'][j],
                                p['rwkv_w2'][j], p['rwkv_a0'][j], p['rwkv_a1'][j], p['rwkv_a2'][j],
                                p['rwkv_g1'][j], p['rwkv_g2'][j], p['rwkv_k_k'][j], p['rwkv_k_a'][j],
                                p['rwkv_r_k'][j], p['rwkv_ln_w'][j], p['rwkv_ln_b'][j], p['rwkv_w_out'][j])
        x = x + swiglu(rmsnorm(x, p['norm_ffn'][i]), p['ffn_w_in'][i], p['ffn_w_out'][i])
    return rmsnorm(x, p['norm_final'])


def setup_inputs(seed: int = 0) -> dict:
    key = jax.random.key(seed)
    ks = iter(jax.random.split(key, 40))

    def nrm(shape, scale):
        return scale * jax.random.normal(next(ks), shape, jnp.float32)

    D, NA, NR, F = D_MODEL, N_ATTN_LAYERS, N_RWKV_LAYERS, FFN_HIDDEN
    return {
        'x_prompt': nrm((BATCH, SEQ, D), 1.0),
        'x_sample': nrm((DEC_BATCH, DEC_SEQ, D), 1.0),
        'norm_mix': 1.0 + nrm((DEPTH, D), 0.02),
        'norm_ffn': 1.0 + nrm((DEPTH, D), 0.02),
        'norm_final': 1.0 + nrm((D,), 0.02),
        'attn_w_qkv': nrm((NA, D, 3 * D), D ** -0.5),
        'attn_lambda_q1': nrm((NA, DA_QK_DIM), 0.1),
        'attn_lambda_k1': nrm((NA, DA_QK_DIM), 0.1),
        'attn_lambda_q2': nrm((NA, DA_QK_DIM), 0.1),
        'attn_lambda_k2': nrm((NA, DA_QK_DIM), 0.1),
        'attn_subln_w': 1.0 + nrm((NA, DA_V_DIM), 0.02),
        'attn_w_o': nrm((NA, DA_HEADS * DA_V_DIM, D), (DA_HEADS * DA_V_DIM) ** -0.5),
        'rwkv_mu': jax.random.uniform(next(ks), (NR, 6, D), jnp.float32),
        'rwkv_w_rkv': nrm((NR, 3, D, D), D ** -0.5),
        'rwkv_w0': nrm((NR, 2, D), 0.5),
        'rwkv_w1': nrm((NR, 2, D, DECAY_LORA), D ** -0.5),
        'rwkv_w2': nrm((NR, 2, DECAY_LORA, D), 0.1 * DECAY_LORA ** -0.5),
        'rwkv_a0': nrm((NR, 2, D), 0.1),
        'rwkv_a1': nrm((NR, 2, D, AAA_LORA), D ** -0.5),
        'rwkv_a2': nrm((NR, 2, AAA_LORA, D), 0.1 * AAA_LORA ** -0.5),
        'rwkv_g1': nrm((NR, D, GATE_LORA), D ** -0.5),
        'rwkv_g2': nrm((NR, GATE_LORA, D), GATE_LORA ** -0.5),
        'rwkv_k_k': 0.85 + nrm((NR, D), 0.05),
        'rwkv_k_a': 1.0 + nrm((NR, D), 0.05),
        'rwkv_r_k': nrm((NR, D), 0.1),
        'rwkv_ln_w': 1.0 + nrm((NR, D), 0.02),
        'rwkv_ln_b': nrm((NR, D), 0.01),
        'rwkv_w_out': nrm((NR, D, D), D ** -0.5),
        'ffn_w_in': nrm((DEPTH, D, 2 * F), D ** -0.5),
        'ffn_w_out': nrm((DEPTH, F, D), F ** -0.5),
    }


def reference(x_prompt, x_sample, norm_mix, norm_ffn, norm_final,
              attn_w_qkv, attn_lambda_q1, attn_lambda_k1, attn_lambda_q2, attn_lambda_k2,
              attn_subln_w, attn_w_o,
              rwkv_mu, rwkv_w_rkv, rwkv_w0, rwkv_w1, rwkv_w2, rwkv_a0, rwkv_a1, rwkv_a2,
              rwkv_g1, rwkv_g2, rwkv_k_k, rwkv_k_a, rwkv_r_k, rwkv_ln_w, rwkv_ln_b, rwkv_w_out,
              ffn_w_in, ffn_w_out):
    params = dict(norm_mix=norm_mix, norm_ffn=norm_ffn, norm_final=norm_final,
                  attn_w_qkv=attn_w_qkv, attn_lambda_q1=attn_lambda_q1, attn_lambda_k1=attn_lambda_k1,
                  attn_lambda_q2=attn_lambda_q2, attn_lambda_k2=attn_lambda_k2,
                  attn_subln_w=attn_subln_w, attn_w_o=attn_w_o,
                  rwkv_mu=rwkv_mu, rwkv_w_rkv=rwkv_w_rkv, rwkv_w0=rwkv_w0, rwkv_w1=rwkv_w1,
                  rwkv_w2=rwkv_w2, rwkv_a0=rwkv_a0, rwkv_a1=rwkv_a1, rwkv_a2=rwkv_a2,
                  rwkv_g1=rwkv_g1, rwkv_g2=rwkv_g2, rwkv_k_k=rwkv_k_k, rwkv_k_a=rwkv_k_a,
                  rwkv_r_k=rwkv_r_k, rwkv_ln_w=rwkv_ln_w, rwkv_ln_b=rwkv_ln_b, rwkv_w_out=rwkv_w_out,
                  ffn_w_in=ffn_w_in, ffn_w_out=ffn_w_out)
    y_prompt = trunk(x_prompt, params)
    y_sample = trunk(x_sample, params)
    return (y_prompt, y_sample)
```

```python
import math
from contextlib import ExitStack

import numpy as np
import concourse.bass as bass
import concourse.mybir as mybir
from concourse.bass_utils import run_bass_kernel_spmd

F32 = mybir.dt.float32
BF16 = mybir.dt.bfloat16
AF = mybir.ActivationFunctionType
ALU = mybir.AluOpType
AX = mybir.AxisListType

D = 1024
KC = 8
FH = 2816
FJ = 22
ROPE_THETA = 500000.0
SAME_ENG_SYNC = True
ENGS = ('pe', 'act', 'dve', 'pool', 'sp')

V_NMIX0, V_NMIX1, V_NFFN0, V_NFFN1, V_NFIN = 0, 1, 2, 3, 4
V_MU = 5
V_W0 = 11
V_A0 = 13
V_KK, V_KA, V_RK, V_LNW, V_LNB = 15, 16, 17, 18, 19
V_OMKA = 20
NV = 21


class Cfg:
    def __init__(self, TP=2048, TS=4096):
        self.TP, self.TS, self.TF = TP, TS, TS
        self.NOWN = TP + TS
        self.NALL = TP + 2 * TS
        self.NTP, self.NTS = TP // 512, TS // 512
        self.NT_OWN = self.NOWN // 512
        self.NT_ALL = self.NALL // 512
        self.XHW = self.NOWN + 4


class Rec:
    def __init__(self, nc):
        self.nc = nc
        self.lists = {e: [] for e in ENGS}
        self.cnt = {e: 0 for e in ENGS}
        self.lastw = {}
        self.readers = {}
        self.waited = {e: {} for e in ENGS}
        self.dcnt = {}
        self.semrefs = set()
        self.stgi = 0
        self.semmap = {}

    def add(self, eng, fn, reads=(), writes=(), dma=None):
        if dma is not None:
            dma = self.semmap.setdefault(dma, 'g%d' % len(self.semmap))
        deps = {}
        for k in reads:
            t = self.lastw.get(k)
            if t and deps.get(t[0], 0) < t[1]:
                deps[t[0]] = t[1]
        for k in writes:
            t = self.lastw.get(k)
            if t and deps.get(t[0], 0) < t[1]:
                deps[t[0]] = t[1]
            r = self.readers.get(k)
            if r:
                for s, v in r.items():
                    if deps.get(s, 0) < v:
                        deps[s] = v
        w = self.waited[eng]
        own = ('e', eng)
        for s in list(deps):
            if s[0] == 'd':
                deps[s] = self.dcnt[s]
        for s, v in deps.items():
            if s == own and (eng == 'pe' or eng == 'sp' or not SAME_ENG_SYNC):
                continue
            if w.get(s, 0) >= v:
                continue
            w[s] = v
            self.semrefs.add(s)
            self.lists[eng].append(('w', s, v))
        if dma is None:
            self.cnt[eng] += 1
            tok = (own, self.cnt[eng])
            self.lists[eng].append(('i', fn, own, 1))
        else:
            ref = ('d', dma)
            self.dcnt[ref] = self.dcnt.get(ref, 0) + 16
            tok = (ref, self.dcnt[ref])
            self.lists[eng].append(('i', fn, ref, 16))
        self.semrefs.add(tok[0])
        for k in writes:
            self.lastw[k] = tok
            self.readers[k] = {}
        for k in reads:
            d = self.readers.setdefault(k, {})
            if d.get(tok[0], 0) < tok[1]:
                d[tok[0]] = tok[1]
        return tok

    def barrier(self):
        for e in ENGS:
            w = self.waited[e]
            for x in ENGS:
                s = ('e', x)
                if x != e and self.cnt[x] > w.get(s, 0):
                    w[s] = self.cnt[x]
                    self.lists[e].append(('w', s, self.cnt[x]))
            for s, v in self.dcnt.items():
                if v > w.get(s, 0):
                    w[s] = v
                    self.lists[e].append(('w', s, v))
        self.lastw = {}
        self.readers = {}
        self.semmap = {}

    def mm(self, out, lhsT, rhs, st, sp, r, w):
        self.add('pe', lambda e: e.matmul(out, lhsT, rhs, start=st, stop=sp), r, w)

    def tr(self, out, in_, ident, r, w):
        self.add('pe', lambda e: e.transpose(out, in_, ident), r, w)

    def act(self, out, in_, func, r, w, bias=None, scale=None):
        kw = {}
        if bias is not None:
            kw['bias'] = bias
        if scale is not None:
            kw['scale'] = scale
        self.add('act', lambda e: e.activation(out=out, in_=in_, func=func, **kw), r, w)

    def tt(self, eng, out, in0, in1, op, r, w):
        self.add(eng, lambda e: e.tensor_tensor(out=out, in0=in0, in1=in1, op=op), r, w)

    def ts(self, eng, out, in0, s1, s2, op0, op1, r, w):
        self.add(eng, lambda e: e.tensor_scalar(out=out, in0=in0, scalar1=s1, scalar2=s2, op0=op0, op1=op1), r, w)

    def ts1(self, eng, out, in0, s1, op0, r, w):
        self.add(eng, lambda e: e.tensor_single_scalar(out=out, in_=in0, scalar=s1, op=op0), r, w)

    def stt(self, out, in0, scalar, in1, op0, op1, r, w):
        self.add('dve', lambda e: e.scalar_tensor_tensor(out=out, in0=in0, scalar=scalar, in1=in1, op0=op0, op1=op1), r, w)

    def cp(self, eng, out, in_, r, w):
        if eng == 'act':
            self.add('act', lambda e: e.activation(out=out, in_=in_, func=AF.Copy), r, w)
        else:
            self.add(eng, lambda e: e.tensor_copy(out=out, in_=in_), r, w)

    def scp(self, eng, out, in_, sc, r, w):
        if eng == 'act':
            self.add('act', lambda e: e.activation(out=out, in_=in_, func=AF.Copy, scale=sc), r, w)
        else:
            self.add(eng, lambda e: e.tensor_scalar_mul(out=out, in0=in_, scalar1=sc), r, w)

    def recip(self, out, in_, r, w):
        self.add('dve', lambda e: e.reciprocal(out=out, in_=in_), r, w)

    def red(self, out, in_, op, r, w):
        self.add('dve', lambda e: e.tensor_reduce(out=out, in_=in_, axis=AX.X, op=op), r, w)

    def scan(self, out, d0, d1, r, w):
        self.add('dve', lambda e: e.tensor_tensor_scan(out=out, data0=d0, data1=d1, initial=0.0,
                                                       op0=ALU.mult, op1=ALU.add), r, w)

    def memset(self, eng, ap, val, w):
        self.add(eng, lambda e: e.memset(ap, val), (), w)

    def dma(self, q, out, in_, sem, r, w, slow=False):
        if slow:
            self.add(q, lambda e: e.dma_start(out=out, in_=in_, allow_slow_non_contiguous=True), r, w, dma=sem)
        else:
            self.add(q, lambda e: e.dma_start(out=out, in_=in_), r, w, dma=sem)


class Defer:
    def __init__(self):
        self.q = []

    def __getattr__(self, name):
        return lambda *a, **kw: self.q.append((name, a, kw))


class _Probe(Rec):
    def __init__(self):
        self.meta = None

    def add(self, eng, fn, reads=(), writes=(), dma=None):
        self.meta = (eng, tuple(reads), tuple(writes), dma)


def _free_size(a):
    for x in a:
        sh = getattr(x, 'shape', None)
        if sh is not None and len(sh) >= 2:
            n = 1
            for v in sh[1:]:
                n *= int(v)
            return n
    return 64


def list_schedule(R, ops):
    import heapq
    probe = _Probe()
    n = len(ops)
    eng_of = [None] * n
    dur = [0.0] * n
    lat = [0.0] * n
    succ = [[] for _ in range(n)]
    npred = [0] * n
    lastw = {}
    readers = {}
    for i, (name, a, kw) in enumerate(ops):
        getattr(probe, name)(*a, **kw)
        eng, rd, wr, dma = probe.meta
        eng_of[i] = eng
        N = _free_size(a)
        if dma is not None:
            dur[i], lat[i] = 80.0, 2200.0
        elif eng == 'pe':
            dur[i] = 60.0 + 0.45 * N
        elif eng == 'act':
            dur[i] = 230.0 + 0.72 * N
        elif eng == 'dve':
            dur[i] = 70.0 + 1.1 * N
        else:
            dur[i] = 120.0 + 1.25 * N
        preds = set()
        for k_ in rd:
            t = lastw.get(k_)
            if t is not None:
                preds.add(t)
        for k_ in wr:
            t = lastw.get(k_)
            if t is not None:
                preds.add(t)
            for t in readers.get(k_, ()):
                preds.add(t)
        preds.discard(i)
        for p_ in preds:
            succ[p_].append(i)
        npred[i] = len(preds)
        for k_ in wr:
            lastw[k_] = i
            readers[k_] = []
        for k_ in rd:
            readers.setdefault(k_, []).append(i)
    SYNC = 350.0
    future = {e: [] for e in ENGS}
    avail = {e: [] for e in ENGS}
    rtime = [0.0] * n
    efree = {e: 0.0 for e in ENGS}
    for i in range(n):
        if npred[i] == 0:
            heapq.heappush(avail[eng_of[i]], i)
    order = []
    while len(order) < n:
        best = None
        for e in ENGS:
            f, av = future[e], avail[e]
            while f and f[0][0] <= efree[e]:
                heapq.heappush(av, heapq.heappop(f)[1])
            if av:
                c = (efree[e], av[0], e, True)
            elif f:
                c = (f[0][0], f[0][1], e, False)
            else:
                continue
            if best is None or c[:2] < best[:2]:
                best = c
        st_, i, e, from_av = best
        if from_av:
            heapq.heappop(avail[e])
        else:
            heapq.heappop(future[e])
        efree[e] = st_ + dur[i]
        fin_i = st_ + dur[i] + lat[i]
        order.append(i)
        for s_ in succ[i]:
            npred[s_] -= 1
            t_ = fin_i + (SYNC if (eng_of[s_] != e or lat[i]) else 0.0)
            if t_ > rtime[s_]:
                rtime[s_] = t_
            if npred[s_] == 0:
                heapq.heappush(future[eng_of[s_]], (rtime[s_], s_))
    for i in order:
        name, a, kw = ops[i]
        getattr(R, name)(*a, **kw)


class Arena:
    def __init__(self, sb, cap):
        self.sb, self.cap, self.off, self.base = sb, cap, 0, 0

    def alloc(self, shape, dt):
        n = int(np.prod(shape))
        sz = 4 if dt == F32 else 2
        nb = (n * sz + 63) // 64 * 64
        assert self.off + nb <= self.cap, ("SBUF arena overflow", self.off, nb, self.cap)
        a = self.sb[:, self.off // 4:(self.off + nb) // 4]
        self.off += nb
        if dt == BF16:
            a = a.bitcast(BF16)
        a = a[:, 0:n]
        if len(shape) == 2:
            a = a.rearrange("p (a b) -> p a b", b=shape[1])
        elif len(shape) == 3:
            a = a.rearrange("p (a b c) -> p a b c", b=shape[1], c=shape[2])
        return a

    def set_base(self):
        self.base = self.off

    def reset(self):
        self.off = self.base


def fm(ap):
    return ap.rearrange("(k p) t -> p k t", p=128)


def tm(ap):
    return ap.rearrange("(n p) c -> p n c", p=128)


class K:
    pass


def load_w(k, dst, src, nk, ncols, name, scale=None, last_rows=128, cb=1024):
    cb = k.stgw
    R = k.R
    for kc in range(nk):
        rows = 128 if kc < nk - 1 else last_rows
        for c0 in range(0, ncols, cb):
            c1 = min(ncols, c0 + cb)
            b = R.stgi % len(k.stg)
            R.stgi += 1
            st = k.stg[b][0:rows, 0:c1 - c0]
            R.dma('sp' if R.stgi % 2 else 'act', st, src[kc * 128:kc * 128 + rows, c0:c1], 'stg%d' % b, [], [('stg', b)])
            eng = 'act' if (R.stgi % 2) else 'dve'
            o = dst[0:rows, kc, c0:c1]
            if scale is not None:
                R.scp(eng, o, st, scale(kc)[0:rows], [('stg', b)], [(name, kc, c0 // cb)])
            else:
                R.cp(eng, o, st, [('stg', b)], [(name, kc, c0 // cb)])


def wkeys(name, kc, c0, c1, cb=None):
    cb = cb or K.stgw
    return [(name, kc, j) for j in range(c0 // cb, (c1 - 1) // cb + 1)]


def vcol(k, v, kc):
    return k.vecs[:, v, kc:kc + 1]


def rms_rstd(k, xT, xkey, nchunks, inv_n, eps, psb, sq, rs, rstd, tagsq):
    R = k.R
    for kc in range(nchunks):
        s = sq[kc % 2]
        R.act(s, xT[:, kc, :], AF.Square, [(xkey, kc)], [(tagsq, kc % 2)])
        R.mm(k.ps[psb], k.ones_bf, s, kc == 0, kc == nchunks - 1, [(tagsq, kc % 2)], [('ps', psb)])
    R.act(rs, k.ps[psb], AF.Sqrt, [('ps', psb)], [tagsq + 'rs'], bias=eps, scale=inv_n)
    R.recip(rstd, rs, [tagsq + 'rs'], [tagsq + 'rstd'])


def phase_consts(k):
    R, ar, cfg = k.R, k.ar, k.cfg
    k.cmf = ar.alloc([3, 128], F32)
    k.cmb = ar.alloc([3, 128], BF16)
    k.vecs = ar.alloc([NV, 8], F32)
    k.sel = ar.alloc([2], F32)
    k.sublnw = ar.alloc([1], F32)
    k.lamt = ar.alloc([4], F32)
    k.qkmax = ar.alloc([2, 2, 8], F32)
    k.negM = ar.alloc([2, 8], F32)
    k.lamneg = ar.alloc([1], F32)
    k.zero_bf = ar.alloc([8], BF16)
    k.eps5 = ar.alloc([1], F32)
    ar.set_base()
    R.dma('sp', k.cmf, k.d['cmat'], 'c0', [], ['cmf'])
    R.dma('sp', k.vecs[:, 0:NV - 1, :], k.d['vecs'], 'c1', [], ['vecs'])
    R.dma('sp', k.sel, k.d['sel'], 'c4', [], ['sel'])
    R.dma('sp', k.sublnw, k.d['sublnw'], 'c5', [], ['sublnw'])
    R.dma('sp', k.lamt[0:64, :], k.d['lamt'], 'c6', [], ['lamt'])
    R.cp('dve', k.cmb, k.cmf, ['cmf'], ['cmb'])
    R.memset('dve', k.qkmax, 0.0, ['qkmax'])
    R.memset('dve', k.zero_bf, 0.0, ['zero_bf'])
    R.memset('dve', k.eps5, 1e-5, ['eps5'])
    R.ts('dve', k.vecs[:, V_OMKA, :], k.vecs[:, V_KA, :], -1.0, 1.0, ALU.mult, ALU.add, ['vecs'], ['vecs'])
    k.ident_f = k.cmf[:, 0, :]
    k.ones_f = k.cmf[:, 1, :]
    k.ident_bf = k.cmb[:, 0, :]
    k.ones_bf = k.cmb[:, 1, :]
    k.onesbd_bf = k.cmb[:, 2, :]
    R.barrier()


def phase_A(k):
    R, ar, cfg, d = k.R, k.ar, k.cfg, k.d
    ar.reset()
    Wqkv = ar.alloc([8, 3072], BF16)
    Wqks = ar.alloc([8, 2048], BF16)
    k.stg = [ar.alloc([1024], F32) for _ in range(4)]
    K.stgw = k.stgw = 1024
    load_w(k, Wqkv, d['w_qkv'], 8, 3072, 'Wqkv', scale=lambda kc: vcol(k, V_NMIX0, kc))
    load_w(k, Wqks, d['w_qks'], 8, 2048, 'Wqks', scale=lambda kc: vcol(k, V_NMIX0, kc))
    xtok = [ar.alloc([4, 1024], F32) for _ in range(2)]
    xT = ar.alloc([8, 512], F32)
    sq = [ar.alloc([512], BF16) for _ in range(2)]
    rs = ar.alloc([512], F32)
    rstd = ar.alloc([512], F32)
    xn = ar.alloc([8, 512], BF16)
    Ct = [ar.alloc([512], F32) for _ in range(2)]
    St = [ar.alloc([512], F32) for _ in range(2)]
    t1 = [ar.alloc([512], F32) for _ in range(2)]
    t2 = [ar.alloc([512], F32) for _ in range(2)]
    qr = [ar.alloc([512], BF16) for _ in range(3)]
    sq2 = [ar.alloc([512], BF16) for _ in range(2)]
    mx = [ar.alloc([1], F32) for _ in range(2)]
    vtok = ar.alloc([4, 1024], BF16)
    xin = tm(d['x_in'])
    XT = fm(d['XT'])

    def loads(i):
        b = i % 2
        R.dma('sp', xtok[b], xin[:, 4 * i:4 * i + 4, :], 'xtok%d' % b, [], [('xtok', b)])
        R.dma('sp', Ct[b], d['ropeC'][:, i * 512:(i + 1) * 512], 'rope%d' % b, [], [('Ct', b)])
        R.dma('sp', St[b], d['ropeS'][:, i * 512:(i + 1) * 512], 'rope%d' % b, [], [('St', b)])

    R_real = R
    R = k.R = Defer()
    loads(0)
    cnt = 0
    pend = []
    for i in range(cfg.NT_ALL):
        own = i < cfg.NT_OWN
        seq = 0 if i < cfg.NTP else 1
        b = i % 2
        if i + 1 < cfg.NT_ALL:
            loads(i + 1)
        for kc in range(8):
            pb = kc % 2
            for n in range(4):
                R.tr(k.ps[pb][:, n * 128:(n + 1) * 128], xtok[b][:, n, kc * 128:(kc + 1) * 128], k.ident_f,
                     [('xtok', b)], [('ps', pb)])
            R.cp('act' if kc % 2 else 'dve', xT[:, kc, :], k.ps[pb], [('ps', pb)], [('xT', kc)])
        if own:
            R.dma('pool', XT[:, :, i * 512:(i + 1) * 512], xT, 'xTst', [('xT', kc) for kc in range(8)], [])
        rms_rstd(k, xT, 'xT', 8, 1.0 / D, 1e-6, 2, sq, rs, rstd, 'A')
        for kc in range(8):
            R.tt('pool' if kc % 2 else 'dve', xn[:, kc, :], xT[:, kc, :], rstd, ALU.mult,
                 [('xT', kc), 'Arstd'], [('xn', kc)])
        xnk = [('xn', kc) for kc in range(8)]
        for which in ((0, 1) if own else (1,)):
            for hc in range(8):
                pa, pb2 = (3, 4) if cnt % 2 == 0 else (5, 6)
                c0 = which * 1024 + hc * 128
                for kc in range(8):
                    R.mm(k.ps[pa], Wqkv[:, kc, c0:c0 + 128], xn[:, kc, :], kc == 0, kc == 7,
                         [('xn', kc)] + wkeys('Wqkv', kc, c0, c0 + 128), [('ps', pa)])
                for kc in range(8):
                    R.mm(k.ps[pb2], Wqks[:, kc, c0:c0 + 128], xn[:, kc, :], kc == 0, kc == 7,
                         [('xn', kc)] + wkeys('Wqks', kc, c0, c0 + 128), [('ps', pb2)])
                tb = cnt % 2
                R.tt('dve', t1[tb], k.ps[pa], Ct[b], ALU.mult, [('ps', pa), ('Ct', b)], [('t1', tb)])
                R.tt('dve', t2[tb], k.ps[pb2], St[b], ALU.mult, [('ps', pb2), ('St', b)], [('t2', tb)])
                qb = cnt % 3
                R.tt('pool', qr[qb], t1[tb], t2[tb], ALU.add, [('t1', tb), ('t2', tb)], [('qr', qb)])
                dst = d['QT'] if which == 0 else d['KT']
                R.dma('pool', dst[hc * 128:(hc + 1) * 128, i * 512:(i + 1) * 512], qr[qb], 'qr%d' % qb,
                      [('qr', qb)], [])
                R.act(sq2[tb], qr[qb], AF.Square, [('qr', qb)], [('sq2', tb)])
                if pend:
                    pend.pop()()

                def stats(tb=tb, dstm=k.qkmax[:, which, seq, hc:hc + 1]):
                    R.mm(k.ps[7], k.ones_bf, sq2[tb], True, True, [('sq2', tb)], [('ps', 7)])
                    R.red(mx[tb], k.ps[7], ALU.max, [('ps', 7)], [('mx', tb)])
                    R.tt('dve', dstm, dstm, mx[tb], ALU.max, [('mx', tb), 'qkmax'], ['qkmax'])

                pend.append(stats)
                cnt += 1
        for n in range(4):
            for cbk in range(2):
                pv = 3 + (cnt % 4)
                cnt += 1
                c0 = 2048 + cbk * 512
                for kc in range(8):
                    R.mm(k.ps[pv], xn[:, kc, n * 128:(n + 1) * 128], Wqkv[:, kc, c0:c0 + 512], kc == 0, kc == 7,
                         [('xn', kc)] + wkeys('Wqkv', kc, c0, c0 + 512), [('ps', pv)])
                R.cp('act' if cbk else 'dve', vtok[:, n, cbk * 512:(cbk + 1) * 512], k.ps[pv],
                     [('ps', pv)], [('vtok', n, cbk)])
        R.dma('pool', tm(d['VV'])[:, 4 * i:4 * i + 4, :], vtok, 'vtok',
              [('vtok', n, c) for n in range(4) for c in range(2)], [])
    if pend:
        pend.pop()()
    ops_ = R.q
    R = k.R = R_real
    list_schedule(R, ops_)
    R.barrier()


def phase_B(k):
    R, ar, cfg, d = k.R, k.ar, k.cfg, k.d
    ar.reset()
    SKM = cfg.TS + cfg.TF
    KhT = [ar.alloc([SKM], BF16) for _ in range(2)]
    Vh = [ar.alloc([SKM // 128, 128], BF16) for _ in range(2)]
    qz = [ar.alloc([2, 512], BF16) for _ in range(2)]
    et = [ar.alloc([2, 512], BF16) for _ in range(4)]
    accD = ar.alloc([512], F32)
    accP = ar.alloc([512], F32)
    accb = ar.alloc([512], BF16)
    nsb = ar.alloc([3, 512], F32)
    rr = ar.alloc([2, 512], F32)
    lnl = ar.alloc([2, 512], F32)
    tq = ar.alloc([2, 512], F32)
    o = ar.alloc([512], F32)
    sq = ar.alloc([512], BF16)
    rs = ar.alloc([512], F32)
    rstd = ar.alloc([512], F32)
    oT = [ar.alloc([512], BF16) for _ in range(2)]
    tmp = ar.alloc([4], F32)
    fl = lambda x: x.rearrange("p a b -> p (a b)")
    R.tt('dve', tmp[0:64, 0:1], k.lamt[0:64, 0:1], k.lamt[0:64, 1:2], ALU.mult, ['lamt'], ['ltmp'])
    R.tt('dve', tmp[0:64, 1:2], k.lamt[0:64, 2:3], k.lamt[0:64, 3:4], ALU.mult, ['lamt', 'ltmp'], ['ltmp'])
    R.mm(k.ps[0][:, 0:2], k.ones_f[0:64, :], tmp[0:64, 0:2], True, True, ['ltmp'], [('ps', 0)])
    R.act(tmp[:, 2:4], k.ps[0][:, 0:2], AF.Exp, [('ps', 0)], ['ltmp2'])
    R.tt('dve', k.lamneg, tmp[:, 3:4], tmp[:, 2:3], ALU.subtract, ['ltmp2'], ['lamneg'])
    R.ts1('dve', k.lamneg, k.lamneg, -0.2, ALU.add, ['lamneg'], ['lamneg'])
    R.tt('dve', k.negM, k.qkmax[:, 0], k.qkmax[:, 1], ALU.mult, ['qkmax'], ['negM'])
    R.act(k.negM, k.negM, AF.Sqrt, ['negM'], ['negM'])
    R.ts1('dve', k.negM, k.negM, -0.125, ALU.mult, ['negM'], ['negM'])
    for b in range(2):
        R.memset('dve', qz[b], 0.0, [('qz', b)])
    R_real = R
    R = k.R = Defer()
    VVt = tm(d['VV'])
    hcount = 0
    qcount = 0
    ecount = 0
    for seq in range(2):
        if seq == 0:
            Sq, q0, Sk, k0 = cfg.TP, 0, cfg.TP, 0
        else:
            Sq, q0, Sk, k0 = cfg.TS, cfg.TP, cfg.TS + cfg.TF, cfg.TP
        nk = Sk // 128
        for h in range(8):
            hb = hcount % 2
            hcount += 1
            R.dma('sp', KhT[hb][:, 0:Sk], d['KT'][h * 128:(h + 1) * 128, k0:k0 + Sk], 'kv%d' % hb,
                  [], [('KhT', hb)])
            for n0 in range(0, nk, 16):
                n1 = min(nk, n0 + 16)
                R.dma('sp', Vh[hb][:, n0:n1, :], VVt[:, k0 // 128 + n0:k0 // 128 + n1, h * 128:(h + 1) * 128],
                      'kv%d' % hb, [], [('Vh', hb)])
            negM = k.negM[:, seq, h:h + 1]
            for qt in range(Sq // 512):
                qb = qcount % 2
                qcount += 1
                qc0 = q0 + qt * 512
                R.dma('sp', qz[qb][0:64, 0, :], d['QT'][h * 128:h * 128 + 64, qc0:qc0 + 512], 'qz%d' % qb,
                      [], [('qz', qb)])
                R.dma('sp', qz[qb][64:128, 1, :], d['QT'][h * 128 + 64:h * 128 + 128, qc0:qc0 + 512], 'qz%d' % qb,
                      [], [('qz', qb)])

                def smm(j):
                    sb = (j % 2) * 2
                    for c in range(2):
                        R.mm(k.ps[sb + c], KhT[hb][:, j * 128:(j + 1) * 128], qz[qb][:, c, :], True, True,
                             [('KhT', hb), ('qz', qb)], [('ps', sb + c)])

                smm(0)
                inited = {'D': False, 'P': False}
                for j in range(nk):
                    if j + 1 < nk:
                        smm(j + 1)
                    sb = (j % 2) * 2
                    eb = ecount % 4
                    ecount += 1
                    e2 = et[eb]
                    R.act(fl(e2), k.pd[j % 2], AF.Exp, [('ps', sb), ('ps', sb + 1), 'negM'],
                          [('et', eb)], bias=negM, scale=0.125)
                    for c in range(2):
                        R.mm(k.ps[4 + c], Vh[hb][:, j, :], e2[:, c, :], j == 0, j == nk - 1,
                             [('Vh', hb), ('et', eb)], [('ps', 4 + c)])
                    R.mm(k.ps[6], k.ones_bf, e2[:, 0, :], j == 0, j == nk - 1, [('et', eb)], [('ps', 6)])
                    who = 'P' if j % 3 == 2 else 'D'
                    eng, acc = ('pool', accP) if who == 'P' else ('dve', accD)
                    if not inited[who]:
                        R.cp(eng, acc, e2[:, 1, :], [('et', eb)], ['acc' + who])
                        inited[who] = True
                    else:
                        R.tt(eng, acc, acc, e2[:, 1, :], ALU.add, [('et', eb), 'acc' + who], ['acc' + who])
                for c in range(3):
                    R.cp('dve', nsb[:, c, :], k.ps[4 + c], [('ps', 4 + c)], [('nsb', c)])
                if inited['P']:
                    R.tt('pool', accb, accD, accP, ALU.add, ['accD', 'accP'], ['accb'])
                else:
                    R.cp('pool', accb, accD, ['accD'], ['accb'])
                R.mm(k.ps[7], k.ones_bf, accb, True, True, ['accb'], [('ps', 7)])
                R.act(lnl[:, 0, :], nsb[:, 2, :], AF.Ln, [('nsb', 2)], [('lnl', 0)])
                R.act(lnl[:, 1, :], k.ps[7], AF.Ln, [('ps', 7)], [('lnl', 1)])
                R.act(fl(rr), fl(lnl), AF.Exp, [('lnl', 0), ('lnl', 1)], [('rr', 0), ('rr', 1)], scale=-1.0)
                R.tt('pool', fl(tq), fl(nsb[:, 0:2, :]), fl(rr), ALU.mult, [('nsb', 0), ('nsb', 1), ('rr', 0), ('rr', 1)],
                     ['tq'])
                R.stt(o, tq[:, 1, :], k.lamneg, tq[:, 0, :], ALU.mult, ALU.add, ['tq', 'lamneg'], ['o'])
                R.tt('pool', sq, o, o, ALU.mult, ['o'], ['osq'])
                R.mm(k.ps[7], k.ones_bf, sq, True, True, ['osq'], [('ps', 7)])
                R.act(rs, k.ps[7], AF.Ln, [('ps', 7)], ['ors'], bias=k.eps5, scale=1.0 / 128)
                R.act(rstd, rs, AF.Exp, ['ors'], ['orstd'], scale=-0.5)
                ob = qcount % 2
                R.tt('pool', oT[ob], o, rstd, ALU.mult, ['o', 'orstd'], [('oT', ob)])
                R.dma('pool', d['OT'][h * 128:(h + 1) * 128, qc0:qc0 + 512], oT[ob], 'oT%d' % ob, [('oT', ob)], [])
    ops_ = R.q
    R = k.R = R_real
    list_schedule(R, ops_)
    R.barrier()


def phase_C1(k):
    R, ar, cfg, d = k.R, k.ar, k.cfg, k.d
    ar.reset()
    Wo = ar.alloc([8, 1024], BF16)
    k.stg = [ar.alloc([1024], F32) for _ in range(4)]
    K.stgw = k.stgw = 1024
    sc = ar.alloc([1], F32)
    R.ts1('dve', sc, k.sublnw, 0.8, ALU.mult, ['sublnw'], ['sc'])
    load_w(k, Wo, d['w_o'], 8, 1024, 'Wo', scale=lambda kc: sc)
    oT = [ar.alloc([8, 512], BF16) for _ in range(2)]
    xT = [ar.alloc([8, 512], F32) for _ in range(2)]
    XT = fm(d['XT'])
    OT = fm(d['OT'])

    def loads(i):
        b = i % 2
        R.dma('sp', oT[b], OT[:, :, i * 512:(i + 1) * 512], 'c1o%d' % b, [], [('oT', b)])
        R.dma('sp', xT[b], XT[:, :, i * 512:(i + 1) * 512], 'c1x%d' % b, [], [('xT', b, kc) for kc in range(8)])

    R_real = R
    R = k.R = Defer()
    loads(0)
    for i in range(cfg.NT_OWN):
        b = i % 2
        if i + 1 < cfg.NT_OWN:
            loads(i + 1)
        for dc in range(8):
            pb = dc % 4
            for h in range(8):
                R.mm(k.ps[pb], Wo[:, h, dc * 128:(dc + 1) * 128], oT[b][:, h, :], h == 0, h == 7,
                     [('oT', b)] + wkeys('Wo', h, dc * 128, (dc + 1) * 128), [('ps', pb)])
            R.tt('dve', xT[b][:, dc, :], k.ps[pb], xT[b][:, dc, :], ALU.add, [('ps', pb), ('xT', b, dc)],
                 [('xT', b, dc)])
        R.dma('pool', XT[:, :, i * 512:(i + 1) * 512], xT[b], 'c1x%d' % b, [('xT', b, kc) for kc in range(8)], [])
    ops_ = R.q
    R = k.R = R_real
    list_schedule(R, ops_)
    R.barrier()


def phase_FFN(k, layer):
    R, ar, cfg, d = k.R, k.ar, k.cfg, k.d
    ar.reset()
    Win = ar.alloc([8, 2 * FH], BF16)
    Wout = ar.alloc([FJ, 1024], BF16)
    hidraw = ar.alloc([FJ * 256], F32)
    k.stg = [hidraw[:, j * 1024:(j + 1) * 1024] for j in range(5)]
    K.stgw = k.stgw = 1024
    vn = V_NFFN0 if layer == 0 else V_NFFN1
    load_w(k, Win, d['w_ffn_in'][layer], 8, 2 * FH, 'Win', scale=lambda kc: vcol(k, vn, kc))
    load_w(k, Wout, d['w_ffn_out'][layer], FJ, 1024, 'Wout')
    R.barrier()
    R_real = R
    R = k.R = Defer()
    xT = ar.alloc([8, 512], F32)
    xn = ar.alloc([8, 512], BF16)
    hid = hidraw.bitcast(BF16).rearrange("p (a b) -> p a b", b=512)
    ytok = hidraw[:, 0:4096].rearrange("p (n c) -> p n c", c=1024)
    sq = [ar.alloc([512], BF16) for _ in range(2)]
    rs = ar.alloc([512], F32)
    rstd = ar.alloc([512], F32)
    sg = [ar.alloc([512], F32) for _ in range(2)]
    lastc = ar.alloc([8], F32)
    xh = xn
    yT = xT
    XT = fm(d['XT'])
    XH = fm(d['XH'])
    if layer == 0:
        for c in (0, cfg.TP + 1, cfg.TP + 2):
            R.dma('pool', XH[:, :, c:c + 1], k.zero_bf.unsqueeze(2), 'xhz', ['zero_bf'], [], slow=True)
    xk = [('xT', kc) for kc in range(8)]
    for i in range(cfg.NT_OWN):
        R.dma('sp', xT, XT[:, :, i * 512:(i + 1) * 512], 'fx', [], xk)
        rms_rstd(k, xT, 'xT', 8, 1.0 / D, 1e-6, 7, sq, rs, rstd, 'F')
        for kc in range(8):
            R.tt('pool' if kc % 2 else 'dve', xn[:, kc, :], xT[:, kc, :], rstd, ALU.mult,
                 [('xT', kc), 'Frstd'], [('xn', kc)])
        for j in range(FJ):
            pg, pu = (0, 1) if j % 2 == 0 else (2, 3)
            for kc in range(8):
                R.mm(k.ps[pg], Win[:, kc, j * 128:(j + 1) * 128], xn[:, kc, :], kc == 0, kc == 7,
                     [('xn', kc)] + wkeys('Win', kc, j * 128, (j + 1) * 128), [('ps', pg)])
            for kc in range(8):
                c0 = FH + j * 128
                R.mm(k.ps[pu], Win[:, kc, c0:c0 + 128], xn[:, kc, :], kc == 0, kc == 7,
                     [('xn', kc)] + wkeys('Win', kc, c0, c0 + 128), [('ps', pu)])
            R.act(sg[j % 2], k.ps[pg], AF.Silu, [('ps', pg)], [('sg', j % 2)])
            R.tt('dve', hid[:, j, :], k.ps[pu], sg[j % 2], ALU.mult, [('ps', pu), ('sg', j % 2)], [('hid', j)])
        for dc in range(8):
            pb = 4 + dc % 2
            for j in range(FJ):
                R.mm(k.ps[pb], Wout[:, j, dc * 128:(dc + 1) * 128], hid[:, j, :], j == 0, j == FJ - 1,
                     [('hid', j)] + wkeys('Wout', j, dc * 128, (dc + 1) * 128), [('ps', pb)])
            R.tt('dve', xT[:, dc, :], k.ps[pb], xT[:, dc, :], ALU.add, [('ps', pb), ('xT', dc)], [('xT', dc)])
        rms_rstd(k, xT, 'xT', 8, 1.0 / D, 1e-6, 6, sq, rs, rstd, 'G')
        if layer == 0:
            R.dma('pool', XT[:, :, i * 512:(i + 1) * 512], xT, 'fxs', xk, [])
            for kc in range(8):
                R.tt('pool' if kc % 2 else 'dve', xh[:, kc, :], xT[:, kc, :], rstd, ALU.mult,
                     [('xT', kc), 'Grstd'], [('xn', kc)])
            xhk = [('xn', kc) for kc in range(8)]
            base = 1 + i * 512 if i < cfg.NTP else cfg.TP + 3 + (i - cfg.NTP) * 512
            R.dma('pool', XH[:, :, base:base + 512], xh, 'fxh', xhk, [])
            if i == cfg.NT_OWN - 1:
                R.cp('dve', lastc, xh[:, :, 511], xhk, ['lastc'])
                R.dma('pool', d['ex1_in'], lastc, 'ex1', ['lastc'], ['ex1_in'])
        else:
            for kc in range(8):
                R.stt(yT[:, kc, :], xT[:, kc, :], vcol(k, V_NFIN, kc), rstd, ALU.mult, ALU.mult,
                      [('xT', kc), 'Grstd'], [('xT', kc)])
            for n in range(4):
                for half in range(2):
                    pb = half
                    for q4 in range(4):
                        kc = half * 4 + q4
                        R.tr(k.ps[pb][:, q4 * 128:(q4 + 1) * 128], yT[:, kc, n * 128:(n + 1) * 128], k.ident_f,
                             [('xT', kc)], [('ps', pb)])
                    R.cp('act' if half else 'dve', ytok[:, n, half * 512:(half + 1) * 512], k.ps[pb],
                         [('ps', pb)], [('hid', jj) for jj in range(FJ)])
            R.dma('pool', tm(d['y_out'])[:, 4 * i:4 * i + 4, :], ytok, 'yst',
                  [('hid', jj) for jj in range(FJ)], [])
    ops_ = R.q
    R = k.R = R_real
    list_schedule(R, ops_)
    R.barrier()


CDEC = math.exp(-0.5)
PAIRS = [[0, 1], [2, 3], [4, 5], [6, 7]]


def add_cc(R, fn, r, w, name):
    deps = {}
    for kk_ in list(r) + list(w):
        t = R.lastw.get(kk_)
        if t and deps.get(t[0], 0) < t[1]:
            deps[t[0]] = t[1]
    wd = R.waited['pool']
    for s_, v in deps.items():
        if s_[0] == 'd':
            v = R.dcnt[s_]
        if wd.get(s_, 0) >= v:
            continue
        wd[s_] = v
        R.semrefs.add(s_)
        R.lists['pool'].append(('w', s_, v))
    ref = ('d', 'cc_' + name)
    R.dcnt[ref] = R.dcnt.get(ref, 0) + 1
    tok = (ref, R.dcnt[ref])
    R.lists['pool'].append(('i', fn, ref, 0))
    R.semrefs.add(ref)
    for kk_ in w:
        R.lastw[kk_] = tok
        R.readers[kk_] = {}


def phase_X1(k):
    R, ar, cfg, d = k.R, k.ar, k.cfg, k.d
    ar.reset()
    g = ar.alloc([2, 8], F32)
    t = ar.alloc([8], F32)
    pb = ar.alloc([8], BF16)
    add_cc(R, lambda e: e.collective_compute("AllGather", ALU.bypass, replica_groups=PAIRS,
                                             ins=[d['ex1_in']], outs=[d['ex1_out']]), [], ['ex1o'], 'cc1')
    R.dma('sp', g, d['ex1_out'].rearrange("(s p) c -> p s c", p=128), 'x1g', ['ex1o'], ['x1g'])
    R.scp('dve', t, g[:, 0, :], k.sel[:, 0:1], ['x1g'], ['x1t'])
    R.stt(pb, g[:, 1, :], k.sel[:, 1:2], t, ALU.mult, ALU.add, ['x1g', 'x1t'], ['x1p'])
    R.dma('sp', fm(d['XH'])[:, :, cfg.XHW - 1:cfg.XHW], pb.unsqueeze(2), 'x1s', ['x1p'], [], slow=True)
    R.barrier()


def phase_D(k):
    R, ar, cfg, d = k.R, k.ar, k.cfg, k.d
    ar.reset()
    TD = 256
    NTD = cfg.NOWN // TD
    Wrkv = ar.alloc([8, 3072], BF16)
    W1s = ar.alloc([8, 128], BF16)
    A1s = ar.alloc([8, 128], BF16)
    G1 = ar.alloc([8, 256], BF16)
    W2p = ar.alloc([2, 1024], BF16)
    A2p = ar.alloc([2, 1024], BF16)
    G2 = ar.alloc([2, 1024], BF16)
    k.stg = [ar.alloc([1024], F32) for _ in range(2)]
    K.stgw = k.stgw = 1024
    nm = lambda kc: vcol(k, V_NMIX1, kc)
    for j in range(3):
        load_w(k, Wrkv[:, :, j * 1024:(j + 1) * 1024], d['w_rkv'][j], 8, 1024, 'Wrkv%d' % j, scale=nm)
    load_w(k, W1s, d['w1s'], 8, 128, 'W1s', scale=nm)
    load_w(k, A1s, d['a1s'], 8, 128, 'A1s', scale=nm)
    load_w(k, G1, d['g1p'], 8, 256, 'G1', scale=nm)
    load_w(k, W2p, d['w2p'], 2, 1024, 'W2p')
    load_w(k, A2p, d['a2p'], 2, 1024, 'A2p')
    load_w(k, G2, d['g2p'], 2, 1024, 'G2')
    smask = ar.alloc([512], F32)
    R.dma('sp', smask, d['smask'], 'dsm', [], ['smask'])
    sm = smask[:, 0:TD]
    xh = [ar.alloc([8, TD + 2], BF16) for _ in range(2)]
    xm = [[ar.alloc([8, TD], BF16) for _ in range(6)] for _ in range(2)]
    hw = [ar.alloc([TD], BF16) for _ in range(2)]
    ha = [ar.alloc([TD], BF16) for _ in range(2)]
    hg = [ar.alloc([2, TD], BF16) for _ in range(2)]
    mtmp = ar.alloc([TD], F32)
    mxx = ar.alloc([TD], F32)
    f32n = ['rT', 'kT', 'sig0', 'sig1', 'a0', 'a1', 'tk', 'ssm', 'lnv', 'rin', 'kk', 'tK0', 'tK1', 'kd0', 'kd1',
            'b0', 'b1', 'cs', 'pre', 'cum', 'cume', 'E1', 'E2', 'E3']
    T = [{n: ar.alloc([TD], F32) for n in f32n} for _ in range(2)]
    bfn = ['vTb', 'sqk', 'cr', 'gT', 'cv', 'o_r', 'o_kk', 'o_b', 'o_k']
    B = [{n: ar.alloc([TD], BF16) for n in bfn} for _ in range(2)]
    tmj = [{n: ar.alloc([2, 128], BF16) for n in ('bh', 'kh', 'vt')} for _ in range(2)]
    pc = [ar.alloc([2], F32) for _ in range(2)]
    XH = fm(d['XH'])
    wr = lambda j, kc: [('Wrkv%d' % j, kc, 0)]
    cnt = [0]
    NSUB = TD // 128

    scnt = [0, 0]

    def bank(st=None):
        if st is None:
            cnt[0] += 1
            return cnt[0] % 6
        scnt[st] += 1
        return 3 * st + scnt[st] % 3

    def loads(i):
        t0 = i * TD
        base = 1 + t0 if t0 < cfg.TP else cfg.TP + 3 + (t0 - cfg.TP)
        R.dma('sp', xh[i % 2], XH[:, :, base - 1:base + TD + 1], 'dxh%d' % (i % 2), [], [('xh', i % 2)])

    def to_tm(R, src, skey, name, par, dst, i, dc):
        pb = 6 + par
        for n in range(NSUB):
            R.tr(k.psb[pb][:, n * 128:(n + 1) * 128], src[:, n * 128:(n + 1) * 128], k.ident_bf, [skey], [('ps', pb)])
        R.cp('dve', tmj[par][name], k.psb[pb][:, 0:TD].rearrange("p (n c) -> p n c", c=128), [('ps', pb)],
             [(name, par)])
        R.dma('sp', tm(dst)[:, NSUB * i:NSUB * i + NSUB, dc * 128:(dc + 1) * 128], tmj[par][name],
              'dt_%s%d' % (name, par), [(name, par)], [])

    def mix_kc(R, i, kc):
        b = i % 2
        xb = xh[b]
        cur = xb[:, kc, 1:TD + 1]
        R.tt('pool', mtmp, xb[:, kc, 0:TD], xb[:, kc, 2:TD + 2], ALU.add, [('xh', b)], ['mtmp'])
        R.stt(mxx, mtmp, 0.5, cur, ALU.mult, ALU.subtract, ['mtmp', ('xh', b)], ['mxx'])
        for m in range(6):
            R.stt(xm[b][m][:, kc, :], mxx, vcol(k, V_MU + m, kc), cur, ALU.mult, ALU.add,
                  ['mxx', ('xh', b)], [('xm', b, m, kc)])

    def hidden(i):
        b = i % 2
        pb = bank()
        for kc in range(8):
            R.mm(k.ps[pb][:, 0:TD], W1s[:, kc, :], xm[b][3][:, kc, :], kc == 0, kc == 7,
                 [('xm', b, 3, kc), ('W1s', kc, 0)], [('ps', pb)])
        R.act(hw[b], k.ps[pb][:, 0:TD], AF.Tanh, [('ps', pb)], [('hw', b)])
        pb = bank()
        for kc in range(8):
            R.mm(k.ps[pb][:, 0:TD], A1s[:, kc, :], xm[b][4][:, kc, :], kc == 0, kc == 7,
                 [('xm', b, 4, kc), ('A1s', kc, 0)], [('ps', pb)])
        R.cp('act', ha[b], k.ps[pb][:, 0:TD], [('ps', pb)], [('ha', b)])
        for hf in range(2):
            pb = bank()
            for kc in range(8):
                R.mm(k.ps[pb][:, 0:TD], G1[:, kc, hf * 128:(hf + 1) * 128], xm[b][5][:, kc, :], kc == 0, kc == 7,
                     [('xm', b, 5, kc), ('G1', kc, 0)], [('ps', pb)])
            R.act(hg[b][:, hf, :], k.ps[pb][:, 0:TD], AF.Sigmoid, [('ps', pb)], [('hg', b, hf)])

    def front(R, i, dc, par):
        b = i % 2
        t, bb = T[par], B[par]
        K_ = lambda n: (n, par)
        dsl = slice(dc * 128, (dc + 1) * 128)
        cols = slice(i * TD, (i + 1) * TD)
        pr, pk, pv = bank(par), bank(par), bank(par)
        for j, pp in ((0, pr), (1, pk), (2, pv)):
            for kc in range(8):
                R.mm(k.ps[pp][:, 0:TD], Wrkv[:, kc, j * 1024 + dc * 128:j * 1024 + (dc + 1) * 128], xm[b][j][:, kc, :],
                     kc == 0, kc == 7, [('xm', b, j, kc)] + wr(j, kc), [('ps', pp)])
        R.cp('act', t['rT'], k.ps[pr][:, 0:TD], [('ps', pr)], [K_('rT')])
        R.cp('act', t['kT'], k.ps[pk][:, 0:TD], [('ps', pk)], [K_('kT')])
        R.cp('act', bb['vTb'], k.ps[pv][:, 0:TD], [('ps', pv)], [K_('vTb')])
        for z in range(2):
            pw = bank(par)
            R.mm(k.ps[pw][:, 0:TD], W2p[:, z, dsl], hw[b], True, True, [('hw', b), ('W2p', z, 0)], [('ps', pw)])
            R.act(t['sig%d' % z], k.ps[pw][:, 0:TD], AF.Sigmoid, [('ps', pw)], [K_('sig%d' % z)],
                  bias=vcol(k, V_W0 + z, dc))
            pa = bank(par)
            R.mm(k.ps[pa][:, 0:TD], A2p[:, z, dsl], ha[b], True, True, [('ha', b), ('A2p', z, 0)], [('ps', pa)])
            R.act(t['a%d' % z], k.ps[pa][:, 0:TD], AF.Sigmoid, [('ps', pa)], [K_('a%d' % z)],
                  bias=vcol(k, V_A0 + z, dc))
        pg = bank(par)
        for hf in range(2):
            R.mm(k.ps[pg][:, 0:TD], G2[:, hf, dsl], hg[b][:, hf, :], hf == 0, hf == 1,
                 [('hg', b, hf), ('G2', hf, 0)], [('ps', pg)])
        R.cp('act', bb['gT'], k.ps[pg][:, 0:TD], [('ps', pg)], [K_('gT')])
        R.dma('sp', d['GT'][dsl, cols], bb['gT'], 'd_g%d' % par, [K_('gT')], [])
        R.scp('act', t['tk'], t['kT'], vcol(k, V_KK, dc), [K_('kT')], [K_('tk')])
        for z in range(2):
            R.add('act', (lambda o_, i_, s_, b_: (lambda e: e.activation(out=o_, in_=i_, func=AF.Identity,
                                                                         bias=b_, scale=s_)))(
                t['tK%d' % z], t['a%d' % z], vcol(k, V_KA, dc), vcol(k, V_OMKA, dc)),
                [K_('a%d' % z)], [K_('tK%d' % z)])

    def back(R, i, dc, par):
        t, bb = T[par], B[par]
        K_ = lambda n: (n, par)
        dsl = slice(dc * 128, (dc + 1) * 128)
        cols = slice(i * TD, (i + 1) * TD)
        R.tt('pool', bb['sqk'], t['tk'], t['tk'], ALU.mult, [K_('tk')], [K_('sqk')])
        pn = bank(par)
        R.mm(k.ps[pn][:, 0:TD], k.onesbd_bf, bb['sqk'], True, True, [K_('sqk')], [('ps', pn)])
        R.ts1('dve', t['ssm'], k.ps[pn][:, 0:TD], 1e-24, ALU.max, [('ps', pn)], [K_('ssm')])
        R.act(t['lnv'], t['ssm'], AF.Ln, [K_('ssm')], [K_('lnv')])
        R.act(t['rin'], t['lnv'], AF.Exp, [K_('lnv')], [K_('rin')], scale=-0.5)
        R.tt('pool', t['kk'], t['tk'], t['rin'], ALU.mult, [K_('tk'), K_('rin')], [K_('kk')])
        for z in range(2):
            R.tt('pool', t['kd%d' % z], t['kT'], t['tK%d' % z], ALU.mult, [K_('kT'), K_('tK%d' % z)], [K_('kd%d' % z)])
            R.tt('pool', t['b%d' % z], t['kk'], t['a%d' % z], ALU.mult, [K_('kk'), K_('a%d' % z)], [K_('b%d' % z)])
        R.tt('pool', t['cs'], t['kd0'], t['kd1'], ALU.add, [K_('kd0'), K_('kd1')], [K_('cs')])
        R.stt(bb['cr'], t['cs'], vcol(k, V_RK, dc), t['rT'], ALU.mult, ALU.mult, [K_('cs'), K_('rT')], [K_('cr')])
        pc_ = bank(par)
        R.mm(k.ps[pc_][:, 0:TD], k.onesbd_bf, bb['cr'], True, True, [K_('cr')], [('ps', pc_)])
        R.tt('dve', bb['cv'], k.ps[pc_][:, 0:TD], bb['vTb'], ALU.mult, [('ps', pc_), K_('vTb')], [K_('cv')])
        R.dma('sp', d['CVT'][dsl, cols], bb['cv'], 'd_cv%d' % par, [K_('cv')], [])
        to_tm(R, bb['vTb'], K_('vTb'), 'vt', par, d['VT'], i, dc)
        for z in range(2):
            sg_ = t['sig%d' % z]
            sk = K_('sig%d' % z)
            cum3 = t['cum'].rearrange("p (n t) -> p n t", t=128)
            if z == 0:
                R.scan(t['cum'], sm, sg_, [sk, 'smask'], [K_('cum')])
                tot = cum3[:, :, 127:128]
            else:
                R.scan(t['pre'], sm, sg_, [sk, 'smask'], [K_('pre')])
                pre3 = t['pre'].rearrange("p (n t) -> p n t", t=128)
                R.tt('dve', t['cum'], sg_, t['pre'], ALU.subtract, [sk, K_('pre')], [K_('cum')])
                R.tt('dve', cum3, cum3, pre3[:, :, 127:128].to_broadcast([128, NSUB, 128]), ALU.add,
                     [K_('cum'), K_('pre')], [K_('cum')])
                tot = cum3[:, :, 0:1]
            R.tt('pool', t['cume'], t['cum'], sg_, ALU.subtract, [K_('cum'), sk], [K_('cume')])
            R.act(t['E1'], t['cum'], AF.Exp, [K_('cum')], [K_('E1')], scale=-CDEC)
            R.act(t['E2'], t['cume'], AF.Exp, [K_('cume')], [K_('E2')], scale=-CDEC)
            R.act(t['E3'], t['cum'], AF.Exp, [K_('cum')], [K_('E3')], scale=CDEC)
            R.act(pc[par].unsqueeze(2), tot, AF.Exp, [K_('cum')], [K_('pc')], scale=-CDEC)
            R.dma('sp', d['PC%d' % z][dsl, i * NSUB:i * NSUB + NSUB], pc[par], 'd_pc%d' % par, [K_('pc')], [])
            R.tt('pool', bb['o_r'], t['rT'], t['E1'], ALU.mult, [K_('rT'), K_('E1')], [K_('o_r')])
            R.dma('sp', d['RT%d' % z][dsl, cols], bb['o_r'], 'd_r%d' % par, [K_('o_r')], [])
            R.tt('pool', bb['o_kk'], t['kk'], t['E2'], ALU.mult, [K_('kk'), K_('E2')], [K_('o_kk')])
            R.dma('sp', d['KKT%d' % z][dsl, cols], bb['o_kk'], 'd_kk%d' % par, [K_('o_kk')], [])
            R.tt('dve', bb['o_b'], t['b%d' % z], t['E3'], ALU.mult, [K_('b%d' % z), K_('E3')], [K_('o_b')])
            R.dma('sp', d['BT%d' % z][dsl, cols], bb['o_b'], 'd_b%d' % par, [K_('o_b')], [])
            to_tm(R, bb['o_b'], K_('o_b'), 'bh', par, d['BH%d' % z], i, dc)
            R.tt('dve', bb['o_k'], t['kd%d' % z], t['E3'], ALU.mult, [K_('kd%d' % z), K_('E3')], [K_('o_k')])
            R.dma('sp', d['KT2%d' % z][dsl, cols], bb['o_k'], 'd_k%d' % par, [K_('o_k')], [])
            to_tm(R, bb['o_k'], K_('o_k'), 'kh', par, d['KH%d' % z], i, dc)

    RD = Defer()

    def loads_d(i):
        t0 = i * TD
        base = 1 + t0 if t0 < cfg.TP else cfg.TP + 3 + (t0 - cfg.TP)
        RD.dma('sp', xh[i % 2], XH[:, :, base - 1:base + TD + 1], 'dxh%d' % (i % 2), [], [('xh', i % 2)])

    def hidden_d(i):
        nonlocal R
        R_save = R
        R = RD
        try:
            hidden(i)
        finally:
            R = R_save

    loads_d(0)
    for i in range(NTD):
        if i + 1 < NTD:
            loads_d(i + 1)
        for kc in range(8):
            mix_kc(RD, i, kc)
        hidden_d(i)
        for dc in range(8):
            front(RD, i, dc, dc % 2)
            back(RD, i, dc, dc % 2)
    list_schedule(R, RD.q)
    R.barrier()


def phase_E(k):
    R, ar, cfg, d = k.R, k.ar, k.cfg, k.d
    ar.reset()
    masks = ar.alloc([7, 512], BF16)
    NS = 3
    P = [dict(kkP=ar.alloc([8, 128], BF16), rP=ar.alloc([8, 128], BF16), bP=ar.alloc([8, 128], BF16),
              kP=ar.alloc([8, 128], BF16), KR=ar.alloc([8, 4, 128], BF16), bB=ar.alloc([8, 2, 128], BF16),
              bh=ar.alloc([1024], BF16), kh=ar.alloc([1024], BF16), v=ar.alloc([1024], BF16),
              pc=ar.alloc([8], F32)) for _ in range(NS)]
    ATb = [ar.alloc([8, 512], BF16) for _ in range(2)]
    ATk = [ar.alloc([8, 512], BF16) for _ in range(2)]
    Mb = [[[ar.alloc([4, 128], BF16) for _ in range(2)] for _ in range(4)] for _ in range(2)]
    Nb = [[[ar.alloc([4, 128], BF16) for _ in range(2)] for _ in range(4)] for _ in range(2)]
    Zb = [[[ar.alloc([4, 128], BF16) for _ in range(2)] for _ in range(4)] for _ in range(2)]
    Hf = ar.alloc([8, 128], F32)
    Hb = ar.alloc([8, 128], BF16)
    Ht = ar.alloc([8, 64], F32)
    Gn = ar.alloc([8, 128], BF16)
    Ub = ar.alloc([8, 128], BF16)
    Ys = [ar.alloc([1024], F32) for _ in range(2)]
    xg = ar.alloc([2, 1024], F32)
    xgf = xg.rearrange("p a b -> p (a b)")
    for r0 in (0, 4):
        n_ = min(4, 7 - r0)
        stv = xgf[:, 0:n_ * 512].rearrange("p (a b) -> p a b", b=512)
        R.dma('sp', stv, d['masks'][:, r0:r0 + n_, :], 'em', [], ['x2g'])
        R.cp('dve', masks[:, r0:r0 + n_, :], stv, ['x2g'], ['masks'])
    for s_ in range(NS):
        R.memset('dve', P[s_]['KR'], 0.0, [('KR', s_)])
        R.memset('dve', P[s_]['bB'], 0.0, [('bB', s_)])
    cnt = [0]
    cur = [[0, 0, 0, 0], [0, 0, 0, 0]]
    fl = lambda a: a.rearrange("p a b -> p (a b)")

    def bank():
        cnt[0] += 1
        return cnt[0] % 8

    def loads(z, tok0, s_):
        p = P[s_]
        c = slice(tok0, tok0 + 128)
        sfx = '%d' % s_
        R.dma('sp', p['kkP'], fm(d['KKT%d' % z])[:, :, c], 'e_kk' + sfx, [], [('kkP', s_)])
        R.dma('sp', p['bP'], fm(d['BT%d' % z])[:, :, c], 'e_b' + sfx, [], [('bP', s_)])
        R.dma('sp', p['kP'], fm(d['KT2%d' % z])[:, :, c], 'e_k' + sfx, [], [('kP', s_)])
        for hh in range(2):
            rows = slice(hh * 64, hh * 64 + 64)
            R.dma('sp', p['KR'][rows, :, hh, :], fm(d['KKT%d' % z])[rows, :, c], 'e_KR' + sfx, [], [('KR', s_)])
            R.dma('sp', p['KR'][rows, :, 2 + hh, :], fm(d['RT%d' % z])[rows, :, c], 'e_KR' + sfx, [], [('KR', s_)])
            R.dma('sp', p['bB'][rows, :, hh, :], fm(d['BT%d' % z])[rows, :, c], 'e_bB' + sfx, [], [('bB', s_)])
        R.dma('sp', p['rP'], fm(d['RT%d' % z])[:, :, c], 'e_r' + sfx, [], [('rP', s_)])
        R.dma('sp', p['bh'], d['BH%d' % z][c, :], 'e_bh' + sfx, [], [('bh', s_)])
        R.dma('sp', p['kh'], d['KH%d' % z][c, :], 'e_kh' + sfx, [], [('kh', s_)])
        R.dma('sp', p['v'], d['VT'][c, :], 'e_v' + sfx, [], [('v', s_)])
        n = tok0 // 128
        R.dma('sp', p['pc'].unsqueeze(2), fm(d['PC%d' % z])[:, :, n:n + 1], 'e_pc' + sfx, [], [('pc', s_)], slow=True)

    def pre_stages(z, s_, par):
        p = P[s_]
        atb, atk = ATb[par], ATk[par]
        M_, N_, Z_ = Mb[par], Nb[par], Zb[par]
        cu = cur[par]
        mB, mK, mM, mI = masks[:, 3 * z, :], masks[:, 3 * z + 1, :], masks[:, 3 * z + 2, :], masks[:, 6, :]

        def level(lev):
            info = []
            for g in range(4):
                c0, c1 = cu[g], 1 - cu[g]
                Mp, Np = M_[g][c0], N_[g][c0]
                bn = None
                if lev < 6:
                    bn = bank()
                    for x in range(4):
                        R.mm(k.ps[bn][:, x * 128:(x + 1) * 128], Mp[:, x, :], Np[:, x, :], True, True,
                             [('M', par, g, c0), ('N', par, g, c0)], [('ps', bn)])
                bm = bank()
                for x in range(4):
                    R.mm(k.ps[bm][:, x * 128:(x + 1) * 128], Np[:, x, :], Mp[:, x, :], True, True,
                         [('M', par, g, c0), ('N', par, g, c0)], [('ps', bm)])
                info.append((c0, c1, bn, bm))
            for g in range(4):
                c0, c1, bn, bm = info[g]
                R.cp('act', fl(M_[g][c1]), k.ps[bm], [('ps', bm)], [('M', par, g, c1)])
                if bn is not None:
                    R.cp('dve' if g == 0 else 'act', fl(N_[g][c1]), k.ps[bn], [('ps', bn)], [('N', par, g, c1)])
            bzs = []
            for g in range(4):
                c0, c1, bn, bm = info[g]
                bz = bank()
                bzs.append(bz)
                for x in range(4):
                    R.mm(k.ps[bz][:, x * 128:(x + 1) * 128], M_[g][c1][:, x, :], Z_[g][c0][:, x, :], True, True,
                         [('M', par, g, c1), ('Z', par, g, c0)], [('ps', bz)])
            for g in range(4):
                c0, c1, bn, bm = info[g]
                R.tt('dve', fl(Z_[g][c1]), k.ps[bzs[g]], fl(Z_[g][c0]), ALU.add,
                     [('ps', bzs[g]), ('Z', par, g, c0)], [('Z', par, g, c1)])
                cu[g] = c1

        def stA():
            for dc in range(8):
                rhs = fl(p['KR'][:, dc, :, :])
                b1 = bank()
                R.mm(k.ps[b1], p['bP'][:, dc, :], rhs, True, True, [('bP', s_), ('KR', s_)], [('ps', b1)])
                b2 = bank()
                R.mm(k.ps[b2], p['kP'][:, dc, :], rhs, True, True, [('kP', s_), ('KR', s_)], [('ps', b2)])
                R.tt('dve', atb[:, dc, :], k.ps[b1], mB, ALU.mult, [('ps', b1), 'masks'], [('ATb', par, dc)])
                R.tt('dve', atk[:, dc, :], k.ps[b2], mK, ALU.mult, [('ps', b2), 'masks'], [('ATk', par, dc)])
            for g in range(4):
                cu[g] = 0
                b1 = bank()
                for dd in range(2):
                    dc = 2 * g + dd
                    R.mm(k.ps[b1][:, dd * 256:(dd + 1) * 256], p['kkP'][:, dc, :], fl(p['bB'][:, dc, :, :]), True, True,
                         [('kkP', s_), ('bB', s_)], [('ps', b1)])
                R.tt('dve', fl(M_[g][0]), k.ps[b1], mM, ALU.mult, [('ps', b1), 'masks'], [('M', par, g, 0)])
                n0 = atb[:, 2 * g:2 * g + 2, 0:256]
                nk_ = [('ATb', par, 2 * g), ('ATb', par, 2 * g + 1)]
                R.cp('pool', N_[g][0].rearrange("p (a c) b -> p a (c b)", c=2), n0, nk_, [('N', par, g, 0)])
                R.tt('pool', Z_[g][0].rearrange("p (a c) b -> p a (c b)", c=2), n0,
                     mI.rearrange("p (a b) -> p a b", b=256), ALU.add, nk_ + ['masks'], [('Z', par, g, 0)])
            level(1)
            level(2)

        def stB():
            level(3)

        def stC():
            level(4)

        def stD():
            level(5)
            level(6)

        return [stA, stB, stC, stD]

    step = [0]

    def seq_stages(z, tok0, s_, par):
        p = P[s_]
        atb, atk = ATb[par], ATk[par]
        TT = lambda h: Zb[par][h // 4][cur[par][h // 4]][:, h % 4, :]
        tkey = lambda h: ('Z', par, h // 4, cur[par][h // 4])
        st = {}

        def s1():
            for half in range(2):
                dcs = range(4 * half, 4 * half + 4)
                bg = bank()
                for dl, dc in enumerate(dcs):
                    o = dl * 128
                    R.mm(k.ps[bg][:, o:o + 128], p['kkP'][:, dc, :], Hb[:, dc, :], True, False,
                         [('kkP', s_), 'Hb'], [('ps', bg)])
                    for hh in range(2):
                        R.mm(k.ps[bg][:, o + hh * 64:o + hh * 64 + 64], atk[:, dc, hh * 128:(hh + 1) * 128],
                             p['v'][:, dc * 128 + hh * 64:dc * 128 + hh * 64 + 64], False, hh == 1,
                             [('ATk', par, dc), ('v', s_)], [('ps', bg)])
                R.act(fl(Gn[:, 4 * half:4 * half + 4, :]), k.ps[bg], AF.Copy, [('ps', bg)], [('Gn', half)], scale=-1.0)

        def s2():
            for half in range(2):
                dcs = range(4 * half, 4 * half + 4)
                bu = bank()
                for dl, dc in enumerate(dcs):
                    for hh in range(2):
                        h = 2 * dc + hh
                        o = dl * 128 + hh * 64
                        R.mm(k.ps[bu][:, o:o + 64], TT(h), Gn[:, dc, hh * 64:hh * 64 + 64], True, True,
                             [tkey(h), ('Gn', half)], [('ps', bu)])
                R.cp('dve', fl(Ub[:, 4 * half:4 * half + 4, :]), k.ps[bu], [('ps', bu)], [('Ub', half)])

        def s3():
            ys = Ys[step[0] % 2]
            yk = ('Ys', step[0] % 2)
            ysem = 'e_y%d' % (step[0] % 2)
            step[0] += 1
            hbanks = []
            for half in range(2):
                dcs = range(4 * half, 4 * half + 4)
                by = bank()
                for dl, dc in enumerate(dcs):
                    o = dl * 128
                    R.mm(k.ps[by][:, o:o + 128], p['rP'][:, dc, :], Hb[:, dc, :], True, False,
                         [('rP', s_), 'Hb'], [('ps', by)])
                    for hh in range(2):
                        oo = o + hh * 64
                        R.mm(k.ps[by][:, oo:oo + 64], atb[:, dc, 256 + hh * 128:256 + (hh + 1) * 128],
                             Ub[:, dc, hh * 64:hh * 64 + 64], False, False, [('ATb', par, dc), ('Ub', half)],
                             [('ps', by)])
                        R.mm(k.ps[by][:, oo:oo + 64], atk[:, dc, 256 + hh * 128:256 + (hh + 1) * 128],
                             p['v'][:, dc * 128 + hh * 64:dc * 128 + hh * 64 + 64], False, hh == 1,
                             [('ATk', par, dc), ('v', s_)], [('ps', by)])
                R.cp('act', ys[:, half * 512:(half + 1) * 512], k.ps[by], [('ps', by)], [yk])
                bh_ = bank()
                hbanks.append(bh_)
                for dl, dc in enumerate(dcs):
                    o = dl * 128
                    R.mm(k.ps[bh_][:, o:o + 128], p['bh'][:, dc * 128:(dc + 1) * 128], Ub[:, dc, :], True, False,
                         [('bh', s_), ('Ub', half)], [('ps', bh_)])
                    R.mm(k.ps[bh_][:, o:o + 128], p['kh'][:, dc * 128:(dc + 1) * 128],
                         p['v'][:, dc * 128:(dc + 1) * 128], False, True, [('kh', s_), ('v', s_)], [('ps', bh_)])
            for half in range(2):
                ps3 = k.ps[hbanks[half]].rearrange("p (a b) -> p a b", b=128)
                for hh in range(2):
                    rows = slice(hh * 64, hh * 64 + 64)
                    csl = slice(hh * 64, hh * 64 + 64)
                    hblk = Hf[rows, 4 * half:4 * half + 4, csl]
                    tblk = Ht[rows, 4 * half:4 * half + 4, :]
                    R.tt('dve', tblk, hblk, ps3[rows, :, csl], ALU.add, [('Hf', half, hh), ('ps', hbanks[half])],
                         [('Ht', half, hh)])
                    R.tt('pool', hblk, tblk,
                         p['pc'][rows, 4 * half:4 * half + 4].unsqueeze(2).to_broadcast([64, 4, 64]),
                         ALU.mult, [('Ht', half, hh), ('pc', s_)], [('Hf', half, hh)])
            hk = [('Hf', a, b) for a in range(2) for b in range(2)]
            R.cp('act', Hb[0:64, :, 0:64], Hf[0:64, :, 0:64], hk, ['Hb'])
            R.cp('act', Hb[64:128, :, 64:128], Hf[64:128, :, 64:128], hk, ['Hb'])
            R.dma('pool', d['Y%d' % z][tok0:tok0 + 128, :], ys, ysem, [yk], [])

        return [s1, s2, s3]

    slot = [0]
    parc = [0]
    HK = [('Hf', a, b) for a in range(2) for b in range(2)]

    def run(seq, z, init):
        nonlocal R
        t0, T_ = (0, cfg.TP) if seq == 0 else (cfg.TP, cfg.TS)
        nch = T_ // 128
        order = list(range(nch)) if z == 0 else list(range(nch - 1, -1, -1))
        toks = [t0 + n * 128 for n in order]
        if init is None:
            R.memset('dve', Hf, 0.0, HK)
            R.memset('pool', Hb, 0.0, ['Hb'])
        else:
            init()
        s0 = slot[0]
        p0 = parc[0]
        slot[0] += nch
        parc[0] += nch
        R_real = R
        R = Defer()
        try:
            emit_run(z, toks, nch, s0, p0)
        finally:
            ops_ = R.q
            R = R_real
        list_schedule(R, ops_)

    def emit_run(z, toks, nch, s0, p0):
        loads(z, toks[0], s0 % NS)
        if nch > 1:
            loads(z, toks[1], (s0 + 1) % NS)
        for stg_ in pre_stages(z, s0 % NS, p0 % 2):
            stg_()
        for ci in range(nch):
            if ci + 2 < nch:
                loads(z, toks[ci + 2], (s0 + ci + 2) % NS)
            sq_ = seq_stages(z, toks[ci], (s0 + ci) % NS, (p0 + ci) % 2)
            if ci + 1 < nch:
                nx = pre_stages(z, (s0 + ci + 1) % NS, (p0 + ci + 1) % 2)
            else:
                nx = [lambda: None] * 4
            nx[0]()
            sq_[0]()
            nx[1]()
            sq_[1]()
            nx[2]()
            sq_[2]()
            nx[3]()

    def publish():
        R.dma('pool', d['ex2_in'], fl(Hf), 'x2p', HK, ['ex2_in'])
        add_cc(R, lambda e: e.collective_compute("AllGather", ALU.bypass, replica_groups=PAIRS,
                                                 ins=[d['ex2_in']], outs=[d['ex2_out']]), ['ex2_in'], ['ex2o'], 'cc2')

    def init_from_partner():
        R.dma('sp', xg, d['ex2_out'].rearrange("(s p) c -> p s c", p=128), 'x2g', ['ex2o'], ['x2g'])
        hfl = fl(Hf)
        R.scp('dve', hfl, xg[:, 0, :], k.sel[:, 0:1], ['x2g'], HK)
        R.stt(hfl, xg[:, 1, :], k.sel[:, 1:2], hfl, ALU.mult, ALU.add, ['x2g'] + HK, HK)
        R.cp('act', Hb, Hf, HK, ['Hb'])

    run(1, 0, None)
    publish()
    run(0, 0, None)
    run(0, 1, None)
    run(1, 1, init_from_partner)
    R.barrier()


def phase_F(k):
    R, ar, cfg, d = k.R, k.ar, k.cfg, k.d
    ar.reset()
    Wo = ar.alloc([8, 1024], BF16)
    k.stg = [ar.alloc([1024], F32) for _ in range(4)]
    K.stgw = k.stgw = 1024
    load_w(k, Wo, d['w_out'], 8, 1024, 'WoR')
    y0s = [ar.alloc([4, 1024], F32) for _ in range(2)]
    y1s = [ar.alloc([4, 1024], F32) for _ in range(2)]
    ynb = ar.alloc([4, 1024], BF16)
    cvs = [ar.alloc([8, 512], BF16) for _ in range(2)]
    gts = [ar.alloc([8, 512], BF16) for _ in range(2)]
    zT = ar.alloc([8, 512], BF16)
    xTs = [ar.alloc([8, 512], F32) for _ in range(2)]
    s1 = ar.alloc([64], F32)
    s2 = ar.alloc([64], F32)
    mean = ar.alloc([64], F32)
    msq = ar.alloc([64], F32)
    var = ar.alloc([64], F32)
    rstd = ar.alloc([64], F32)
    yb = [ar.alloc([512], F32) for _ in range(2)]
    XT = fm(d['XT'])
    def floads(i):
        b = i % 2
        cols = slice(i * 512, (i + 1) * 512)
        R.dma('sp', y0s[b], tm(d['Y0'])[:, 4 * i:4 * i + 4, :], 'f_y0%d' % b, [], [('y0', b)])
        R.dma('sp', y1s[b], tm(d['Y1'])[:, 4 * i:4 * i + 4, :], 'f_y1%d' % b, [], [('y1', b)])
        R.dma('sp', cvs[b], fm(d['CVT'])[:, :, cols], 'f_cv%d' % b, [], [('cv', b)])
        R.dma('sp', gts[b], fm(d['GT'])[:, :, cols], 'f_g%d' % b, [], [('gt', b)])
        R.dma('sp', xTs[b], XT[:, :, cols], 'f_x%d' % b, [], [('xT', b, dc) for dc in range(8)])

    R_real = R
    R = k.R = Defer()
    floads(0)
    for i in range(cfg.NT_OWN):
        cols = slice(i * 512, (i + 1) * 512)
        pb_ = i % 2
        if i + 1 < cfg.NT_OWN:
            floads(i + 1)
        y0, y1, cv, gt, xT = y0s[pb_], y1s[pb_], cvs[pb_], gts[pb_], xTs[pb_]
        y0f = y0.rearrange("p a b -> p (a b)")
        y1f = y1.rearrange("p a b -> p (a b)")
        R.tt('pool', y0f, y0f, y1f, ALU.add, [('y0', pb_), ('y1', pb_)], [('y0', pb_)])
        y3 = y0.rearrange("p a (h n) -> p (a h) n", n=64)
        q3 = y1.rearrange("p a (h n) -> p (a h) n", n=64)
        R.red(s1, y3, ALU.add, [('y0', pb_)], ['s1'])
        R.tt('pool', y1f, y0f, y0f, ALU.mult, [('y0', pb_), ('y1', pb_)], [('y1', pb_)])
        R.red(s2, q3, ALU.add, [('y1', pb_)], ['s2'])
        R.ts1('dve', mean, s1, 1.0 / 64, ALU.mult, ['s1'], ['mean'])
        R.tt('dve', msq, mean, mean, ALU.mult, ['mean'], ['msq'])
        R.stt(var, s2, 1.0 / 64, msq, ALU.mult, ALU.subtract, ['s2', 'msq'], ['var'])
        R.act(var, var, AF.Sqrt, ['var'], ['var'], bias=64e-5, scale=1.0)
        R.recip(rstd, var, ['var'], ['rstd'])
        R.tt('dve', y3, y3, mean.unsqueeze(2).to_broadcast([128, 64, 64]), ALU.subtract, [('y0', pb_), 'mean'], [('y0', pb_)])
        R.tt('dve', ynb.rearrange("p a (h n) -> p (a h) n", n=64), y3,
             rstd.unsqueeze(2).to_broadcast([128, 64, 64]), ALU.mult, [('y0', pb_), 'rstd'], ['ynb'])
        for dc in range(8):
            pb = dc % 2
            for n in range(4):
                R.tr(k.psb[pb][:, n * 128:(n + 1) * 128], ynb[:, n, dc * 128:(dc + 1) * 128], k.ident_bf,
                     ['ynb'], [('ps', pb)])
            R.act(yb[dc % 2], k.psb[pb][:, 0:512], AF.Identity, [('ps', pb)], [('yb', dc % 2)],
                  bias=vcol(k, V_LNB, dc), scale=vcol(k, V_LNW, dc))
            R.tt('dve', yb[dc % 2], yb[dc % 2], cv[:, dc, :], ALU.add, [('yb', dc % 2), ('cv', pb_)], [('yb', dc % 2)])
            R.tt('pool', zT[:, dc, :], yb[dc % 2], gt[:, dc, :], ALU.mult, [('yb', dc % 2), ('gt', pb_)], [('zT', dc)])
        for dco in range(8):
            pb = 2 + dco % 4
            for dc in range(8):
                R.mm(k.ps[pb], Wo[:, dc, dco * 128:(dco + 1) * 128], zT[:, dc, :], dc == 0, dc == 7,
                     [('zT', dc), ('WoR', dc, 0)], [('ps', pb)])
            R.tt('dve', xT[:, dco, :], k.ps[pb], xT[:, dco, :], ALU.add, [('ps', pb), ('xT', pb_, dco)], [('xT', pb_, dco)])
        R.dma('pool', XT[:, :, cols], xT, 'f_x%d' % pb_, [('xT', pb_, dc) for dc in range(8)], [])
    ops_ = R.q
    R = k.R = R_real
    list_schedule(R, ops_)
    R.barrier()


def EXTRA_PHASES(k):
    return [('X1', lambda: phase_X1(k)), ('D', lambda: phase_D(k)), ('E', lambda: phase_E(k)),
            ('F', lambda: phase_F(k)), ('G', lambda: phase_FFN(k, 1))]


def build(cfg, upto='all', dbg=()):
    nc = bass.Bass("TRN2", target_bir_lowering=False)
    k = K()
    k.cfg = cfg
    k.nc = nc
    d = {}

    def inp(name, shape, dt=F32):
        d[name] = nc.dram_tensor(name, list(shape), dt, kind="ExternalInput").ap()

    def scr(name, shape, dt):
        kind = "ExternalOutput" if name in dbg else "Internal"
        d[name] = nc.dram_tensor(name, list(shape), dt, kind=kind).ap()

    inp('x_in', [cfg.NALL, D])
    inp('ropeC', [128, cfg.NALL])
    inp('ropeS', [128, cfg.NALL])
    inp('cmat', [128, 3, 128])
    inp('vecs', [128, NV - 1, 8])
    inp('smask', [128, 512])
    inp('masks', [128, 7, 512])
    inp('sel', [128, 2])
    inp('sublnw', [128, 1])
    inp('lamt', [64, 4])
    inp('w_qkv', [D, 3072])
    inp('w_qks', [D, 2048])
    inp('w_o', [D, D])
    inp('w_ffn_in', [2, D, 2 * FH])
    inp('w_ffn_out', [2, FH, D])
    inp('w_rkv', [3, D, D])
    inp('w1s', [D, 128])
    inp('w2p', [256, D])
    inp('a1s', [D, 128])
    inp('a2p', [256, D])
    inp('g1p', [D, 256])
    inp('g2p', [256, D])
    inp('w_out', [D, D])
    d['y_out'] = nc.dram_tensor('y_out', [cfg.NOWN, D], F32, kind="ExternalOutput").ap()
    scr('XT', [D, cfg.NOWN], F32)
    scr('QT', [D, cfg.NOWN], BF16)
    scr('KT', [D, cfg.NALL], BF16)
    scr('VV', [cfg.NALL, D], BF16)
    scr('OT', [D, cfg.NOWN], BF16)
    scr('XH', [D, cfg.XHW], BF16)
    NCH = cfg.NOWN // 128
    for z in range(2):
        for nm_ in ('RT', 'KKT', 'BT', 'KT2'):
            scr('%s%d' % (nm_, z), [D, cfg.NOWN], BF16)
        for nm_ in ('BH', 'KH'):
            scr('%s%d' % (nm_, z), [cfg.NOWN, D], BF16)
        scr('PC%d' % z, [D, NCH], F32)
        scr('Y%d' % z, [cfg.NOWN, D], F32)
    scr('VT', [cfg.NOWN, D], BF16)
    scr('GT', [D, cfg.NOWN], BF16)
    scr('CVT', [D, cfg.NOWN], BF16)
    d['ex2_in'] = nc.dram_tensor('ex2_in', [128, 1024], F32, kind="Internal").ap()
    d['ex2_out'] = nc.dram_tensor('ex2_out', [256, 1024], F32, kind="Internal").ap()
    d['ex1_in'] = nc.dram_tensor('ex1_in', [128, 8], F32, kind="Internal").ap()
    d['ex1_out'] = nc.dram_tensor('ex1_out', [256, 8], F32, kind="Internal").ap()
    k.d = d

    with ExitStack() as stack:
        sb = nc.alloc_sbuf_tensor("sball", [128, 206 * 1024 // 4], F32)
        k.ar = Arena(sb, 206 * 1024)
        k.pd = [nc.alloc_psum_tensor("pd%d" % i, [128, 1024], F32)[:, :] for i in range(4)]
        k.ps = [k.pd[i // 2][:, (i % 2) * 512:(i % 2) * 512 + 512] for i in range(8)]
        k.psb = [p.bitcast(BF16) for p in k.ps]
        R = Rec(nc)
        k.R = R
        phases = [('consts', lambda: phase_consts(k)), ('A', lambda: phase_A(k)), ('B', lambda: phase_B(k)),
                  ('C1', lambda: phase_C1(k)), ('C2', lambda: phase_FFN(k, 0))]
        extra = globals().get('EXTRA_PHASES')
        if extra:
            phases += extra(k)
        for name, fn in phases:
            fn()
            if name == upto:
                break
        R.barrier()
        sems = {}
        for i, ref in enumerate(sorted(R.semrefs, key=str)):
            sems[ref] = stack.enter_context(nc.semaphore("s%d" % i))
        k.n_instr = {e: len(R.lists[e]) for e in ENGS}
        block = stack.enter_context(nc.Block())

        def emit(eng):
            def body(e):
                for it in R.lists[eng]:
                    if it[0] == 'w':
                        e.wait_ge(sems[it[1]], it[2])
                    else:
                        ins = it[1](e)
                        if it[3] == 1:
                            ins.then_inc(sems[it[2]], 1)
                        elif it[3] == 16:
                            ins.then_inc(sems[it[2]], 16)
                        else:
                            ins.then_inc(sems[it[2]])
            return body

        block.tensor(emit('pe'))
        block.scalar(emit('act'))
        block.vector(emit('dve'))
        block.gpsimd(emit('pool'))
        block.sync(emit('sp'))
    return nc, k


def _fmvec(v):
    return np.ascontiguousarray(np.asarray(v, np.float32).reshape(8, 128).T)


def _tri_masks():
    s = np.arange(128)[:, None]
    t = np.arange(128)[None, :]
    su = (s < t).astype(np.float32)
    iu = (s <= t).astype(np.float32)
    sl = (s > t).astype(np.float32)
    il = (s >= t).astype(np.float32)
    I = np.eye(128, dtype=np.float32)
    m = np.zeros((128, 7, 512), np.float32)
    m[:, 0] = np.concatenate([-su, -su, iu, iu], 1)
    m[:, 1] = np.concatenate([su, su, iu, iu], 1)
    m[:, 2] = np.concatenate([-sl] * 4, 1)
    m[:, 3] = np.concatenate([-sl, -sl, il, il], 1)
    m[:, 4] = np.concatenate([sl, sl, il, il], 1)
    m[:, 5] = np.concatenate([-su] * 4, 1)
    m[:, 6] = np.concatenate([I] * 4, 1)
    return m


def _padz(a, b):
    o = np.zeros((256, D), np.float32)
    o[0:64] = a
    o[128 + 64:256] = b
    return o


def prep_inputs(inp, cfg, ncores=8):
    f32 = np.float32
    TP, TS = cfg.TP, cfg.TS
    inv_freq = (f32(ROPE_THETA) ** (-np.arange(0, 16, 2, dtype=f32) / f32(16))).astype(f32)
    wqkv = np.asarray(inp['attn_w_qkv'][0], f32)
    wqks = np.zeros((D, 2048), f32)
    for which in range(2):
        for blk in range(16):
            b0 = which * 1024 + blk * 64
            wqks[:, b0:b0 + 8] = wqkv[:, b0 + 8:b0 + 16]
            wqks[:, b0 + 8:b0 + 16] = wqkv[:, b0:b0 + 8]
    cm = np.zeros((128, 3, 128), f32)
    cm[:, 0] = np.eye(128, dtype=f32)
    cm[:, 1] = 1.0
    cm[0:64, 2, 0:64] = 1.0
    cm[64:128, 2, 64:128] = 1.0
    smask = np.ones((128, 512), f32)
    smask[:, 0::128] = 0.0
    masks = _tri_masks()
    lamt = np.stack([np.asarray(inp['attn_lambda_q1'][0], f32), np.asarray(inp['attn_lambda_k1'][0], f32),
                     np.asarray(inp['attn_lambda_q2'][0], f32), np.asarray(inp['attn_lambda_k2'][0], f32)], 1)
    sublnw = np.asarray(inp['attn_subln_w'][0], f32).reshape(128, 1)
    g1p = np.zeros((D, 256), f32)
    g1p[:, :160] = np.asarray(inp['rwkv_g1'][0], f32)
    g2p = np.zeros((256, D), f32)
    g2p[:160] = np.asarray(inp['rwkv_g2'][0], f32)
    shared = {
        'cmat': cm, 'smask': smask, 'masks': masks, 'lamt': np.ascontiguousarray(lamt), 'sublnw': sublnw,
        'w_qkv': wqkv, 'w_qks': wqks, 'w_o': np.asarray(inp['attn_w_o'][0], f32),
        'w_ffn_in': np.asarray(inp['ffn_w_in'], f32), 'w_ffn_out': np.asarray(inp['ffn_w_out'], f32),
        'w_rkv': np.asarray(inp['rwkv_w_rkv'][0], f32),
        'g1p': g1p, 'g2p': g2p,
        'w_out': np.asarray(inp['rwkv_w_out'][0], f32),
    }
    maps = []
    for c in range(ncores):
        s, q = c // 2, c % 2
        xp = np.asarray(inp['x_prompt'][c], f32)
        xs = np.asarray(inp['x_sample'][s], f32)
        pos_p = np.arange(TP)
        if q == 0:
            xo, xf = xs[:TS], xs[TS:]
            pos_o, pos_f = np.arange(TS), np.arange(TS, 2 * TS)
        else:
            xp = xp[::-1]
            pos_p = pos_p[::-1]
            xo, xf = xs[TS:][::-1], xs[:TS]
            pos_o, pos_f = np.arange(TS, 2 * TS)[::-1], np.arange(TS)
        x_in = np.ascontiguousarray(np.concatenate([xp, xo, xf], 0))
        pos = np.concatenate([pos_p, pos_o, pos_f]).astype(f32)
        ang = (pos[:, None] * inv_freq[None, :]).astype(f32)
        cos, sin = np.cos(ang).astype(f32), np.sin(ang).astype(f32)
        C = np.ones((128, cfg.NALL), f32)
        S = np.zeros((128, cfg.NALL), f32)
        for half in range(2):
            o = half * 64
            C[o:o + 8] = cos.T
            C[o + 8:o + 16] = cos.T
            S[o:o + 8] = -sin.T
            S[o + 8:o + 16] = sin.T
        p1, p2 = q, 1 - q
        vl = [inp['norm_mix'][0], inp['norm_mix'][1], inp['norm_ffn'][0], inp['norm_ffn'][1], inp['norm_final']]
        vl += [inp['rwkv_mu'][0][m] for m in range(6)]
        vl += [inp['rwkv_w0'][0][p1], inp['rwkv_w0'][0][p2], inp['rwkv_a0'][0][p1], inp['rwkv_a0'][0][p2]]
        vl += [inp['rwkv_k_k'][0], inp['rwkv_k_a'][0], inp['rwkv_r_k'][0], inp['rwkv_ln_w'][0], inp['rwkv_ln_b'][0]]
        vecs = np.ascontiguousarray(np.stack([_fmvec(v) for v in vl], 1))
        sel = np.zeros((128, 2), f32)
        sel[:, 1 - q] = 1.0
        w1, w2 = np.asarray(inp['rwkv_w1'][0], f32), np.asarray(inp['rwkv_w2'][0], f32)
        a1, a2 = np.asarray(inp['rwkv_a1'][0], f32), np.asarray(inp['rwkv_a2'][0], f32)
        m = dict(shared)
        m.update({
            'x_in': x_in, 'ropeC': C, 'ropeS': S, 'vecs': vecs, 'sel': sel,
            'w1s': np.ascontiguousarray(np.concatenate([w1[p1], w1[p2]], 1)),
            'w2p': _padz(w2[p1], w2[p2]),
            'a1s': np.ascontiguousarray(np.concatenate([a1[p1], a1[p2]], 1)),
            'a2p': _padz(a2[p1], a2[p2]),
        })
        maps.append(m)
    return maps


_CACHE = {}


def kernel(**inputs):
    cfg = Cfg(inputs['x_prompt'].shape[1], inputs['x_sample'].shape[1] // 2)
    key = (cfg.TP, cfg.TS)
    if key not in _CACHE:
        _CACHE[key] = build(cfg)
    nc, k = _CACHE[key]
    maps = prep_inputs(inputs, cfg)
    res = run_bass_kernel_spmd(nc, maps, core_ids=list(range(8)))
    B, S = inputs['x_prompt'].shape[0], inputs['x_prompt'].shape[1]
    DB, DS = inputs['x_sample'].shape[0], inputs['x_sample'].shape[1]
    yp = np.zeros((B, S, D), np.float32)
    ys = np.zeros((DB, DS, D), np.float32)
    for c in range(8):
        y = res.results[c]['y_out']
        s, q = c // 2, c % 2
        a, b = y[:cfg.TP], y[cfg.TP:]
        if q == 0:
            yp[c] = a
            ys[s, :cfg.TS] = b
        else:
            yp[c] = a[::-1]
            ys[s, cfg.TS:] = b[::-1]
    return yp, ys
```

```python
import math
from contextlib import ExitStack

import numpy as np
import concourse.bass as bass
import concourse.mybir as mybir
from concourse.bass_utils import run_bass_kernel_spmd

F32 = mybir.dt.float32
BF16 = mybir.dt.bfloat16
AF = mybir.ActivationFunctionType
ALU = mybir.AluOpType
AX = mybir.AxisListType

D = 1024
KC = 8
FH = 2816
FJ = 22
ROPE_THETA = 500000.0
SAME_ENG_SYNC = True
ENGS = ('pe', 'act', 'dve', 'pool', 'sp')

V_NMIX0, V_NMIX1, V_NFFN0, V_NFFN1, V_NFIN = 0, 1, 2, 3, 4
V_MU = 5
V_W0 = 11
V_A0 = 13
V_KK, V_KA, V_RK, V_LNW, V_LNB = 15, 16, 17, 18, 19
V_OMKA = 20
NV = 21


class Cfg:
    def __init__(self, TP=2048, TS=4096):
        self.TP, self.TS, self.TF = TP, TS, TS
        self.NOWN = TP + TS
        self.NALL = TP + 2 * TS
        self.NTP, self.NTS = TP // 512, TS // 512
        self.NT_OWN = self.NOWN // 512
        self.NT_ALL = self.NALL // 512
        self.XHW = self.NOWN + 4


class Rec:
    def __init__(self, nc):
        self.nc = nc
        self.lists = {e: [] for e in ENGS}
        self.cnt = {e: 0 for e in ENGS}
        self.lastw = {}
        self.readers = {}
        self.waited = {e: {} for e in ENGS}
        self.dcnt = {}
        self.semrefs = set()
        self.stgi = 0
        self.semmap = {}

    def add(self, eng, fn, reads=(), writes=(), dma=None):
        if dma is not None:
            dma = self.semmap.setdefault(dma, 'g%d' % len(self.semmap))
        deps = {}
        for k in reads:
            t = self.lastw.get(k)
            if t and deps.get(t[0], 0) < t[1]:
                deps[t[0]] = t[1]
        for k in writes:
            t = self.lastw.get(k)
            if t and deps.get(t[0], 0) < t[1]:
                deps[t[0]] = t[1]
            r = self.readers.get(k)
            if r:
                for s, v in r.items():
                    if deps.get(s, 0) < v:
                        deps[s] = v
        w = self.waited[eng]
        own = ('e', eng)
        for s in list(deps):
            if s[0] == 'd':
                deps[s] = self.dcnt[s]
        for s, v in deps.items():
            if s == own and (eng == 'pe' or eng == 'sp' or not SAME_ENG_SYNC):
                continue
            if w.get(s, 0) >= v:
                continue
            w[s] = v
            self.semrefs.add(s)
            self.lists[eng].append(('w', s, v))
        if dma is None:
            self.cnt[eng] += 1
            tok = (own, self.cnt[eng])
            self.lists[eng].append(('i', fn, own, 1))
        else:
            ref = ('d', dma)
            self.dcnt[ref] = self.dcnt.get(ref, 0) + 16
            tok = (ref, self.dcnt[ref])
            self.lists[eng].append(('i', fn, ref, 16))
        self.semrefs.add(tok[0])
        for k in writes:
            self.lastw[k] = tok
            self.readers[k] = {}
        for k in reads:
            d = self.readers.setdefault(k, {})
            if d.get(tok[0], 0) < tok[1]:
                d[tok[0]] = tok[1]
        return tok

    def barrier(self):
        for e in ENGS:
            w = self.waited[e]
            for x in ENGS:
                s = ('e', x)
                if x != e and self.cnt[x] > w.get(s, 0):
                    w[s] = self.cnt[x]
                    self.lists[e].append(('w', s, self.cnt[x]))
            for s, v in self.dcnt.items():
                if v > w.get(s, 0):
                    w[s] = v
                    self.lists[e].append(('w', s, v))
        self.lastw = {}
        self.readers = {}
        self.semmap = {}

    def mm(self, out, lhsT, rhs, st, sp, r, w):
        self.add('pe', lambda e: e.matmul(out, lhsT, rhs, start=st, stop=sp), r, w)

    def tr(self, out, in_, ident, r, w):
        self.add('pe', lambda e: e.transpose(out, in_, ident), r, w)

    def act(self, out, in_, func, r, w, bias=None, scale=None):
        kw = {}
        if bias is not None:
            kw['bias'] = bias
        if scale is not None:
            kw['scale'] = scale
        self.add('act', lambda e: e.activation(out=out, in_=in_, func=func, **kw), r, w)

    def tt(self, eng, out, in0, in1, op, r, w):
        self.add(eng, lambda e: e.tensor_tensor(out=out, in0=in0, in1=in1, op=op), r, w)

    def ts(self, eng, out, in0, s1, s2, op0, op1, r, w):
        self.add(eng, lambda e: e.tensor_scalar(out=out, in0=in0, scalar1=s1, scalar2=s2, op0=op0, op1=op1), r, w)

    def ts1(self, eng, out, in0, s1, op0, r, w):
        self.add(eng, lambda e: e.tensor_single_scalar(out=out, in_=in0, scalar=s1, op=op0), r, w)

    def stt(self, out, in0, scalar, in1, op0, op1, r, w):
        self.add('dve', lambda e: e.scalar_tensor_tensor(out=out, in0=in0, scalar=scalar, in1=in1, op0=op0, op1=op1), r, w)

    def cp(self, eng, out, in_, r, w):
        if eng == 'act':
            self.add('act', lambda e: e.activation(out=out, in_=in_, func=AF.Copy), r, w)
        else:
            self.add(eng, lambda e: e.tensor_copy(out=out, in_=in_), r, w)

    def scp(self, eng, out, in_, sc, r, w):
        if eng == 'act':
            self.add('act', lambda e: e.activation(out=out, in_=in_, func=AF.Copy, scale=sc), r, w)
        else:
            self.add(eng, lambda e: e.tensor_scalar_mul(out=out, in0=in_, scalar1=sc), r, w)

    def recip(self, out, in_, r, w):
        self.add('dve', lambda e: e.reciprocal(out=out, in_=in_), r, w)

    def red(self, out, in_, op, r, w):
        self.add('dve', lambda e: e.tensor_reduce(out=out, in_=in_, axis=AX.X, op=op), r, w)

    def scan(self, out, d0, d1, r, w):
        self.add('dve', lambda e: e.tensor_tensor_scan(out=out, data0=d0, data1=d1, initial=0.0,
                                                       op0=ALU.mult, op1=ALU.add), r, w)

    def memset(self, eng, ap, val, w):
        self.add(eng, lambda e: e.memset(ap, val), (), w)

    def dma(self, q, out, in_, sem, r, w, slow=False):
        if slow:
            self.add(q, lambda e: e.dma_start(out=out, in_=in_, allow_slow_non_contiguous=True), r, w, dma=sem)
        else:
            self.add(q, lambda e: e.dma_start(out=out, in_=in_), r, w, dma=sem)


class Defer:
    def __init__(self):
        self.q = []

    def __getattr__(self, name):
        return lambda *a, **kw: self.q.append((name, a, kw))


class _Probe(Rec):
    def __init__(self):
        self.meta = None

    def add(self, eng, fn, reads=(), writes=(), dma=None):
        self.meta = (eng, tuple(reads), tuple(writes), dma)


def _free_size(a):
    for x in a:
        sh = getattr(x, 'shape', None)
        if sh is not None and len(sh) >= 2:
            n = 1
            for v in sh[1:]:
                n *= int(v)
            return n
    return 64


def list_schedule(R, ops):
    import heapq
    probe = _Probe()
    n = len(ops)
    eng_of = [None] * n
    dur = [0.0] * n
    lat = [0.0] * n
    succ = [[] for _ in range(n)]
    npred = [0] * n
    lastw = {}
    readers = {}
    for i, (name, a, kw) in enumerate(ops):
        getattr(probe, name)(*a, **kw)
        eng, rd, wr, dma = probe.meta
        eng_of[i] = eng
        N = _free_size(a)
        if dma is not None:
            dur[i], lat[i] = 80.0, 2200.0
        elif eng == 'pe':
            dur[i] = 60.0 + 0.45 * N
        elif eng == 'act':
            dur[i] = 230.0 + 0.72 * N
        elif eng == 'dve':
            dur[i] = 70.0 + 1.1 * N
        else:
            dur[i] = 120.0 + 1.25 * N
        preds = set()
        for k_ in rd:
            t = lastw.get(k_)
            if t is not None:
                preds.add(t)
        for k_ in wr:
            t = lastw.get(k_)
            if t is not None:
                preds.add(t)
            for t in readers.get(k_, ()):
                preds.add(t)
        preds.discard(i)
        for p_ in preds:
            succ[p_].append(i)
        npred[i] = len(preds)
        for k_ in wr:
            lastw[k_] = i
            readers[k_] = []
        for k_ in rd:
            readers.setdefault(k_, []).append(i)
    SYNC = 350.0
    future = {e: [] for e in ENGS}
    avail = {e: [] for e in ENGS}
    rtime = [0.0] * n
    efree = {e: 0.0 for e in ENGS}
    for i in range(n):
        if npred[i] == 0:
            heapq.heappush(avail[eng_of[i]], i)
    order = []
    while len(order) < n:
        best = None
        for e in ENGS:
            f, av = future[e], avail[e]
            while f and f[0][0] <= efree[e]:
                heapq.heappush(av, heapq.heappop(f)[1])
            if av:
                c = (efree[e], av[0], e, True)
            elif f:
                c = (f[0][0], f[0][1], e, False)
            else:
                continue
            if best is None or c[:2] < best[:2]:
                best = c
        st_, i, e, from_av = best
        if from_av:
            heapq.heappop(avail[e])
        else:
            heapq.heappop(future[e])
        efree[e] = st_ + dur[i]
        fin_i = st_ + dur[i] + lat[i]
        order.append(i)
        for s_ in succ[i]:
            npred[s_] -= 1
            t_ = fin_i + (SYNC if (eng_of[s_] != e or lat[i]) else 0.0)
            if t_ > rtime[s_]:
                rtime[s_] = t_
            if npred[s_] == 0:
                heapq.heappush(future[eng_of[s_]], (rtime[s_], s_))
    for i in order:
        name, a, kw = ops[i]
        getattr(R, name)(*a, **kw)


class Arena:
    def __init__(self, sb, cap):
        self.sb, self.cap, self.off, self.base = sb, cap, 0, 0

    def alloc(self, shape, dt):
        n = int(np.prod(shape))
        sz = 4 if dt == F32 else 2
        nb = (n * sz + 63) // 64 * 64
        assert self.off + nb <= self.cap, ("SBUF arena overflow", self.off, nb, self.cap)
        a = self.sb[:, self.off // 4:(self.off + nb) // 4]
        self.off += nb
        if dt == BF16:
            a = a.bitcast(BF16)
        a = a[:, 0:n]
        if len(shape) == 2:
            a = a.rearrange("p (a b) -> p a b", b=shape[1])
        elif len(shape) == 3:
            a = a.rearrange("p (a b c) -> p a b c", b=shape[1], c=shape[2])
        return a

    def set_base(self):
        self.base = self.off

    def reset(self):
        self.off = self.base


def fm(ap):
    return ap.rearrange("(k p) t -> p k t", p=128)


def tm(ap):
    return ap.rearrange("(n p) c -> p n c", p=128)


class K:
    pass


def load_w(k, dst, src, nk, ncols, name, scale=None, last_rows=128, cb=1024):
    cb = k.stgw
    R = k.R
    for kc in range(nk):
        rows = 128 if kc < nk - 1 else last_rows
        for c0 in range(0, ncols, cb):
            c1 = min(ncols, c0 + cb)
            b = R.stgi % len(k.stg)
            R.stgi += 1
            st = k.stg[b][0:rows, 0:c1 - c0]
            R.dma('sp' if R.stgi % 2 else 'act', st, src[kc * 128:kc * 128 + rows, c0:c1], 'stg%d' % b, [], [('stg', b)])
            eng = 'act' if (R.stgi % 2) else 'dve'
            o = dst[0:rows, kc, c0:c1]
            if scale is not None:
                R.scp(eng, o, st, scale(kc)[0:rows], [('stg', b)], [(name, kc, c0 // cb)])
            else:
                R.cp(eng, o, st, [('stg', b)], [(name, kc, c0 // cb)])


def wkeys(name, kc, c0, c1, cb=None):
    cb = cb or K.stgw
    return [(name, kc, j) for j in range(c0 // cb, (c1 - 1) // cb + 1)]


def vcol(k, v, kc):
    return k.vecs[:, v, kc:kc + 1]


def rms_rstd(k, xT, xkey, nchunks, inv_n, eps, psb, sq, rs, rstd, tagsq):
    R = k.R
    for kc in range(nchunks):
        s = sq[kc % 2]
        R.act(s, xT[:, kc, :], AF.Square, [(xkey, kc)], [(tagsq, kc % 2)])
        R.mm(k.ps[psb], k.ones_bf, s, kc == 0, kc == nchunks - 1, [(tagsq, kc % 2)], [('ps', psb)])
    R.act(rs, k.ps[psb], AF.Sqrt, [('ps', psb)], [tagsq + 'rs'], bias=eps, scale=inv_n)
    R.recip(rstd, rs, [tagsq + 'rs'], [tagsq + 'rstd'])


def phase_consts(k):
    R, ar, cfg = k.R, k.ar, k.cfg
    k.cmf = ar.alloc([3, 128], F32)
    k.cmb = ar.alloc([3, 128], BF16)
    k.vecs = ar.alloc([NV, 8], F32)
    k.sel = ar.alloc([2], F32)
    k.sublnw = ar.alloc([1], F32)
    k.lamt = ar.alloc([4], F32)
    k.qkmax = ar.alloc([2, 2, 8], F32)
    k.negM = ar.alloc([2, 8], F32)
    k.lamneg = ar.alloc([1], F32)
    k.zero_bf = ar.alloc([8], BF16)
    k.eps5 = ar.alloc([1], F32)
    ar.set_base()
    R.dma('sp', k.cmf, k.d['cmat'], 'c0', [], ['cmf'])
    R.dma('sp', k.vecs[:, 0:NV - 1, :], k.d['vecs'], 'c1', [], ['vecs'])
    R.dma('sp', k.sel, k.d['sel'], 'c4', [], ['sel'])
    R.dma('sp', k.sublnw, k.d['sublnw'], 'c5', [], ['sublnw'])
    R.dma('sp', k.lamt[0:64, :], k.d['lamt'], 'c6', [], ['lamt'])
    R.cp('dve', k.cmb, k.cmf, ['cmf'], ['cmb'])
    R.memset('dve', k.qkmax, 0.0, ['qkmax'])
    R.memset('dve', k.zero_bf, 0.0, ['zero_bf'])
    R.memset('dve', k.eps5, 1e-5, ['eps5'])
    R.ts('dve', k.vecs[:, V_OMKA, :], k.vecs[:, V_KA, :], -1.0, 1.0, ALU.mult, ALU.add, ['vecs'], ['vecs'])
    k.ident_f = k.cmf[:, 0, :]
    k.ones_f = k.cmf[:, 1, :]
    k.ident_bf = k.cmb[:, 0, :]
    k.ones_bf = k.cmb[:, 1, :]
    k.onesbd_bf = k.cmb[:, 2, :]
    R.barrier()


def phase_A(k):
    R, ar, cfg, d = k.R, k.ar, k.cfg, k.d
    ar.reset()
    Wqkv = ar.alloc([8, 3072], BF16)
    Wqks = ar.alloc([8, 2048], BF16)
    k.stg = [ar.alloc([1024], F32) for _ in range(4)]
    K.stgw = k.stgw = 1024
    load_w(k, Wqkv, d['w_qkv'], 8, 3072, 'Wqkv', scale=lambda kc: vcol(k, V_NMIX0, kc))
    load_w(k, Wqks, d['w_qks'], 8, 2048, 'Wqks', scale=lambda kc: vcol(k, V_NMIX0, kc))
    xtok = [ar.alloc([4, 1024], F32) for _ in range(2)]
    xT = ar.alloc([8, 512], F32)
    sq = [ar.alloc([512], BF16) for _ in range(2)]
    rs = ar.alloc([512], F32)
    rstd = ar.alloc([512], F32)
    xn = ar.alloc([8, 512], BF16)
    Ct = [ar.alloc([512], F32) for _ in range(2)]
    St = [ar.alloc([512], F32) for _ in range(2)]
    t1 = [ar.alloc([512], F32) for _ in range(2)]
    t2 = [ar.alloc([512], F32) for _ in range(2)]
    qr = [ar.alloc([512], BF16) for _ in range(3)]
    sq2 = [ar.alloc([512], BF16) for _ in range(2)]
    mx = [ar.alloc([1], F32) for _ in range(2)]
    vtok = ar.alloc([4, 1024], BF16)
    xin = tm(d['x_in'])
    XT = fm(d['XT'])

    def loads(i):
        b = i % 2
        R.dma('sp', xtok[b], xin[:, 4 * i:4 * i + 4, :], 'xtok%d' % b, [], [('xtok', b)])
        R.dma('sp', Ct[b], d['ropeC'][:, i * 512:(i + 1) * 512], 'rope%d' % b, [], [('Ct', b)])
        R.dma('sp', St[b], d['ropeS'][:, i * 512:(i + 1) * 512], 'rope%d' % b, [], [('St', b)])

    R_real = R
    R = k.R = Defer()
    loads(0)
    cnt = 0
    pend = []
    for i in range(cfg.NT_ALL):
        own = i < cfg.NT_OWN
        seq = 0 if i < cfg.NTP else 1
        b = i % 2
        if i + 1 < cfg.NT_ALL:
            loads(i + 1)
        for kc in range(8):
            pb = kc % 2
            for n in range(4):
                R.tr(k.ps[pb][:, n * 128:(n + 1) * 128], xtok[b][:, n, kc * 128:(kc + 1) * 128], k.ident_f,
                     [('xtok', b)], [('ps', pb)])
            R.cp('act' if kc % 2 else 'dve', xT[:, kc, :], k.ps[pb], [('ps', pb)], [('xT', kc)])
        if own:
            R.dma('pool', XT[:, :, i * 512:(i + 1) * 512], xT, 'xTst', [('xT', kc) for kc in range(8)], [])
        rms_rstd(k, xT, 'xT', 8, 1.0 / D, 1e-6, 2, sq, rs, rstd, 'A')
        for kc in range(8):
            R.tt('pool' if kc % 2 else 'dve', xn[:, kc, :], xT[:, kc, :], rstd, ALU.mult,
                 [('xT', kc), 'Arstd'], [('xn', kc)])
        xnk = [('xn', kc) for kc in range(8)]
        for which in ((0, 1) if own else (1,)):
            for hc in range(8):
                pa, pb2 = (3, 4) if cnt % 2 == 0 else (5, 6)
                c0 = which * 1024 + hc * 128
                for kc in range(8):
                    R.mm(k.ps[pa], Wqkv[:, kc, c0:c0 + 128], xn[:, kc, :], kc == 0, kc == 7,
                         [('xn', kc)] + wkeys('Wqkv', kc, c0, c0 + 128), [('ps', pa)])
                for kc in range(8):
                    R.mm(k.ps[pb2], Wqks[:, kc, c0:c0 + 128], xn[:, kc, :], kc == 0, kc == 7,
                         [('xn', kc)] + wkeys('Wqks', kc, c0, c0 + 128), [('ps', pb2)])
                tb = cnt % 2
                R.tt('dve', t1[tb], k.ps[pa], Ct[b], ALU.mult, [('ps', pa), ('Ct', b)], [('t1', tb)])
                R.tt('dve', t2[tb], k.ps[pb2], St[b], ALU.mult, [('ps', pb2), ('St', b)], [('t2', tb)])
                qb = cnt % 3
                R.tt('pool', qr[qb], t1[tb], t2[tb], ALU.add, [('t1', tb), ('t2', tb)], [('qr', qb)])
                dst = d['QT'] if which == 0 else d['KT']
                R.dma('pool', dst[hc * 128:(hc + 1) * 128, i * 512:(i + 1) * 512], qr[qb], 'qr%d' % qb,
                      [('qr', qb)], [])
                R.act(sq2[tb], qr[qb], AF.Square, [('qr', qb)], [('sq2', tb)])
                if pend:
                    pend.pop()()

                def stats(tb=tb, dstm=k.qkmax[:, which, seq, hc:hc + 1]):
                    R.mm(k.ps[7], k.ones_bf, sq2[tb], True, True, [('sq2', tb)], [('ps', 7)])
                    R.red(mx[tb], k.ps[7], ALU.max, [('ps', 7)], [('mx', tb)])
                    R.tt('dve', dstm, dstm, mx[tb], ALU.max, [('mx', tb), 'qkmax'], ['qkmax'])

                pend.append(stats)
                cnt += 1
        for n in range(4):
            for cbk in range(2):
                pv = 3 + (cnt % 4)
                cnt += 1
                c0 = 2048 + cbk * 512
                for kc in range(8):
                    R.mm(k.ps[pv], xn[:, kc, n * 128:(n + 1) * 128], Wqkv[:, kc, c0:c0 + 512], kc == 0, kc == 7,
                         [('xn', kc)] + wkeys('Wqkv', kc, c0, c0 + 512), [('ps', pv)])
                R.cp('act' if cbk else 'dve', vtok[:, n, cbk * 512:(cbk + 1) * 512], k.ps[pv],
                     [('ps', pv)], [('vtok', n, cbk)])
        R.dma('pool', tm(d['VV'])[:, 4 * i:4 * i + 4, :], vtok, 'vtok',
              [('vtok', n, c) for n in range(4) for c in range(2)], [])
    if pend:
        pend.pop()()
    ops_ = R.q
    R = k.R = R_real
    list_schedule(R, ops_)
    R.barrier()


def phase_B(k):
    R, ar, cfg, d = k.R, k.ar, k.cfg, k.d
    ar.reset()
    SKM = cfg.TS + cfg.TF
    KhT = [ar.alloc([SKM], BF16) for _ in range(2)]
    Vh = [ar.alloc([SKM // 128, 128], BF16) for _ in range(2)]
    qz = [ar.alloc([2, 512], BF16) for _ in range(2)]
    et = [ar.alloc([2, 512], BF16) for _ in range(4)]
    accD = ar.alloc([512], F32)
    accP = ar.alloc([512], F32)
    accb = ar.alloc([512], BF16)
    nsb = ar.alloc([3, 512], F32)
    rr = ar.alloc([2, 512], F32)
    lnl = ar.alloc([2, 512], F32)
    tq = ar.alloc([2, 512], F32)
    o = ar.alloc([512], F32)
    sq = ar.alloc([512], BF16)
    rs = ar.alloc([512], F32)
    rstd = ar.alloc([512], F32)
    oT = [ar.alloc([512], BF16) for _ in range(2)]
    tmp = ar.alloc([4], F32)
    fl = lambda x: x.rearrange("p a b -> p (a b)")
    R.tt('dve', tmp[0:64, 0:1], k.lamt[0:64, 0:1], k.lamt[0:64, 1:2], ALU.mult, ['lamt'], ['ltmp'])
    R.tt('dve', tmp[0:64, 1:2], k.lamt[0:64, 2:3], k.lamt[0:64, 3:4], ALU.mult, ['lamt', 'ltmp'], ['ltmp'])
    R.mm(k.ps[0][:, 0:2], k.ones_f[0:64, :], tmp[0:64, 0:2], True, True, ['ltmp'], [('ps', 0)])
    R.act(tmp[:, 2:4], k.ps[0][:, 0:2], AF.Exp, [('ps', 0)], ['ltmp2'])
    R.tt('dve', k.lamneg, tmp[:, 3:4], tmp[:, 2:3], ALU.subtract, ['ltmp2'], ['lamneg'])
    R.ts1('dve', k.lamneg, k.lamneg, -0.2, ALU.add, ['lamneg'], ['lamneg'])
    R.tt('dve', k.negM, k.qkmax[:, 0], k.qkmax[:, 1], ALU.mult, ['qkmax'], ['negM'])
    R.act(k.negM, k.negM, AF.Sqrt, ['negM'], ['negM'])
    R.ts1('dve', k.negM, k.negM, -0.125, ALU.mult, ['negM'], ['negM'])
    for b in range(2):
        R.memset('dve', qz[b], 0.0, [('qz', b)])
    R_real = R
    R = k.R = Defer()
    VVt = tm(d['VV'])
    hcount = 0
    qcount = 0
    ecount = 0
    for seq in range(2):
        if seq == 0:
            Sq, q0, Sk, k0 = cfg.TP, 0, cfg.TP, 0
        else:
            Sq, q0, Sk, k0 = cfg.TS, cfg.TP, cfg.TS + cfg.TF, cfg.TP
        nk = Sk // 128
        for h in range(8):
            hb = hcount % 2
            hcount += 1
            R.dma('sp', KhT[hb][:, 0:Sk], d['KT'][h * 128:(h + 1) * 128, k0:k0 + Sk], 'kv%d' % hb,
                  [], [('KhT', hb)])
            for n0 in range(0, nk, 16):
                n1 = min(nk, n0 + 16)
                R.dma('sp', Vh[hb][:, n0:n1, :], VVt[:, k0 // 128 + n0:k0 // 128 + n1, h * 128:(h + 1) * 128],
                      'kv%d' % hb, [], [('Vh', hb)])
            negM = k.negM[:, seq, h:h + 1]
            for qt in range(Sq // 512):
                qb = qcount % 2
                qcount += 1
                qc0 = q0 + qt * 512
                R.dma('sp', qz[qb][0:64, 0, :], d['QT'][h * 128:h * 128 + 64, qc0:qc0 + 512], 'qz%d' % qb,
                      [], [('qz', qb)])
                R.dma('sp', qz[qb][64:128, 1, :], d['QT'][h * 128 + 64:h * 128 + 128, qc0:qc0 + 512], 'qz%d' % qb,
                      [], [('qz', qb)])

                def smm(j):
                    sb = (j % 2) * 2
                    for c in range(2):
                        R.mm(k.ps[sb + c], KhT[hb][:, j * 128:(j + 1) * 128], qz[qb][:, c, :], True, True,
                             [('KhT', hb), ('qz', qb)], [('ps', sb + c)])

                smm(0)
                inited = {'D': False, 'P': False}
                for j in range(nk):
                    if j + 1 < nk:
                        smm(j + 1)
                    sb = (j % 2) * 2
                    eb = ecount % 4
                    ecount += 1
                    e2 = et[eb]
                    R.act(fl(e2), k.pd[j % 2], AF.Exp, [('ps', sb), ('ps', sb + 1), 'negM'],
                          [('et', eb)], bias=negM, scale=0.125)
                    for c in range(2):
                        R.mm(k.ps[4 + c], Vh[hb][:, j, :], e2[:, c, :], j == 0, j == nk - 1,
                             [('Vh', hb), ('et', eb)], [('ps', 4 + c)])
                    R.mm(k.ps[6], k.ones_bf, e2[:, 0, :], j == 0, j == nk - 1, [('et', eb)], [('ps', 6)])
                    who = 'D'
                    eng, acc = ('pool', accP) if who == 'P' else ('dve', accD)
                    if not inited[who]:
                        R.cp(eng, acc, e2[:, 1, :], [('et', eb)], ['acc' + who])
                        inited[who] = True
                    else:
                        R.tt(eng, acc, acc, e2[:, 1, :], ALU.add, [('et', eb), 'acc' + who], ['acc' + who])
                for c in range(3):
                    R.cp('dve', nsb[:, c, :], k.ps[4 + c], [('ps', 4 + c)], [('nsb', c)])
                if inited['P']:
                    R.tt('pool', accb, accD, accP, ALU.add, ['accD', 'accP'], ['accb'])
                else:
                    R.cp('pool', accb, accD, ['accD'], ['accb'])
                R.mm(k.ps[7], k.ones_bf, accb, True, True, ['accb'], [('ps', 7)])
                R.act(lnl[:, 0, :], nsb[:, 2, :], AF.Ln, [('nsb', 2)], [('lnl', 0)])
                R.act(lnl[:, 1, :], k.ps[7], AF.Ln, [('ps', 7)], [('lnl', 1)])
                R.act(fl(rr), fl(lnl), AF.Exp, [('lnl', 0), ('lnl', 1)], [('rr', 0), ('rr', 1)], scale=-1.0)
                R.tt('pool', fl(tq), fl(nsb[:, 0:2, :]), fl(rr), ALU.mult, [('nsb', 0), ('nsb', 1), ('rr', 0), ('rr', 1)],
                     ['tq'])
                R.stt(o, tq[:, 1, :], k.lamneg, tq[:, 0, :], ALU.mult, ALU.add, ['tq', 'lamneg'], ['o'])
                R.tt('pool', sq, o, o, ALU.mult, ['o'], ['osq'])
                R.mm(k.ps[7], k.ones_bf, sq, True, True, ['osq'], [('ps', 7)])
                R.act(rs, k.ps[7], AF.Ln, [('ps', 7)], ['ors'], bias=k.eps5, scale=1.0 / 128)
                R.act(rstd, rs, AF.Exp, ['ors'], ['orstd'], scale=-0.5)
                ob = qcount % 2
                R.tt('pool', oT[ob], o, rstd, ALU.mult, ['o', 'orstd'], [('oT', ob)])
                R.dma('pool', d['OT'][h * 128:(h + 1) * 128, qc0:qc0 + 512], oT[ob], 'oT%d' % ob, [('oT', ob)], [])
    ops_ = R.q
    R = k.R = R_real
    list_schedule(R, ops_)
    R.barrier()


def phase_C1(k):
    R, ar, cfg, d = k.R, k.ar, k.cfg, k.d
    ar.reset()
    Wo = ar.alloc([8, 1024], BF16)
    k.stg = [ar.alloc([1024], F32) for _ in range(4)]
    K.stgw = k.stgw = 1024
    sc = ar.alloc([1], F32)
    R.ts1('dve', sc, k.sublnw, 0.8, ALU.mult, ['sublnw'], ['sc'])
    load_w(k, Wo, d['w_o'], 8, 1024, 'Wo', scale=lambda kc: sc)
    oT = [ar.alloc([8, 512], BF16) for _ in range(2)]
    xT = [ar.alloc([8, 512], F32) for _ in range(2)]
    XT = fm(d['XT'])
    OT = fm(d['OT'])

    def loads(i):
        b = i % 2
        R.dma('sp', oT[b], OT[:, :, i * 512:(i + 1) * 512], 'c1o%d' % b, [], [('oT', b)])
        R.dma('sp', xT[b], XT[:, :, i * 512:(i + 1) * 512], 'c1x%d' % b, [], [('xT', b, kc) for kc in range(8)])

    R_real = R
    R = k.R = Defer()
    loads(0)
    for i in range(cfg.NT_OWN):
        b = i % 2
        if i + 1 < cfg.NT_OWN:
            loads(i + 1)
        for dc in range(8):
            pb = dc % 4
            for h in range(8):
                R.mm(k.ps[pb], Wo[:, h, dc * 128:(dc + 1) * 128], oT[b][:, h, :], h == 0, h == 7,
                     [('oT', b)] + wkeys('Wo', h, dc * 128, (dc + 1) * 128), [('ps', pb)])
            R.tt('dve', xT[b][:, dc, :], k.ps[pb], xT[b][:, dc, :], ALU.add, [('ps', pb), ('xT', b, dc)],
                 [('xT', b, dc)])
        R.dma('pool', XT[:, :, i * 512:(i + 1) * 512], xT[b], 'c1x%d' % b, [('xT', b, kc) for kc in range(8)], [])
    ops_ = R.q
    R = k.R = R_real
    list_schedule(R, ops_)
    R.barrier()


def phase_FFN(k, layer):
    R, ar, cfg, d = k.R, k.ar, k.cfg, k.d
    ar.reset()
    Win = ar.alloc([8, 2 * FH], BF16)
    Wout = ar.alloc([FJ, 1024], BF16)
    hidraw = ar.alloc([FJ * 256], F32)
    k.stg = [hidraw[:, j * 1024:(j + 1) * 1024] for j in range(5)]
    K.stgw = k.stgw = 1024
    vn = V_NFFN0 if layer == 0 else V_NFFN1
    load_w(k, Win, d['w_ffn_in'][layer], 8, 2 * FH, 'Win', scale=lambda kc: vcol(k, vn, kc))
    load_w(k, Wout, d['w_ffn_out'][layer], FJ, 1024, 'Wout')
    R.barrier()
    R_real = R
    R = k.R = Defer()
    xT = ar.alloc([8, 512], F32)
    xn = ar.alloc([8, 512], BF16)
    hid = hidraw.bitcast(BF16).rearrange("p (a b) -> p a b", b=512)
    ytok = hidraw[:, 0:4096].rearrange("p (n c) -> p n c", c=1024)
    sq = [ar.alloc([512], BF16) for _ in range(2)]
    rs = ar.alloc([512], F32)
    rstd = ar.alloc([512], F32)
    sg = [ar.alloc([512], F32) for _ in range(2)]
    lastc = ar.alloc([8], F32)
    xh = xn
    yT = xT
    XT = fm(d['XT'])
    XH = fm(d['XH'])
    if layer == 0:
        for c in (0, cfg.TP + 1, cfg.TP + 2):
            R.dma('pool', XH[:, :, c:c + 1], k.zero_bf.unsqueeze(2), 'xhz', ['zero_bf'], [], slow=True)
    xk = [('xT', kc) for kc in range(8)]
    for i in range(cfg.NT_OWN):
        R.dma('sp', xT, XT[:, :, i * 512:(i + 1) * 512], 'fx', [], xk)
        rms_rstd(k, xT, 'xT', 8, 1.0 / D, 1e-6, 7, sq, rs, rstd, 'F')
        for kc in range(8):
            R.tt('pool' if kc % 2 else 'dve', xn[:, kc, :], xT[:, kc, :], rstd, ALU.mult,
                 [('xT', kc), 'Frstd'], [('xn', kc)])
        for j in range(FJ):
            pg, pu = (0, 1) if j % 2 == 0 else (2, 3)
            for kc in range(8):
                R.mm(k.ps[pg], Win[:, kc, j * 128:(j + 1) * 128], xn[:, kc, :], kc == 0, kc == 7,
                     [('xn', kc)] + wkeys('Win', kc, j * 128, (j + 1) * 128), [('ps', pg)])
            for kc in range(8):
                c0 = FH + j * 128
                R.mm(k.ps[pu], Win[:, kc, c0:c0 + 128], xn[:, kc, :], kc == 0, kc == 7,
                     [('xn', kc)] + wkeys('Win', kc, c0, c0 + 128), [('ps', pu)])
            R.act(sg[j % 2], k.ps[pg], AF.Silu, [('ps', pg)], [('sg', j % 2)])
            R.tt('dve', hid[:, j, :], k.ps[pu], sg[j % 2], ALU.mult, [('ps', pu), ('sg', j % 2)], [('hid', j)])
        for dc in range(8):
            pb = 4 + dc % 2
            for j in range(FJ):
                R.mm(k.ps[pb], Wout[:, j, dc * 128:(dc + 1) * 128], hid[:, j, :], j == 0, j == FJ - 1,
                     [('hid', j)] + wkeys('Wout', j, dc * 128, (dc + 1) * 128), [('ps', pb)])
            R.tt('dve', xT[:, dc, :], k.ps[pb], xT[:, dc, :], ALU.add, [('ps', pb), ('xT', dc)], [('xT', dc)])
        rms_rstd(k, xT, 'xT', 8, 1.0 / D, 1e-6, 6, sq, rs, rstd, 'G')
        if layer == 0:
            R.dma('pool', XT[:, :, i * 512:(i + 1) * 512], xT, 'fxs', xk, [])
            for kc in range(8):
                R.tt('pool' if kc % 2 else 'dve', xh[:, kc, :], xT[:, kc, :], rstd, ALU.mult,
                     [('xT', kc), 'Grstd'], [('xn', kc)])
            xhk = [('xn', kc) for kc in range(8)]
            base = 1 + i * 512 if i < cfg.NTP else cfg.TP + 3 + (i - cfg.NTP) * 512
            R.dma('pool', XH[:, :, base:base + 512], xh, 'fxh', xhk, [])
            if i == cfg.NT_OWN - 1:
                R.cp('dve', lastc, xh[:, :, 511], xhk, ['lastc'])
                R.dma('pool', d['ex1_in'], lastc, 'ex1', ['lastc'], ['ex1_in'])
        else:
            for kc in range(8):
                R.stt(yT[:, kc, :], xT[:, kc, :], vcol(k, V_NFIN, kc), rstd, ALU.mult, ALU.mult,
                      [('xT', kc), 'Grstd'], [('xT', kc)])
            for n in range(4):
                for half in range(2):
                    pb = half
                    for q4 in range(4):
                        kc = half * 4 + q4
                        R.tr(k.ps[pb][:, q4 * 128:(q4 + 1) * 128], yT[:, kc, n * 128:(n + 1) * 128], k.ident_f,
                             [('xT', kc)], [('ps', pb)])
                    R.cp('act' if half else 'dve', ytok[:, n, half * 512:(half + 1) * 512], k.ps[pb],
                         [('ps', pb)], [('hid', jj) for jj in range(FJ)])
            R.dma('pool', tm(d['y_out'])[:, 4 * i:4 * i + 4, :], ytok, 'yst',
                  [('hid', jj) for jj in range(FJ)], [])
    ops_ = R.q
    R = k.R = R_real
    list_schedule(R, ops_)
    R.barrier()


CDEC = math.exp(-0.5)
PAIRS = [[0, 1], [2, 3], [4, 5], [6, 7]]


def add_cc(R, fn, r, w, name):
    deps = {}
    for kk_ in list(r) + list(w):
        t = R.lastw.get(kk_)
        if t and deps.get(t[0], 0) < t[1]:
            deps[t[0]] = t[1]
    wd = R.waited['pool']
    for s_, v in deps.items():
        if s_[0] == 'd':
            v = R.dcnt[s_]
        if wd.get(s_, 0) >= v:
            continue
        wd[s_] = v
        R.semrefs.add(s_)
        R.lists['pool'].append(('w', s_, v))
    ref = ('d', 'cc_' + name)
    R.dcnt[ref] = R.dcnt.get(ref, 0) + 1
    tok = (ref, R.dcnt[ref])
    R.lists['pool'].append(('i', fn, ref, 0))
    R.semrefs.add(ref)
    for kk_ in w:
        R.lastw[kk_] = tok
        R.readers[kk_] = {}


def phase_X1(k):
    R, ar, cfg, d = k.R, k.ar, k.cfg, k.d
    ar.reset()
    g = ar.alloc([2, 8], F32)
    t = ar.alloc([8], F32)
    pb = ar.alloc([8], BF16)
    add_cc(R, lambda e: e.collective_compute("AllGather", ALU.bypass, replica_groups=PAIRS,
                                             ins=[d['ex1_in']], outs=[d['ex1_out']]), [], ['ex1o'], 'cc1')
    R.dma('sp', g, d['ex1_out'].rearrange("(s p) c -> p s c", p=128), 'x1g', ['ex1o'], ['x1g'])
    R.scp('dve', t, g[:, 0, :], k.sel[:, 0:1], ['x1g'], ['x1t'])
    R.stt(pb, g[:, 1, :], k.sel[:, 1:2], t, ALU.mult, ALU.add, ['x1g', 'x1t'], ['x1p'])
    R.dma('sp', fm(d['XH'])[:, :, cfg.XHW - 1:cfg.XHW], pb.unsqueeze(2), 'x1s', ['x1p'], [], slow=True)
    R.barrier()


def phase_D(k):
    R, ar, cfg, d = k.R, k.ar, k.cfg, k.d
    ar.reset()
    TD = 256
    NTD = cfg.NOWN // TD
    Wrkv = ar.alloc([8, 3072], BF16)
    W1s = ar.alloc([8, 128], BF16)
    A1s = ar.alloc([8, 128], BF16)
    G1 = ar.alloc([8, 256], BF16)
    W2p = ar.alloc([2, 1024], BF16)
    A2p = ar.alloc([2, 1024], BF16)
    G2 = ar.alloc([2, 1024], BF16)
    k.stg = [ar.alloc([1024], F32) for _ in range(2)]
    K.stgw = k.stgw = 1024
    nm = lambda kc: vcol(k, V_NMIX1, kc)
    for j in range(3):
        load_w(k, Wrkv[:, :, j * 1024:(j + 1) * 1024], d['w_rkv'][j], 8, 1024, 'Wrkv%d' % j, scale=nm)
    load_w(k, W1s, d['w1s'], 8, 128, 'W1s', scale=nm)
    load_w(k, A1s, d['a1s'], 8, 128, 'A1s', scale=nm)
    load_w(k, G1, d['g1p'], 8, 256, 'G1', scale=nm)
    load_w(k, W2p, d['w2p'], 2, 1024, 'W2p')
    load_w(k, A2p, d['a2p'], 2, 1024, 'A2p')
    load_w(k, G2, d['g2p'], 2, 1024, 'G2')
    smask = ar.alloc([512], F32)
    R.dma('sp', smask, d['smask'], 'dsm', [], ['smask'])
    sm = smask[:, 0:TD]
    xh = [ar.alloc([8, TD + 2], BF16) for _ in range(2)]
    xm = [[ar.alloc([8, TD], BF16) for _ in range(6)] for _ in range(2)]
    hw = [ar.alloc([TD], BF16) for _ in range(2)]
    ha = [ar.alloc([TD], BF16) for _ in range(2)]
    hg = [ar.alloc([2, TD], BF16) for _ in range(2)]
    mtmp = ar.alloc([TD], F32)
    mxx = ar.alloc([TD], F32)
    f32n = ['rT', 'kT', 'sig0', 'sig1', 'a0', 'a1', 'tk', 'ssm', 'lnv', 'rin', 'kk', 'tK0', 'tK1', 'kd0', 'kd1',
            'b0', 'b1', 'cs', 'pre', 'cum', 'cume', 'E1', 'E2', 'E3']
    T = [{n: ar.alloc([TD], F32) for n in f32n} for _ in range(2)]
    bfn = ['vTb', 'sqk', 'cr', 'gT', 'cv', 'o_r', 'o_kk', 'o_b', 'o_k']
    B = [{n: ar.alloc([TD], BF16) for n in bfn} for _ in range(2)]
    tmj = [{n: ar.alloc([2, 128], BF16) for n in ('bh', 'kh', 'vt')} for _ in range(2)]
    pc = [ar.alloc([2], F32) for _ in range(2)]
    XH = fm(d['XH'])
    wr = lambda j, kc: [('Wrkv%d' % j, kc, 0)]
    cnt = [0]
    NSUB = TD // 128

    scnt = [0, 0]

    def bank(st=None):
        if st is None:
            cnt[0] += 1
            return cnt[0] % 6
        scnt[st] += 1
        return 3 * st + scnt[st] % 3

    def loads(i):
        t0 = i * TD
        base = 1 + t0 if t0 < cfg.TP else cfg.TP + 3 + (t0 - cfg.TP)
        R.dma('sp', xh[i % 2], XH[:, :, base - 1:base + TD + 1], 'dxh%d' % (i % 2), [], [('xh', i % 2)])

    def to_tm(R, src, skey, name, par, dst, i, dc):
        pb = 6 + par
        for n in range(NSUB):
            R.tr(k.psb[pb][:, n * 128:(n + 1) * 128], src[:, n * 128:(n + 1) * 128], k.ident_bf, [skey], [('ps', pb)])
        R.cp('dve', tmj[par][name], k.psb[pb][:, 0:TD].rearrange("p (n c) -> p n c", c=128), [('ps', pb)],
             [(name, par)])
        R.dma('sp', tm(dst)[:, NSUB * i:NSUB * i + NSUB, dc * 128:(dc + 1) * 128], tmj[par][name],
              'dt_%s%d' % (name, par), [(name, par)], [])

    def mix_kc(R, i, kc):
        b = i % 2
        xb = xh[b]
        cur = xb[:, kc, 1:TD + 1]
        R.tt('pool', mtmp, xb[:, kc, 0:TD], xb[:, kc, 2:TD + 2], ALU.add, [('xh', b)], ['mtmp'])
        R.stt(mxx, mtmp, 0.5, cur, ALU.mult, ALU.subtract, ['mtmp', ('xh', b)], ['mxx'])
        for m in range(6):
            R.stt(xm[b][m][:, kc, :], mxx, vcol(k, V_MU + m, kc), cur, ALU.mult, ALU.add,
                  ['mxx', ('xh', b)], [('xm', b, m, kc)])

    def hidden(i):
        b = i % 2
        pb = bank()
        for kc in range(8):
            R.mm(k.ps[pb][:, 0:TD], W1s[:, kc, :], xm[b][3][:, kc, :], kc == 0, kc == 7,
                 [('xm', b, 3, kc), ('W1s', kc, 0)], [('ps', pb)])
        R.act(hw[b], k.ps[pb][:, 0:TD], AF.Tanh, [('ps', pb)], [('hw', b)])
        pb = bank()
        for kc in range(8):
            R.mm(k.ps[pb][:, 0:TD], A1s[:, kc, :], xm[b][4][:, kc, :], kc == 0, kc == 7,
                 [('xm', b, 4, kc), ('A1s', kc, 0)], [('ps', pb)])
        R.cp('act', ha[b], k.ps[pb][:, 0:TD], [('ps', pb)], [('ha', b)])
        for hf in range(2):
            pb = bank()
            for kc in range(8):
                R.mm(k.ps[pb][:, 0:TD], G1[:, kc, hf * 128:(hf + 1) * 128], xm[b][5][:, kc, :], kc == 0, kc == 7,
                     [('xm', b, 5, kc), ('G1', kc, 0)], [('ps', pb)])
            R.act(hg[b][:, hf, :], k.ps[pb][:, 0:TD], AF.Sigmoid, [('ps', pb)], [('hg', b, hf)])

    def front(R, i, dc, par):
        b = i % 2
        t, bb = T[par], B[par]
        K_ = lambda n: (n, par)
        dsl = slice(dc * 128, (dc + 1) * 128)
        cols = slice(i * TD, (i + 1) * TD)
        pr, pk, pv = bank(par), bank(par), bank(par)
        for j, pp in ((0, pr), (1, pk), (2, pv)):
            for kc in range(8):
                R.mm(k.ps[pp][:, 0:TD], Wrkv[:, kc, j * 1024 + dc * 128:j * 1024 + (dc + 1) * 128], xm[b][j][:, kc, :],
                     kc == 0, kc == 7, [('xm', b, j, kc)] + wr(j, kc), [('ps', pp)])
        R.cp('act', t['rT'], k.ps[pr][:, 0:TD], [('ps', pr)], [K_('rT')])
        R.cp('act', t['kT'], k.ps[pk][:, 0:TD], [('ps', pk)], [K_('kT')])
        R.cp('act', bb['vTb'], k.ps[pv][:, 0:TD], [('ps', pv)], [K_('vTb')])
        for z in range(2):
            pw = bank(par)
            R.mm(k.ps[pw][:, 0:TD], W2p[:, z, dsl], hw[b], True, True, [('hw', b), ('W2p', z, 0)], [('ps', pw)])
            R.act(t['sig%d' % z], k.ps[pw][:, 0:TD], AF.Sigmoid, [('ps', pw)], [K_('sig%d' % z)],
                  bias=vcol(k, V_W0 + z, dc))
            pa = bank(par)
            R.mm(k.ps[pa][:, 0:TD], A2p[:, z, dsl], ha[b], True, True, [('ha', b), ('A2p', z, 0)], [('ps', pa)])
            R.act(t['a%d' % z], k.ps[pa][:, 0:TD], AF.Sigmoid, [('ps', pa)], [K_('a%d' % z)],
                  bias=vcol(k, V_A0 + z, dc))
        pg = bank(par)
        for hf in range(2):
            R.mm(k.ps[pg][:, 0:TD], G2[:, hf, dsl], hg[b][:, hf, :], hf == 0, hf == 1,
                 [('hg', b, hf), ('G2', hf, 0)], [('ps', pg)])
        R.cp('act', bb['gT'], k.ps[pg][:, 0:TD], [('ps', pg)], [K_('gT')])
        R.dma('sp', d['GT'][dsl, cols], bb['gT'], 'd_g%d' % par, [K_('gT')], [])
        R.scp('act', t['tk'], t['kT'], vcol(k, V_KK, dc), [K_('kT')], [K_('tk')])
        for z in range(2):
            R.add('act', (lambda o_, i_, s_, b_: (lambda e: e.activation(out=o_, in_=i_, func=AF.Identity,
                                                                         bias=b_, scale=s_)))(
                t['tK%d' % z], t['a%d' % z], vcol(k, V_KA, dc), vcol(k, V_OMKA, dc)),
                [K_('a%d' % z)], [K_('tK%d' % z)])

    def back(R, i, dc, par):
        t, bb = T[par], B[par]
        K_ = lambda n: (n, par)
        dsl = slice(dc * 128, (dc + 1) * 128)
        cols = slice(i * TD, (i + 1) * TD)
        R.tt('pool', bb['sqk'], t['tk'], t['tk'], ALU.mult, [K_('tk')], [K_('sqk')])
        pn = bank(par)
        R.mm(k.ps[pn][:, 0:TD], k.onesbd_bf, bb['sqk'], True, True, [K_('sqk')], [('ps', pn)])
        R.ts1('dve', t['ssm'], k.ps[pn][:, 0:TD], 1e-24, ALU.max, [('ps', pn)], [K_('ssm')])
        R.act(t['lnv'], t['ssm'], AF.Ln, [K_('ssm')], [K_('lnv')])
        R.act(t['rin'], t['lnv'], AF.Exp, [K_('lnv')], [K_('rin')], scale=-0.5)
        R.tt('pool', t['kk'], t['tk'], t['rin'], ALU.mult, [K_('tk'), K_('rin')], [K_('kk')])
        for z in range(2):
            R.tt('pool', t['kd%d' % z], t['kT'], t['tK%d' % z], ALU.mult, [K_('kT'), K_('tK%d' % z)], [K_('kd%d' % z)])
            R.tt('pool', t['b%d' % z], t['kk'], t['a%d' % z], ALU.mult, [K_('kk'), K_('a%d' % z)], [K_('b%d' % z)])
        R.tt('pool', t['cs'], t['kd0'], t['kd1'], ALU.add, [K_('kd0'), K_('kd1')], [K_('cs')])
        R.stt(bb['cr'], t['cs'], vcol(k, V_RK, dc), t['rT'], ALU.mult, ALU.mult, [K_('cs'), K_('rT')], [K_('cr')])
        pc_ = bank(par)
        R.mm(k.ps[pc_][:, 0:TD], k.onesbd_bf, bb['cr'], True, True, [K_('cr')], [('ps', pc_)])
        R.tt('dve', bb['cv'], k.ps[pc_][:, 0:TD], bb['vTb'], ALU.mult, [('ps', pc_), K_('vTb')], [K_('cv')])
        R.dma('sp', d['CVT'][dsl, cols], bb['cv'], 'd_cv%d' % par, [K_('cv')], [])
        to_tm(R, bb['vTb'], K_('vTb'), 'vt', par, d['VT'], i, dc)
        for z in range(2):
            sg_ = t['sig%d' % z]
            sk = K_('sig%d' % z)
            cum3 = t['cum'].rearrange("p (n t) -> p n t", t=128)
            if z == 0:
                R.scan(t['cum'], sm, sg_, [sk, 'smask'], [K_('cum')])
                tot = cum3[:, :, 127:128]
            else:
                R.scan(t['pre'], sm, sg_, [sk, 'smask'], [K_('pre')])
                pre3 = t['pre'].rearrange("p (n t) -> p n t", t=128)
                R.tt('dve', t['cum'], sg_, t['pre'], ALU.subtract, [sk, K_('pre')], [K_('cum')])
                R.tt('dve', cum3, cum3, pre3[:, :, 127:128].to_broadcast([128, NSUB, 128]), ALU.add,
                     [K_('cum'), K_('pre')], [K_('cum')])
                tot = cum3[:, :, 0:1]
            R.tt('pool', t['cume'], t['cum'], sg_, ALU.subtract, [K_('cum'), sk], [K_('cume')])
            R.act(t['E1'], t['cum'], AF.Exp, [K_('cum')], [K_('E1')], scale=-CDEC)
            R.act(t['E2'], t['cume'], AF.Exp, [K_('cume')], [K_('E2')], scale=-CDEC)
            R.act(t['E3'], t['cum'], AF.Exp, [K_('cum')], [K_('E3')], scale=CDEC)
            R.act(pc[par].unsqueeze(2), tot, AF.Exp, [K_('cum')], [K_('pc')], scale=-CDEC)
            R.dma('sp', d['PC%d' % z][dsl, i * NSUB:i * NSUB + NSUB], pc[par], 'd_pc%d' % par, [K_('pc')], [])
            R.tt('pool', bb['o_r'], t['rT'], t['E1'], ALU.mult, [K_('rT'), K_('E1')], [K_('o_r')])
            R.dma('sp', d['RT%d' % z][dsl, cols], bb['o_r'], 'd_r%d' % par, [K_('o_r')], [])
            R.tt('pool', bb['o_kk'], t['kk'], t['E2'], ALU.mult, [K_('kk'), K_('E2')], [K_('o_kk')])
            R.dma('sp', d['KKT%d' % z][dsl, cols], bb['o_kk'], 'd_kk%d' % par, [K_('o_kk')], [])
            R.tt('dve', bb['o_b'], t['b%d' % z], t['E3'], ALU.mult, [K_('b%d' % z), K_('E3')], [K_('o_b')])
            R.dma('sp', d['BT%d' % z][dsl, cols], bb['o_b'], 'd_b%d' % par, [K_('o_b')], [])
            to_tm(R, bb['o_b'], K_('o_b'), 'bh', par, d['BH%d' % z], i, dc)
            R.tt('dve', bb['o_k'], t['kd%d' % z], t['E3'], ALU.mult, [K_('kd%d' % z), K_('E3')], [K_('o_k')])
            R.dma('sp', d['KT2%d' % z][dsl, cols], bb['o_k'], 'd_k%d' % par, [K_('o_k')], [])
            to_tm(R, bb['o_k'], K_('o_k'), 'kh', par, d['KH%d' % z], i, dc)

    RD = Defer()

    def loads_d(i):
        t0 = i * TD
        base = 1 + t0 if t0 < cfg.TP else cfg.TP + 3 + (t0 - cfg.TP)
        RD.dma('sp', xh[i % 2], XH[:, :, base - 1:base + TD + 1], 'dxh%d' % (i % 2), [], [('xh', i % 2)])

    def hidden_d(i):
        nonlocal R
        R_save = R
        R = RD
        try:
            hidden(i)
        finally:
            R = R_save

    loads_d(0)
    for i in range(NTD):
        if i + 1 < NTD:
            loads_d(i + 1)
        for kc in range(8):
            mix_kc(RD, i, kc)
        hidden_d(i)
        for dc in range(8):
            front(RD, i, dc, dc % 2)
            back(RD, i, dc, dc % 2)
    list_schedule(R, RD.q)
    R.barrier()


def phase_E(k):
    R, ar, cfg, d = k.R, k.ar, k.cfg, k.d
    ar.reset()
    masks = ar.alloc([7, 512], BF16)
    NS = 3
    P = [dict(kkP=ar.alloc([8, 128], BF16), rP=ar.alloc([8, 128], BF16), bP=ar.alloc([8, 128], BF16),
              kP=ar.alloc([8, 128], BF16), KR=ar.alloc([8, 4, 128], BF16), bB=ar.alloc([8, 2, 128], BF16),
              bh=ar.alloc([1024], BF16), kh=ar.alloc([1024], BF16), v=ar.alloc([1024], BF16),
              pc=ar.alloc([8], F32)) for _ in range(NS)]
    ATb = [ar.alloc([8, 512], BF16) for _ in range(2)]
    ATk = [ar.alloc([8, 512], BF16) for _ in range(2)]
    Mb = [[[ar.alloc([4, 128], BF16) for _ in range(2)] for _ in range(4)] for _ in range(2)]
    Nb = [[[ar.alloc([4, 128], BF16) for _ in range(2)] for _ in range(4)] for _ in range(2)]
    Zb = [[[ar.alloc([4, 128], BF16) for _ in range(2)] for _ in range(4)] for _ in range(2)]
    Hf = ar.alloc([8, 128], F32)
    Hb = ar.alloc([8, 128], BF16)
    Ht = ar.alloc([8, 64], F32)
    Gn = ar.alloc([8, 128], BF16)
    Ub = ar.alloc([8, 128], BF16)
    Ys = [ar.alloc([1024], F32) for _ in range(2)]
    xg = ar.alloc([2, 1024], F32)
    xgf = xg.rearrange("p a b -> p (a b)")
    for r0 in (0, 4):
        n_ = min(4, 7 - r0)
        stv = xgf[:, 0:n_ * 512].rearrange("p (a b) -> p a b", b=512)
        R.dma('sp', stv, d['masks'][:, r0:r0 + n_, :], 'em', [], ['x2g'])
        R.cp('dve', masks[:, r0:r0 + n_, :], stv, ['x2g'], ['masks'])
    for s_ in range(NS):
        R.memset('dve', P[s_]['KR'], 0.0, [('KR', s_)])
        R.memset('dve', P[s_]['bB'], 0.0, [('bB', s_)])
    cnt = [0]
    cur = [[0, 0, 0, 0], [0, 0, 0, 0]]
    fl = lambda a: a.rearrange("p a b -> p (a b)")

    def bank():
        cnt[0] += 1
        return cnt[0] % 8

    def loads(z, tok0, s_):
        p = P[s_]
        c = slice(tok0, tok0 + 128)
        sfx = '%d' % s_
        R.dma('sp', p['kkP'], fm(d['KKT%d' % z])[:, :, c], 'e_kk' + sfx, [], [('kkP', s_)])
        R.dma('sp', p['bP'], fm(d['BT%d' % z])[:, :, c], 'e_b' + sfx, [], [('bP', s_)])
        R.dma('sp', p['kP'], fm(d['KT2%d' % z])[:, :, c], 'e_k' + sfx, [], [('kP', s_)])
        for hh in range(2):
            rows = slice(hh * 64, hh * 64 + 64)
            R.dma('sp', p['KR'][rows, :, hh, :], fm(d['KKT%d' % z])[rows, :, c], 'e_KR' + sfx, [], [('KR', s_)])
            R.dma('sp', p['KR'][rows, :, 2 + hh, :], fm(d['RT%d' % z])[rows, :, c], 'e_KR' + sfx, [], [('KR', s_)])
            R.dma('sp', p['bB'][rows, :, hh, :], fm(d['BT%d' % z])[rows, :, c], 'e_bB' + sfx, [], [('bB', s_)])
        R.dma('sp', p['rP'], fm(d['RT%d' % z])[:, :, c], 'e_r' + sfx, [], [('rP', s_)])
        R.dma('sp', p['bh'], d['BH%d' % z][c, :], 'e_bh' + sfx, [], [('bh', s_)])
        R.dma('sp', p['kh'], d['KH%d' % z][c, :], 'e_kh' + sfx, [], [('kh', s_)])
        R.dma('sp', p['v'], d['VT'][c, :], 'e_v' + sfx, [], [('v', s_)])
        n = tok0 // 128
        R.dma('sp', p['pc'].unsqueeze(2), fm(d['PC%d' % z])[:, :, n:n + 1], 'e_pc' + sfx, [], [('pc', s_)], slow=True)

    def pre_stages(z, s_, par):
        p = P[s_]
        atb, atk = ATb[par], ATk[par]
        M_, N_, Z_ = Mb[par], Nb[par], Zb[par]
        cu = cur[par]
        mB, mK, mM, mI = masks[:, 3 * z, :], masks[:, 3 * z + 1, :], masks[:, 3 * z + 2, :], masks[:, 6, :]

        def level(lev):
            info = []
            for g in range(4):
                c0, c1 = cu[g], 1 - cu[g]
                Mp, Np = M_[g][c0], N_[g][c0]
                bn = None
                if lev < 6:
                    bn = bank()
                    for x in range(4):
                        R.mm(k.ps[bn][:, x * 128:(x + 1) * 128], Mp[:, x, :], Np[:, x, :], True, True,
                             [('M', par, g, c0), ('N', par, g, c0)], [('ps', bn)])
                bm = bank()
                for x in range(4):
                    R.mm(k.ps[bm][:, x * 128:(x + 1) * 128], Np[:, x, :], Mp[:, x, :], True, True,
                         [('M', par, g, c0), ('N', par, g, c0)], [('ps', bm)])
                info.append((c0, c1, bn, bm))
            for g in range(4):
                c0, c1, bn, bm = info[g]
                R.cp('act', fl(M_[g][c1]), k.ps[bm], [('ps', bm)], [('M', par, g, c1)])
                if bn is not None:
                    R.cp('dve' if g == 0 else 'act', fl(N_[g][c1]), k.ps[bn], [('ps', bn)], [('N', par, g, c1)])
            bzs = []
            for g in range(4):
                c0, c1, bn, bm = info[g]
                bz = bank()
                bzs.append(bz)
                for x in range(4):
                    R.mm(k.ps[bz][:, x * 128:(x + 1) * 128], M_[g][c1][:, x, :], Z_[g][c0][:, x, :], True, True,
                         [('M', par, g, c1), ('Z', par, g, c0)], [('ps', bz)])
            for g in range(4):
                c0, c1, bn, bm = info[g]
                R.tt('dve', fl(Z_[g][c1]), k.ps[bzs[g]], fl(Z_[g][c0]), ALU.add,
                     [('ps', bzs[g]), ('Z', par, g, c0)], [('Z', par, g, c1)])
                cu[g] = c1

        def stA():
            for dc in range(8):
                rhs = fl(p['KR'][:, dc, :, :])
                b1 = bank()
                R.mm(k.ps[b1], p['bP'][:, dc, :], rhs, True, True, [('bP', s_), ('KR', s_)], [('ps', b1)])
                b2 = bank()
                R.mm(k.ps[b2], p['kP'][:, dc, :], rhs, True, True, [('kP', s_), ('KR', s_)], [('ps', b2)])
                R.tt('dve', atb[:, dc, :], k.ps[b1], mB, ALU.mult, [('ps', b1), 'masks'], [('ATb', par, dc)])
                R.tt('dve', atk[:, dc, :], k.ps[b2], mK, ALU.mult, [('ps', b2), 'masks'], [('ATk', par, dc)])
            for g in range(4):
                cu[g] = 0
                b1 = bank()
                for dd in range(2):
                    dc = 2 * g + dd
                    R.mm(k.ps[b1][:, dd * 256:(dd + 1) * 256], p['kkP'][:, dc, :], fl(p['bB'][:, dc, :, :]), True, True,
                         [('kkP', s_), ('bB', s_)], [('ps', b1)])
                R.tt('dve', fl(M_[g][0]), k.ps[b1], mM, ALU.mult, [('ps', b1), 'masks'], [('M', par, g, 0)])
                n0 = atb[:, 2 * g:2 * g + 2, 0:256]
                nk_ = [('ATb', par, 2 * g), ('ATb', par, 2 * g + 1)]
                R.cp('pool', N_[g][0].rearrange("p (a c) b -> p a (c b)", c=2), n0, nk_, [('N', par, g, 0)])
                R.tt('pool', Z_[g][0].rearrange("p (a c) b -> p a (c b)", c=2), n0,
                     mI.rearrange("p (a b) -> p a b", b=256), ALU.add, nk_ + ['masks'], [('Z', par, g, 0)])
            level(1)
            level(2)

        def stB():
            level(3)

        def stC():
            level(4)

        def stD():
            level(5)
            level(6)

        return [stA, stB, stC, stD]

    step = [0]

    def seq_stages(z, tok0, s_, par):
        p = P[s_]
        atb, atk = ATb[par], ATk[par]
        TT = lambda h: Zb[par][h // 4][cur[par][h // 4]][:, h % 4, :]
        tkey = lambda h: ('Z', par, h // 4, cur[par][h // 4])
        st = {}

        def s1():
            for half in range(2):
                dcs = range(4 * half, 4 * half + 4)
                bg = bank()
                for dl, dc in enumerate(dcs):
                    o = dl * 128
                    R.mm(k.ps[bg][:, o:o + 128], p['kkP'][:, dc, :], Hb[:, dc, :], True, False,
                         [('kkP', s_), 'Hb'], [('ps', bg)])
                    for hh in range(2):
                        R.mm(k.ps[bg][:, o + hh * 64:o + hh * 64 + 64], atk[:, dc, hh * 128:(hh + 1) * 128],
                             p['v'][:, dc * 128 + hh * 64:dc * 128 + hh * 64 + 64], False, hh == 1,
                             [('ATk', par, dc), ('v', s_)], [('ps', bg)])
                R.act(fl(Gn[:, 4 * half:4 * half + 4, :]), k.ps[bg], AF.Copy, [('ps', bg)], [('Gn', half)], scale=-1.0)

        def s2():
            for half in range(2):
                dcs = range(4 * half, 4 * half + 4)
                bu = bank()
                for dl, dc in enumerate(dcs):
                    for hh in range(2):
                        h = 2 * dc + hh
                        o = dl * 128 + hh * 64
                        R.mm(k.ps[bu][:, o:o + 64], TT(h), Gn[:, dc, hh * 64:hh * 64 + 64], True, True,
                             [tkey(h), ('Gn', half)], [('ps', bu)])
                R.cp('dve', fl(Ub[:, 4 * half:4 * half + 4, :]), k.ps[bu], [('ps', bu)], [('Ub', half)])

        def s3():
            ys = Ys[step[0] % 2]
            yk = ('Ys', step[0] % 2)
            ysem = 'e_y%d' % (step[0] % 2)
            step[0] += 1
            hbanks = []
            for half in range(2):
                dcs = range(4 * half, 4 * half + 4)
                by = bank()
                for dl, dc in enumerate(dcs):
                    o = dl * 128
                    R.mm(k.ps[by][:, o:o + 128], p['rP'][:, dc, :], Hb[:, dc, :], True, False,
                         [('rP', s_), 'Hb'], [('ps', by)])
                    for hh in range(2):
                        oo = o + hh * 64
                        R.mm(k.ps[by][:, oo:oo + 64], atb[:, dc, 256 + hh * 128:256 + (hh + 1) * 128],
                             Ub[:, dc, hh * 64:hh * 64 + 64], False, False, [('ATb', par, dc), ('Ub', half)],
                             [('ps', by)])
                        R.mm(k.ps[by][:, oo:oo + 64], atk[:, dc, 256 + hh * 128:256 + (hh + 1) * 128],
                             p['v'][:, dc * 128 + hh * 64:dc * 128 + hh * 64 + 64], False, hh == 1,
                             [('ATk', par, dc), ('v', s_)], [('ps', by)])
                R.cp('act', ys[:, half * 512:(half + 1) * 512], k.ps[by], [('ps', by)], [yk])
                bh_ = bank()
                hbanks.append(bh_)
                for dl, dc in enumerate(dcs):
                    o = dl * 128
                    R.mm(k.ps[bh_][:, o:o + 128], p['bh'][:, dc * 128:(dc + 1) * 128], Ub[:, dc, :], True, False,
                         [('bh', s_), ('Ub', half)], [('ps', bh_)])
                    R.mm(k.ps[bh_][:, o:o + 128], p['kh'][:, dc * 128:(dc + 1) * 128],
                         p['v'][:, dc * 128:(dc + 1) * 128], False, True, [('kh', s_), ('v', s_)], [('ps', bh_)])
            for half in range(2):
                ps3 = k.ps[hbanks[half]].rearrange("p (a b) -> p a b", b=128)
                for hh in range(2):
                    rows = slice(hh * 64, hh * 64 + 64)
                    csl = slice(hh * 64, hh * 64 + 64)
                    hblk = Hf[rows, 4 * half:4 * half + 4, csl]
                    tblk = Ht[rows, 4 * half:4 * half + 4, :]
                    R.tt('dve', tblk, hblk, ps3[rows, :, csl], ALU.add, [('Hf', half, hh), ('ps', hbanks[half])],
                         [('Ht', half, hh)])
                    R.tt('pool', hblk, tblk,
                         p['pc'][rows, 4 * half:4 * half + 4].unsqueeze(2).to_broadcast([64, 4, 64]),
                         ALU.mult, [('Ht', half, hh), ('pc', s_)], [('Hf', half, hh)])
            hk = [('Hf', a, b) for a in range(2) for b in range(2)]
            R.cp('act', Hb[0:64, :, 0:64], Hf[0:64, :, 0:64], hk, ['Hb'])
            R.cp('act', Hb[64:128, :, 64:128], Hf[64:128, :, 64:128], hk, ['Hb'])
            R.dma('pool', d['Y%d' % z][tok0:tok0 + 128, :], ys, ysem, [yk], [])

        return [s1, s2, s3]

    slot = [0]
    parc = [0]
    HK = [('Hf', a, b) for a in range(2) for b in range(2)]

    def run(seq, z, init):
        nonlocal R
        t0, T_ = (0, cfg.TP) if seq == 0 else (cfg.TP, cfg.TS)
        nch = T_ // 128
        order = list(range(nch)) if z == 0 else list(range(nch - 1, -1, -1))
        toks = [t0 + n * 128 for n in order]
        if init is None:
            R.memset('dve', Hf, 0.0, HK)
            R.memset('pool', Hb, 0.0, ['Hb'])
        else:
            init()
        s0 = slot[0]
        p0 = parc[0]
        slot[0] += nch
        parc[0] += nch
        R_real = R
        R = Defer()
        try:
            emit_run(z, toks, nch, s0, p0)
        finally:
            ops_ = R.q
            R = R_real
        list_schedule(R, ops_)

    def emit_run(z, toks, nch, s0, p0):
        loads(z, toks[0], s0 % NS)
        if nch > 1:
            loads(z, toks[1], (s0 + 1) % NS)
        for stg_ in pre_stages(z, s0 % NS, p0 % 2):
            stg_()
        for ci in range(nch):
            if ci + 2 < nch:
                loads(z, toks[ci + 2], (s0 + ci + 2) % NS)
            sq_ = seq_stages(z, toks[ci], (s0 + ci) % NS, (p0 + ci) % 2)
            if ci + 1 < nch:
                nx = pre_stages(z, (s0 + ci + 1) % NS, (p0 + ci + 1) % 2)
            else:
                nx = [lambda: None] * 4
            nx[0]()
            sq_[0]()
            nx[1]()
            sq_[1]()
            nx[2]()
            sq_[2]()
            nx[3]()

    def publish():
        R.dma('pool', d['ex2_in'], fl(Hf), 'x2p', HK, ['ex2_in'])
        add_cc(R, lambda e: e.collective_compute("AllGather", ALU.bypass, replica_groups=PAIRS,
                                                 ins=[d['ex2_in']], outs=[d['ex2_out']]), ['ex2_in'], ['ex2o'], 'cc2')

    def init_from_partner():
        R.dma('sp', xg, d['ex2_out'].rearrange("(s p) c -> p s c", p=128), 'x2g', ['ex2o'], ['x2g'])
        hfl = fl(Hf)
        R.scp('dve', hfl, xg[:, 0, :], k.sel[:, 0:1], ['x2g'], HK)
        R.stt(hfl, xg[:, 1, :], k.sel[:, 1:2], hfl, ALU.mult, ALU.add, ['x2g'] + HK, HK)
        R.cp('act', Hb, Hf, HK, ['Hb'])

    run(1, 0, None)
    publish()
    run(0, 0, None)
    run(0, 1, None)
    run(1, 1, init_from_partner)
    R.barrier()


def phase_F(k):
    R, ar, cfg, d = k.R, k.ar, k.cfg, k.d
    ar.reset()
    Wo = ar.alloc([8, 1024], BF16)
    k.stg = [ar.alloc([1024], F32) for _ in range(4)]
    K.stgw = k.stgw = 1024
    load_w(k, Wo, d['w_out'], 8, 1024, 'WoR')
    y0s = [ar.alloc([4, 1024], F32) for _ in range(2)]
    y1s = [ar.alloc([4, 1024], F32) for _ in range(2)]
    ynb = ar.alloc([4, 1024], BF16)
    cvs = [ar.alloc([8, 512], BF16) for _ in range(2)]
    gts = [ar.alloc([8, 512], BF16) for _ in range(2)]
    zT = ar.alloc([8, 512], BF16)
    xTs = [ar.alloc([8, 512], F32) for _ in range(2)]
    s1 = ar.alloc([64], F32)
    s2 = ar.alloc([64], F32)
    mean = ar.alloc([64], F32)
    msq = ar.alloc([64], F32)
    var = ar.alloc([64], F32)
    rstd = ar.alloc([64], F32)
    yb = [ar.alloc([512], F32) for _ in range(2)]
    XT = fm(d['XT'])
    def floads(i):
        b = i % 2
        cols = slice(i * 512, (i + 1) * 512)
        R.dma('sp', y0s[b], tm(d['Y0'])[:, 4 * i:4 * i + 4, :], 'f_y0%d' % b, [], [('y0', b)])
        R.dma('sp', y1s[b], tm(d['Y1'])[:, 4 * i:4 * i + 4, :], 'f_y1%d' % b, [], [('y1', b)])
        R.dma('sp', cvs[b], fm(d['CVT'])[:, :, cols], 'f_cv%d' % b, [], [('cv', b)])
        R.dma('sp', gts[b], fm(d['GT'])[:, :, cols], 'f_g%d' % b, [], [('gt', b)])
        R.dma('sp', xTs[b], XT[:, :, cols], 'f_x%d' % b, [], [('xT', b, dc) for dc in range(8)])

    R_real = R
    R = k.R = Defer()
    floads(0)
    for i in range(cfg.NT_OWN):
        cols = slice(i * 512, (i + 1) * 512)
        pb_ = i % 2
        if i + 1 < cfg.NT_OWN:
            floads(i + 1)
        y0, y1, cv, gt, xT = y0s[pb_], y1s[pb_], cvs[pb_], gts[pb_], xTs[pb_]
        y0f = y0.rearrange("p a b -> p (a b)")
        y1f = y1.rearrange("p a b -> p (a b)")
        R.tt('pool', y0f, y0f, y1f, ALU.add, [('y0', pb_), ('y1', pb_)], [('y0', pb_)])
        y3 = y0.rearrange("p a (h n) -> p (a h) n", n=64)
        q3 = y1.rearrange("p a (h n) -> p (a h) n", n=64)
        R.red(s1, y3, ALU.add, [('y0', pb_)], ['s1'])
        R.tt('pool', y1f, y0f, y0f, ALU.mult, [('y0', pb_), ('y1', pb_)], [('y1', pb_)])
        R.red(s2, q3, ALU.add, [('y1', pb_)], ['s2'])
        R.ts1('dve', mean, s1, 1.0 / 64, ALU.mult, ['s1'], ['mean'])
        R.tt('dve', msq, mean, mean, ALU.mult, ['mean'], ['msq'])
        R.stt(var, s2, 1.0 / 64, msq, ALU.mult, ALU.subtract, ['s2', 'msq'], ['var'])
        R.act(var, var, AF.Sqrt, ['var'], ['var'], bias=64e-5, scale=1.0)
        R.recip(rstd, var, ['var'], ['rstd'])
        R.tt('dve', y3, y3, mean.unsqueeze(2).to_broadcast([128, 64, 64]), ALU.subtract, [('y0', pb_), 'mean'], [('y0', pb_)])
        R.tt('dve', ynb.rearrange("p a (h n) -> p (a h) n", n=64), y3,
             rstd.unsqueeze(2).to_broadcast([128, 64, 64]), ALU.mult, [('y0', pb_), 'rstd'], ['ynb'])
        for dc in range(8):
            pb = dc % 2
            for n in range(4):
                R.tr(k.psb[pb][:, n * 128:(n + 1) * 128], ynb[:, n, dc * 128:(dc + 1) * 128], k.ident_bf,
                     ['ynb'], [('ps', pb)])
            R.act(yb[dc % 2], k.psb[pb][:, 0:512], AF.Identity, [('ps', pb)], [('yb', dc % 2)],
                  bias=vcol(k, V_LNB, dc), scale=vcol(k, V_LNW, dc))
            R.tt('dve', yb[dc % 2], yb[dc % 2], cv[:, dc, :], ALU.add, [('yb', dc % 2), ('cv', pb_)], [('yb', dc % 2)])
            R.tt('pool', zT[:, dc, :], yb[dc % 2], gt[:, dc, :], ALU.mult, [('yb', dc % 2), ('gt', pb_)], [('zT', dc)])
        for dco in range(8):
            pb = 2 + dco % 4
            for dc in range(8):
                R.mm(k.ps[pb], Wo[:, dc, dco * 128:(dco + 1) * 128], zT[:, dc, :], dc == 0, dc == 7,
                     [('zT', dc), ('WoR', dc, 0)], [('ps', pb)])
            R.tt('dve', xT[:, dco, :], k.ps[pb], xT[:, dco, :], ALU.add, [('ps', pb), ('xT', pb_, dco)], [('xT', pb_, dco)])
        R.dma('pool', XT[:, :, cols], xT, 'f_x%d' % pb_, [('xT', pb_, dc) for dc in range(8)], [])
    ops_ = R.q
    R = k.R = R_real
    list_schedule(R, ops_)
    R.barrier()


def EXTRA_PHASES(k):
    return [('X1', lambda: phase_X1(k)), ('D', lambda: phase_D(k)), ('E', lambda: phase_E(k)),
            ('F', lambda: phase_F(k)), ('G', lambda: phase_FFN(k, 1))]


def build(cfg, upto='all', dbg=()):
    nc = bass.Bass("TRN2", target_bir_lowering=False)
    k = K()
    k.cfg = cfg
    k.nc = nc
    d = {}

    def inp(name, shape, dt=F32):
        d[name] = nc.dram_tensor(name, list(shape), dt, kind="ExternalInput").ap()

    def scr(name, shape, dt):
        kind = "ExternalOutput" if name in dbg else "Internal"
        d[name] = nc.dram_tensor(name, list(shape), dt, kind=kind).ap()

    inp('x_in', [cfg.NALL, D])
    inp('ropeC', [128, cfg.NALL])
    inp('ropeS', [128, cfg.NALL])
    inp('cmat', [128, 3, 128])
    inp('vecs', [128, NV - 1, 8])
    inp('smask', [128, 512])
    inp('masks', [128, 7, 512])
    inp('sel', [128, 2])
    inp('sublnw', [128, 1])
    inp('lamt', [64, 4])
    inp('w_qkv', [D, 3072])
    inp('w_qks', [D, 2048])
    inp('w_o', [D, D])
    inp('w_ffn_in', [2, D, 2 * FH])
    inp('w_ffn_out', [2, FH, D])
    inp('w_rkv', [3, D, D])
    inp('w1s', [D, 128])
    inp('w2p', [256, D])
    inp('a1s', [D, 128])
    inp('a2p', [256, D])
    inp('g1p', [D, 256])
    inp('g2p', [256, D])
    inp('w_out', [D, D])
    d['y_out'] = nc.dram_tensor('y_out', [cfg.NOWN, D], F32, kind="ExternalOutput").ap()
    scr('XT', [D, cfg.NOWN], F32)
    scr('QT', [D, cfg.NOWN], BF16)
    scr('KT', [D, cfg.NALL], BF16)
    scr('VV', [cfg.NALL, D], BF16)
    scr('OT', [D, cfg.NOWN], BF16)
    scr('XH', [D, cfg.XHW], BF16)
    NCH = cfg.NOWN // 128
    for z in range(2):
        for nm_ in ('RT', 'KKT', 'BT', 'KT2'):
            scr('%s%d' % (nm_, z), [D, cfg.NOWN], BF16)
        for nm_ in ('BH', 'KH'):
            scr('%s%d' % (nm_, z), [cfg.NOWN, D], BF16)
        scr('PC%d' % z, [D, NCH], F32)
        scr('Y%d' % z, [cfg.NOWN, D], F32)
    scr('VT', [cfg.NOWN, D], BF16)
    scr('GT', [D, cfg.NOWN], BF16)
    scr('CVT', [D, cfg.NOWN], BF16)
    d['ex2_in'] = nc.dram_tensor('ex2_in', [128, 1024], F32, kind="Internal").ap()
    d['ex2_out'] = nc.dram_tensor('ex2_out', [256, 1024], F32, kind="Internal").ap()
    d['ex1_in'] = nc.dram_tensor('ex1_in', [128, 8], F32, kind="Internal").ap()
    d['ex1_out'] = nc.dram_tensor('ex1_out', [256, 8], F32, kind="Internal").ap()
    k.d = d

    with ExitStack() as stack:
        sb = nc.alloc_sbuf_tensor("sball", [128, 206 * 1024 // 4], F32)
        k.ar = Arena(sb, 206 * 1024)
        k.pd = [nc.alloc_psum_tensor("pd%d" % i, [128, 1024], F32)[:, :] for i in range(4)]
        k.ps = [k.pd[i // 2][:, (i % 2) * 512:(i % 2) * 512 + 512] for i in range(8)]
        k.psb = [p.bitcast(BF16) for p in k.ps]
        R = Rec(nc)
        k.R = R
        phases = [('consts', lambda: phase_consts(k)), ('A', lambda: phase_A(k)), ('B', lambda: phase_B(k)),
                  ('C1', lambda: phase_C1(k)), ('C2', lambda: phase_FFN(k, 0))]
        extra = globals().get('EXTRA_PHASES')
        if extra:
            phases += extra(k)
        for name, fn in phases:
            fn()
            if name == upto:
                break
        R.barrier()
        sems = {}
        for i, ref in enumerate(sorted(R.semrefs, key=str)):
            sems[ref] = stack.enter_context(nc.semaphore("s%d" % i))
        k.n_instr = {e: len(R.lists[e]) for e in ENGS}
        block = stack.enter_context(nc.Block())

        def emit(eng):
            def body(e):
                for it in R.lists[eng]:
                    if it[0] == 'w':
                        e.wait_ge(sems[it[1]], it[2])
                    else:
                        ins = it[1](e)
                        if it[3] == 1:
                            ins.then_inc(sems[it[2]], 1)
                        elif it[3] == 16:
                            ins.then_inc(sems[it[2]], 16)
                        else:
                            ins.then_inc(sems[it[2]])
            return body

        block.tensor(emit('pe'))
        block.scalar(emit('act'))
        block.vector(emit('dve'))
        block.gpsimd(emit('pool'))
        block.sync(emit('sp'))
    return nc, k


def _fmvec(v):
    return np.ascontiguousarray(np.asarray(v, np.float32).reshape(8, 128).T)


def _tri_masks():
    s = np.arange(128)[:, None]
    t = np.arange(128)[None, :]
    su = (s < t).astype(np.float32)
    iu = (s <= t).astype(np.float32)
    sl = (s > t).astype(np.float32)
    il = (s >= t).astype(np.float32)
    I = np.eye(128, dtype=np.float32)
    m = np.zeros((128, 7, 512), np.float32)
    m[:, 0] = np.concatenate([-su, -su, iu, iu], 1)
    m[:, 1] = np.concatenate([su, su, iu, iu], 1)
    m[:, 2] = np.concatenate([-sl] * 4, 1)
    m[:, 3] = np.concatenate([-sl, -sl, il, il], 1)
    m[:, 4] = np.concatenate([sl, sl, il, il], 1)
    m[:, 5] = np.concatenate([-su] * 4, 1)
    m[:, 6] = np.concatenate([I] * 4, 1)
    return m


def _padz(a, b):
    o = np.zeros((256, D), np.float32)
    o[0:64] = a
    o[128 + 64:256] = b
    return o


def prep_inputs(inp, cfg, ncores=8):
    f32 = np.float32
    TP, TS = cfg.TP, cfg.TS
    inv_freq = (f32(ROPE_THETA) ** (-np.arange(0, 16, 2, dtype=f32) / f32(16))).astype(f32)
    wqkv = np.asarray(inp['attn_w_qkv'][0], f32)
    wqks = np.zeros((D, 2048), f32)
    for which in range(2):
        for blk in range(16):
            b0 = which * 1024 + blk * 64
            wqks[:, b0:b0 + 8] = wqkv[:, b0 + 8:b0 + 16]
            wqks[:, b0 + 8:b0 + 16] = wqkv[:, b0:b0 + 8]
    cm = np.zeros((128, 3, 128), f32)
    cm[:, 0] = np.eye(128, dtype=f32)
    cm[:, 1] = 1.0
    cm[0:64, 2, 0:64] = 1.0
    cm[64:128, 2, 64:128] = 1.0
    smask = np.ones((128, 512), f32)
    smask[:, 0::128] = 0.0
    masks = _tri_masks()
    lamt = np.stack([np.asarray(inp['attn_lambda_q1'][0], f32), np.asarray(inp['attn_lambda_k1'][0], f32),
                     np.asarray(inp['attn_lambda_q2'][0], f32), np.asarray(inp['attn_lambda_k2'][0], f32)], 1)
    sublnw = np.asarray(inp['attn_subln_w'][0], f32).reshape(128, 1)
    g1p = np.zeros((D, 256), f32)
    g1p[:, :160] = np.asarray(inp['rwkv_g1'][0], f32)
    g2p = np.zeros((256, D), f32)
    g2p[:160] = np.asarray(inp['rwkv_g2'][0], f32)
    shared = {
        'cmat': cm, 'smask': smask, 'masks': masks, 'lamt': np.ascontiguousarray(lamt), 'sublnw': sublnw,
        'w_qkv': wqkv, 'w_qks': wqks, 'w_o': np.asarray(inp['attn_w_o'][0], f32),
        'w_ffn_in': np.asarray(inp['ffn_w_in'], f32), 'w_ffn_out': np.asarray(inp['ffn_w_out'], f32),
        'w_rkv': np.asarray(inp['rwkv_w_rkv'][0], f32),
        'g1p': g1p, 'g2p': g2p,
        'w_out': np.asarray(inp['rwkv_w_out'][0], f32),
    }
    maps = []
    for c in range(ncores):
        s, q = c // 2, c % 2
        xp = np.asarray(inp['x_prompt'][c], f32)
        xs = np.asarray(inp['x_sample'][s], f32)
        pos_p = np.arange(TP)
        if q == 0:
            xo, xf = xs[:TS], xs[TS:]
            pos_o, pos_f = np.arange(TS), np.arange(TS, 2 * TS)
        else:
            xp = xp[::-1]
            pos_p = pos_p[::-1]
            xo, xf = xs[TS:][::-1], xs[:TS]
            pos_o, pos_f = np.arange(TS, 2 * TS)[::-1], np.arange(TS)
        x_in = np.ascontiguousarray(np.concatenate([xp, xo, xf], 0))
        pos = np.concatenate([pos_p, pos_o, pos_f]).astype(f32)
        ang = (pos[:, None] * inv_freq[None, :]).astype(f32)
        cos, sin = np.cos(ang).astype(f32), np.sin(ang).astype(f32)
        C = np.ones((128, cfg.NALL), f32)
        S = np.zeros((128, cfg.NALL), f32)
        for half in range(2):
            o = half * 64
            C[o:o + 8] = cos.T
            C[o + 8:o + 16] = cos.T
            S[o:o + 8] = -sin.T
            S[o + 8:o + 16] = sin.T
        p1, p2 = q, 1 - q
        vl = [inp['norm_mix'][0], inp['norm_mix'][1], inp['norm_ffn'][0], inp['norm_ffn'][1], inp['norm_final']]
        vl += [inp['rwkv_mu'][0][m] for m in range(6)]
        vl += [inp['rwkv_w0'][0][p1], inp['rwkv_w0'][0][p2], inp['rwkv_a0'][0][p1], inp['rwkv_a0'][0][p2]]
        vl += [inp['rwkv_k_k'][0], inp['rwkv_k_a'][0], inp['rwkv_r_k'][0], inp['rwkv_ln_w'][0], inp['rwkv_ln_b'][0]]
        vecs = np.ascontiguousarray(np.stack([_fmvec(v) for v in vl], 1))
        sel = np.zeros((128, 2), f32)
        sel[:, 1 - q] = 1.0
        w1, w2 = np.asarray(inp['rwkv_w1'][0], f32), np.asarray(inp['rwkv_w2'][0], f32)
        a1, a2 = np.asarray(inp['rwkv_a1'][0], f32), np.asarray(inp['rwkv_a2'][0], f32)
        m = dict(shared)
        m.update({
            'x_in': x_in, 'ropeC': C, 'ropeS': S, 'vecs': vecs, 'sel': sel,
            'w1s': np.ascontiguousarray(np.concatenate([w1[p1], w1[p2]], 1)),
            'w2p': _padz(w2[p1], w2[p2]),
            'a1s': np.ascontiguousarray(np.concatenate([a1[p1], a1[p2]], 1)),
            'a2p': _padz(a2[p1], a2[p2]),
        })
        maps.append(m)
    return maps


_CACHE = {}


def kernel(**inputs):
    cfg = Cfg(inputs['x_prompt'].shape[1], inputs['x_sample'].shape[1] // 2)
    key = (cfg.TP, cfg.TS)
    if key not in _CACHE:
        _CACHE[key] = build(cfg)
    nc, k = _CACHE[key]
    maps = prep_inputs(inputs, cfg)
    res = run_bass_kernel_spmd(nc, maps, core_ids=list(range(8)))
    B, S = inputs['x_prompt'].shape[0], inputs['x_prompt'].shape[1]
    DB, DS = inputs['x_sample'].shape[0], inputs['x_sample'].shape[1]
    yp = np.zeros((B, S, D), np.float32)
    ys = np.zeros((DB, DS, D), np.float32)
    for c in range(8):
        y = res.results[c]['y_out']
        s, q = c // 2, c % 2
        a, b = y[:cfg.TP], y[cfg.TP:]
        if q == 0:
            yp[c] = a
            ys[s, :cfg.TS] = b
        else:
            yp[c] = a[::-1]
            ys[s, cfg.TS:] = b[::-1]
    return yp, ys
```

```python
import math
from contextlib import ExitStack

import numpy as np
import concourse.bass as bass
import concourse.mybir as mybir
from concourse.bass_utils import run_bass_kernel_spmd

F32 = mybir.dt.float32
BF16 = mybir.dt.bfloat16
AF = mybir.ActivationFunctionType
ALU = mybir.AluOpType
AX = mybir.AxisListType

D = 1024
KC = 8
FH = 2816
FJ = 22
ROPE_THETA = 500000.0
SAME_ENG_SYNC = True
ENGS = ('pe', 'act', 'dve', 'pool', 'sp')

V_NMIX0, V_NMIX1, V_NFFN0, V_NFFN1, V_NFIN = 0, 1, 2, 3, 4
V_MU = 5
V_W0 = 11
V_A0 = 13
V_KK, V_KA, V_RK, V_LNW, V_LNB = 15, 16, 17, 18, 19
V_OMKA = 20
NV = 21


class Cfg:
    def __init__(self, TP=2048, TS=4096):
        self.TP, self.TS, self.TF = TP, TS, TS
        self.NOWN = TP + TS
        self.NALL = TP + 2 * TS
        self.NTP, self.NTS = TP // 512, TS // 512
        self.NT_OWN = self.NOWN // 512
        self.NT_ALL = self.NALL // 512
        self.XHW = self.NOWN + 4


class Rec:
    def __init__(self, nc):
        self.nc = nc
        self.lists = {e: [] for e in ENGS}
        self.cnt = {e: 0 for e in ENGS}
        self.lastw = {}
        self.readers = {}
        self.waited = {e: {} for e in ENGS}
        self.dcnt = {}
        self.semrefs = set()
        self.stgi = 0
        self.semmap = {}

    def add(self, eng, fn, reads=(), writes=(), dma=None):
        if dma is not None:
            dma = self.semmap.setdefault(dma, 'g%d' % len(self.semmap))
        deps = {}
        for k in reads:
            t = self.lastw.get(k)
            if t and deps.get(t[0], 0) < t[1]:
                deps[t[0]] = t[1]
        for k in writes:
            t = self.lastw.get(k)
            if t and deps.get(t[0], 0) < t[1]:
                deps[t[0]] = t[1]
            r = self.readers.get(k)
            if r:
                for s, v in r.items():
                    if deps.get(s, 0) < v:
                        deps[s] = v
        w = self.waited[eng]
        own = ('e', eng)
        for s in list(deps):
            if s[0] == 'd':
                deps[s] = self.dcnt[s]
        for s, v in deps.items():
            if s == own and (eng == 'pe' or eng == 'sp' or not SAME_ENG_SYNC):
                continue
            if w.get(s, 0) >= v:
                continue
            w[s] = v
            self.semrefs.add(s)
            self.lists[eng].append(('w', s, v))
        if dma is None:
            self.cnt[eng] += 1
            tok = (own, self.cnt[eng])
            self.lists[eng].append(('i', fn, own, 1))
        else:
            ref = ('d', dma)
            self.dcnt[ref] = self.dcnt.get(ref, 0) + 16
            tok = (ref, self.dcnt[ref])
            self.lists[eng].append(('i', fn, ref, 16))
        self.semrefs.add(tok[0])
        for k in writes:
            self.lastw[k] = tok
            self.readers[k] = {}
        for k in reads:
            d = self.readers.setdefault(k, {})
            if d.get(tok[0], 0) < tok[1]:
                d[tok[0]] = tok[1]
        return tok

    def barrier(self):
        for e in ENGS:
            w = self.waited[e]
            for x in ENGS:
                s = ('e', x)
                if x != e and self.cnt[x] > w.get(s, 0):
                    w[s] = self.cnt[x]
                    self.lists[e].append(('w', s, self.cnt[x]))
            for s, v in self.dcnt.items():
                if v > w.get(s, 0):
                    w[s] = v
                    self.lists[e].append(('w', s, v))
        self.lastw = {}
        self.readers = {}
        self.semmap = {}

    def mm(self, out, lhsT, rhs, st, sp, r, w):
        self.add('pe', lambda e: e.matmul(out, lhsT, rhs, start=st, stop=sp), r, w)

    def tr(self, out, in_, ident, r, w):
        self.add('pe', lambda e: e.transpose(out, in_, ident), r, w)

    def act(self, out, in_, func, r, w, bias=None, scale=None):
        kw = {}
        if bias is not None:
            kw['bias'] = bias
        if scale is not None:
            kw['scale'] = scale
        self.add('act', lambda e: e.activation(out=out, in_=in_, func=func, **kw), r, w)

    def tt(self, eng, out, in0, in1, op, r, w):
        self.add(eng, lambda e: e.tensor_tensor(out=out, in0=in0, in1=in1, op=op), r, w)

    def ts(self, eng, out, in0, s1, s2, op0, op1, r, w):
        self.add(eng, lambda e: e.tensor_scalar(out=out, in0=in0, scalar1=s1, scalar2=s2, op0=op0, op1=op1), r, w)

    def ts1(self, eng, out, in0, s1, op0, r, w):
        self.add(eng, lambda e: e.tensor_single_scalar(out=out, in_=in0, scalar=s1, op=op0), r, w)

    def stt(self, out, in0, scalar, in1, op0, op1, r, w):
        self.add('dve', lambda e: e.scalar_tensor_tensor(out=out, in0=in0, scalar=scalar, in1=in1, op0=op0, op1=op1), r, w)

    def cp(self, eng, out, in_, r, w):
        if eng == 'act':
            self.add('act', lambda e: e.activation(out=out, in_=in_, func=AF.Copy), r, w)
        else:
            self.add(eng, lambda e: e.tensor_copy(out=out, in_=in_), r, w)

    def scp(self, eng, out, in_, sc, r, w):
        if eng == 'act':
            self.add('act', lambda e: e.activation(out=out, in_=in_, func=AF.Copy, scale=sc), r, w)
        else:
            self.add(eng, lambda e: e.tensor_scalar_mul(out=out, in0=in_, scalar1=sc), r, w)

    def recip(self, out, in_, r, w):
        self.add('dve', lambda e: e.reciprocal(out=out, in_=in_), r, w)

    def red(self, out, in_, op, r, w):
        self.add('dve', lambda e: e.tensor_reduce(out=out, in_=in_, axis=AX.X, op=op), r, w)

    def scan(self, out, d0, d1, r, w):
        self.add('dve', lambda e: e.tensor_tensor_scan(out=out, data0=d0, data1=d1, initial=0.0,
                                                       op0=ALU.mult, op1=ALU.add), r, w)

    def memset(self, eng, ap, val, w):
        self.add(eng, lambda e: e.memset(ap, val), (), w)

    def dma(self, q, out, in_, sem, r, w, slow=False):
        if slow:
            self.add(q, lambda e: e.dma_start(out=out, in_=in_, allow_slow_non_contiguous=True), r, w, dma=sem)
        else:
            self.add(q, lambda e: e.dma_start(out=out, in_=in_), r, w, dma=sem)


class Defer:
    def __init__(self):
        self.q = []

    def __getattr__(self, name):
        return lambda *a, **kw: self.q.append((name, a, kw))


class _Probe(Rec):
    def __init__(self):
        self.meta = None

    def add(self, eng, fn, reads=(), writes=(), dma=None):
        self.meta = (eng, tuple(reads), tuple(writes), dma)


def _free_size(a):
    for x in a:
        sh = getattr(x, 'shape', None)
        if sh is not None and len(sh) >= 2:
            n = 1
            for v in sh[1:]:
                n *= int(v)
            return n
    return 64


def list_schedule(R, ops):
    import heapq
    probe = _Probe()
    n = len(ops)
    eng_of = [None] * n
    dur = [0.0] * n
    lat = [0.0] * n
    succ = [[] for _ in range(n)]
    npred = [0] * n
    lastw = {}
    readers = {}
    for i, (name, a, kw) in enumerate(ops):
        getattr(probe, name)(*a, **kw)
        eng, rd, wr, dma = probe.meta
        eng_of[i] = eng
        N = _free_size(a)
        if dma is not None:
            dur[i], lat[i] = 80.0, 2200.0
        elif eng == 'pe':
            dur[i] = 60.0 + 0.45 * N
        elif eng == 'act':
            dur[i] = 230.0 + 0.72 * N
        elif eng == 'dve':
            dur[i] = 70.0 + 1.1 * N
        else:
            dur[i] = 120.0 + 1.25 * N
        preds = set()
        for k_ in rd:
            t = lastw.get(k_)
            if t is not None:
                preds.add(t)
        for k_ in wr:
            t = lastw.get(k_)
            if t is not None:
                preds.add(t)
            for t in readers.get(k_, ()):
                preds.add(t)
        preds.discard(i)
        for p_ in preds:
            succ[p_].append(i)
        npred[i] = len(preds)
        for k_ in wr:
            lastw[k_] = i
            readers[k_] = []
        for k_ in rd:
            readers.setdefault(k_, []).append(i)
    SYNC = 350.0
    future = {e: [] for e in ENGS}
    avail = {e: [] for e in ENGS}
    rtime = [0.0] * n
    efree = {e: 0.0 for e in ENGS}
    for i in range(n):
        if npred[i] == 0:
            heapq.heappush(avail[eng_of[i]], i)
    order = []
    while len(order) < n:
        best = None
        for e in ENGS:
            f, av = future[e], avail[e]
            while f and f[0][0] <= efree[e]:
                heapq.heappush(av, heapq.heappop(f)[1])
            if av:
                c = (efree[e], av[0], e, True)
            elif f:
                c = (f[0][0], f[0][1], e, False)
            else:
                continue
            if best is None or c[:2] < best[:2]:
                best = c
        st_, i, e, from_av = best
        if from_av:
            heapq.heappop(avail[e])
        else:
            heapq.heappop(future[e])
        efree[e] = st_ + dur[i]
        fin_i = st_ + dur[i] + lat[i]
        order.append(i)
        for s_ in succ[i]:
            npred[s_] -= 1
            t_ = fin_i + (SYNC if (eng_of[s_] != e or lat[i]) else 0.0)
            if t_ > rtime[s_]:
                rtime[s_] = t_
            if npred[s_] == 0:
                heapq.heappush(future[eng_of[s_]], (rtime[s_], s_))
    for i in order:
        name, a, kw = ops[i]
        getattr(R, name)(*a, **kw)


class Arena:
    def __init__(self, sb, cap):
        self.sb, self.cap, self.off, self.base = sb, cap, 0, 0

    def alloc(self, shape, dt):
        n = int(np.prod(shape))
        sz = 4 if dt == F32 else 2
        nb = (n * sz + 63) // 64 * 64
        assert self.off + nb <= self.cap, ("SBUF arena overflow", self.off, nb, self.cap)
        a = self.sb[:, self.off // 4:(self.off + nb) // 4]
        self.off += nb
        if dt == BF16:
            a = a.bitcast(BF16)
        a = a[:, 0:n]
        if len(shape) == 2:
            a = a.rearrange("p (a b) -> p a b", b=shape[1])
        elif len(shape) == 3:
            a = a.rearrange("p (a b c) -> p a b c", b=shape[1], c=shape[2])
        return a

    def set_base(self):
        self.base = self.off

    def reset(self):
        self.off = self.base


def fm(ap):
    return ap.rearrange("(k p) t -> p k t", p=128)


def tm(ap):
    return ap.rearrange("(n p) c -> p n c", p=128)


class K:
    pass


def load_w(k, dst, src, nk, ncols, name, scale=None, last_rows=128, cb=1024):
    cb = k.stgw
    R = k.R
    for kc in range(nk):
        rows = 128 if kc < nk - 1 else last_rows
        for c0 in range(0, ncols, cb):
            c1 = min(ncols, c0 + cb)
            b = R.stgi % len(k.stg)
            R.stgi += 1
            st = k.stg[b][0:rows, 0:c1 - c0]
            R.dma('sp' if R.stgi % 2 else 'act', st, src[kc * 128:kc * 128 + rows, c0:c1], 'stg%d' % b, [], [('stg', b)])
            eng = 'act' if (R.stgi % 2) else 'dve'
            o = dst[0:rows, kc, c0:c1]
            if scale is not None:
                R.scp(eng, o, st, scale(kc)[0:rows], [('stg', b)], [(name, kc, c0 // cb)])
            else:
                R.cp(eng, o, st, [('stg', b)], [(name, kc, c0 // cb)])


def wkeys(name, kc, c0, c1, cb=None):
    cb = cb or K.stgw
    return [(name, kc, j) for j in range(c0 // cb, (c1 - 1) // cb + 1)]


def vcol(k, v, kc):
    return k.vecs[:, v, kc:kc + 1]


def rms_rstd(k, xT, xkey, nchunks, inv_n, eps, psb, sq, rs, rstd, tagsq):
    R = k.R
    for kc in range(nchunks):
        s = sq[kc % 2]
        R.act(s, xT[:, kc, :], AF.Square, [(xkey, kc)], [(tagsq, kc % 2)])
        R.mm(k.ps[psb], k.ones_bf, s, kc == 0, kc == nchunks - 1, [(tagsq, kc % 2)], [('ps', psb)])
    R.act(rs, k.ps[psb], AF.Sqrt, [('ps', psb)], [tagsq + 'rs'], bias=eps, scale=inv_n)
    R.recip(rstd, rs, [tagsq + 'rs'], [tagsq + 'rstd'])


def phase_consts(k):
    R, ar, cfg = k.R, k.ar, k.cfg
    k.cmf = ar.alloc([3, 128], F32)
    k.cmb = ar.alloc([3, 128], BF16)
    k.vecs = ar.alloc([NV, 8], F32)
    k.sel = ar.alloc([2], F32)
    k.sublnw = ar.alloc([1], F32)
    k.lamt = ar.alloc([4], F32)
    k.qkmax = ar.alloc([2, 2, 8], F32)
    k.negM = ar.alloc([2, 8], F32)
    k.lamneg = ar.alloc([1], F32)
    k.zero_bf = ar.alloc([8], BF16)
    k.eps5 = ar.alloc([1], F32)
    ar.set_base()
    R.dma('sp', k.cmf, k.d['cmat'], 'c0', [], ['cmf'])
    R.dma('sp', k.vecs[:, 0:NV - 1, :], k.d['vecs'], 'c1', [], ['vecs'])
    R.dma('sp', k.sel, k.d['sel'], 'c4', [], ['sel'])
    R.dma('sp', k.sublnw, k.d['sublnw'], 'c5', [], ['sublnw'])
    R.dma('sp', k.lamt[0:64, :], k.d['lamt'], 'c6', [], ['lamt'])
    R.cp('dve', k.cmb, k.cmf, ['cmf'], ['cmb'])
    R.memset('dve', k.qkmax, 0.0, ['qkmax'])
    R.memset('dve', k.zero_bf, 0.0, ['zero_bf'])
    R.memset('dve', k.eps5, 1e-5, ['eps5'])
    R.ts('dve', k.vecs[:, V_OMKA, :], k.vecs[:, V_KA, :], -1.0, 1.0, ALU.mult, ALU.add, ['vecs'], ['vecs'])
    k.ident_f = k.cmf[:, 0, :]
    k.ones_f = k.cmf[:, 1, :]
    k.ident_bf = k.cmb[:, 0, :]
    k.ones_bf = k.cmb[:, 1, :]
    k.onesbd_bf = k.cmb[:, 2, :]
    R.barrier()


def phase_A(k):
    R, ar, cfg, d = k.R, k.ar, k.cfg, k.d
    ar.reset()
    Wqkv = ar.alloc([8, 3072], BF16)
    Wqks = ar.alloc([8, 2048], BF16)
    k.stg = [ar.alloc([1024], F32) for _ in range(4)]
    K.stgw = k.stgw = 1024
    load_w(k, Wqkv, d['w_qkv'], 8, 3072, 'Wqkv', scale=lambda kc: vcol(k, V_NMIX0, kc))
    load_w(k, Wqks, d['w_qks'], 8, 2048, 'Wqks', scale=lambda kc: vcol(k, V_NMIX0, kc))
    xtok = [ar.alloc([4, 1024], F32) for _ in range(2)]
    xT = ar.alloc([8, 512], F32)
    sq = [ar.alloc([512], BF16) for _ in range(2)]
    rs = ar.alloc([512], F32)
    rstd = ar.alloc([512], F32)
    xn = ar.alloc([8, 512], BF16)
    Ct = [ar.alloc([512], F32) for _ in range(2)]
    St = [ar.alloc([512], F32) for _ in range(2)]
    t1 = [ar.alloc([512], F32) for _ in range(2)]
    t2 = [ar.alloc([512], F32) for _ in range(2)]
    qr = [ar.alloc([512], BF16) for _ in range(3)]
    sq2 = [ar.alloc([512], BF16) for _ in range(2)]
    mx = [ar.alloc([1], F32) for _ in range(2)]
    vtok = ar.alloc([4, 1024], BF16)
    xin = tm(d['x_in'])
    XT = fm(d['XT'])

    def loads(i):
        b = i % 2
        R.dma('sp', xtok[b], xin[:, 4 * i:4 * i + 4, :], 'xtok%d' % b, [], [('xtok', b)])
        R.dma('sp', Ct[b], d['ropeC'][:, i * 512:(i + 1) * 512], 'rope%d' % b, [], [('Ct', b)])
        R.dma('sp', St[b], d['ropeS'][:, i * 512:(i + 1) * 512], 'rope%d' % b, [], [('St', b)])

    R_real = R
    R = k.R = Defer()
    loads(0)
    cnt = 0
    pend = []
    for i in range(cfg.NT_ALL):
        own = i < cfg.NT_OWN
        seq = 0 if i < cfg.NTP else 1
        b = i % 2
        if i + 1 < cfg.NT_ALL:
            loads(i + 1)
        for kc in range(8):
            pb = kc % 2
            for n in range(4):
                R.tr(k.ps[pb][:, n * 128:(n + 1) * 128], xtok[b][:, n, kc * 128:(kc + 1) * 128], k.ident_f,
                     [('xtok', b)], [('ps', pb)])
            R.cp('act' if kc % 2 else 'dve', xT[:, kc, :], k.ps[pb], [('ps', pb)], [('xT', kc)])
        if own:
            R.dma('pool', XT[:, :, i * 512:(i + 1) * 512], xT, 'xTst', [('xT', kc) for kc in range(8)], [])
        rms_rstd(k, xT, 'xT', 8, 1.0 / D, 1e-6, 2, sq, rs, rstd, 'A')
        for kc in range(8):
            R.tt('pool' if kc % 2 else 'dve', xn[:, kc, :], xT[:, kc, :], rstd, ALU.mult,
                 [('xT', kc), 'Arstd'], [('xn', kc)])
        xnk = [('xn', kc) for kc in range(8)]
        for which in ((0, 1) if own else (1,)):
            for hc in range(8):
                pa, pb2 = (3, 4) if cnt % 2 == 0 else (5, 6)
                c0 = which * 1024 + hc * 128
                for kc in range(8):
                    R.mm(k.ps[pa], Wqkv[:, kc, c0:c0 + 128], xn[:, kc, :], kc == 0, kc == 7,
                         [('xn', kc)] + wkeys('Wqkv', kc, c0, c0 + 128), [('ps', pa)])
                for kc in range(8):
                    R.mm(k.ps[pb2], Wqks[:, kc, c0:c0 + 128], xn[:, kc, :], kc == 0, kc == 7,
                         [('xn', kc)] + wkeys('Wqks', kc, c0, c0 + 128), [('ps', pb2)])
                tb = cnt % 2
                R.tt('dve', t1[tb], k.ps[pa], Ct[b], ALU.mult, [('ps', pa), ('Ct', b)], [('t1', tb)])
                R.tt('dve', t2[tb], k.ps[pb2], St[b], ALU.mult, [('ps', pb2), ('St', b)], [('t2', tb)])
                qb = cnt % 3
                R.tt('pool', qr[qb], t1[tb], t2[tb], ALU.add, [('t1', tb), ('t2', tb)], [('qr', qb)])
                dst = d['QT'] if which == 0 else d['KT']
                R.dma('pool', dst[hc * 128:(hc + 1) * 128, i * 512:(i + 1) * 512], qr[qb], 'qr%d' % qb,
                      [('qr', qb)], [])
                R.act(sq2[tb], qr[qb], AF.Square, [('qr', qb)], [('sq2', tb)])
                if pend:
                    pend.pop()()

                def stats(tb=tb, dstm=k.qkmax[:, which, seq, hc:hc + 1]):
                    R.mm(k.ps[7], k.ones_bf, sq2[tb], True, True, [('sq2', tb)], [('ps', 7)])
                    R.red(mx[tb], k.ps[7], ALU.max, [('ps', 7)], [('mx', tb)])
                    R.tt('dve', dstm, dstm, mx[tb], ALU.max, [('mx', tb), 'qkmax'], ['qkmax'])

                pend.append(stats)
                cnt += 1
        for n in range(4):
            for cbk in range(2):
                pv = 3 + (cnt % 4)
                cnt += 1
                c0 = 2048 + cbk * 512
                for kc in range(8):
                    R.mm(k.ps[pv], xn[:, kc, n * 128:(n + 1) * 128], Wqkv[:, kc, c0:c0 + 512], kc == 0, kc == 7,
                         [('xn', kc)] + wkeys('Wqkv', kc, c0, c0 + 512), [('ps', pv)])
                R.cp('act' if cbk else 'dve', vtok[:, n, cbk * 512:(cbk + 1) * 512], k.ps[pv],
                     [('ps', pv)], [('vtok', n, cbk)])
        R.dma('pool', tm(d['VV'])[:, 4 * i:4 * i + 4, :], vtok, 'vtok',
              [('vtok', n, c) for n in range(4) for c in range(2)], [])
    if pend:
        pend.pop()()
    ops_ = R.q
    R = k.R = R_real
    list_schedule(R, ops_)
    R.barrier()


def phase_B(k):
    R, ar, cfg, d = k.R, k.ar, k.cfg, k.d
    ar.reset()
    SKM = cfg.TS + cfg.TF
    KhT = [ar.alloc([SKM], BF16) for _ in range(2)]
    Vh = [ar.alloc([SKM // 128, 128], BF16) for _ in range(2)]
    qz = [ar.alloc([2, 512], BF16) for _ in range(2)]
    et = [ar.alloc([2, 512], BF16) for _ in range(6)]
    accD = ar.alloc([512], F32)
    accP = ar.alloc([512], F32)
    accb = ar.alloc([512], BF16)
    nsb = ar.alloc([3, 512], F32)
    rr = ar.alloc([2, 512], F32)
    lnl = ar.alloc([2, 512], F32)
    tq = ar.alloc([2, 512], F32)
    o = ar.alloc([512], F32)
    sq = ar.alloc([512], BF16)
    rs = ar.alloc([512], F32)
    rstd = ar.alloc([512], F32)
    oT = [ar.alloc([512], BF16) for _ in range(2)]
    tmp = ar.alloc([4], F32)
    fl = lambda x: x.rearrange("p a b -> p (a b)")
    R.tt('dve', tmp[0:64, 0:1], k.lamt[0:64, 0:1], k.lamt[0:64, 1:2], ALU.mult, ['lamt'], ['ltmp'])
    R.tt('dve', tmp[0:64, 1:2], k.lamt[0:64, 2:3], k.lamt[0:64, 3:4], ALU.mult, ['lamt', 'ltmp'], ['ltmp'])
    R.mm(k.ps[0][:, 0:2], k.ones_f[0:64, :], tmp[0:64, 0:2], True, True, ['ltmp'], [('ps', 0)])
    R.act(tmp[:, 2:4], k.ps[0][:, 0:2], AF.Exp, [('ps', 0)], ['ltmp2'])
    R.tt('dve', k.lamneg, tmp[:, 3:4], tmp[:, 2:3], ALU.subtract, ['ltmp2'], ['lamneg'])
    R.ts1('dve', k.lamneg, k.lamneg, -0.2, ALU.add, ['lamneg'], ['lamneg'])
    R.tt('dve', k.negM, k.qkmax[:, 0], k.qkmax[:, 1], ALU.mult, ['qkmax'], ['negM'])
    R.act(k.negM, k.negM, AF.Sqrt, ['negM'], ['negM'])
    R.ts1('dve', k.negM, k.negM, -0.125, ALU.mult, ['negM'], ['negM'])
    for b in range(2):
        R.memset('dve', qz[b], 0.0, [('qz', b)])
    R_real = R
    R = k.R = Defer()
    VVt = tm(d['VV'])
    hcount = 0
    qcount = 0
    ecount = 0
    for seq in range(2):
        if seq == 0:
            Sq, q0, Sk, k0 = cfg.TP, 0, cfg.TP, 0
        else:
            Sq, q0, Sk, k0 = cfg.TS, cfg.TP, cfg.TS + cfg.TF, cfg.TP
        nk = Sk // 128
        for h in range(8):
            hb = hcount % 2
            hcount += 1
            R.dma('sp', KhT[hb][:, 0:Sk], d['KT'][h * 128:(h + 1) * 128, k0:k0 + Sk], 'kv%d' % hb,
                  [], [('KhT', hb)])
            for n0 in range(0, nk, 16):
                n1 = min(nk, n0 + 16)
                R.dma('sp', Vh[hb][:, n0:n1, :], VVt[:, k0 // 128 + n0:k0 // 128 + n1, h * 128:(h + 1) * 128],
                      'kv%d' % hb, [], [('Vh', hb)])
            negM = k.negM[:, seq, h:h + 1]
            for qt in range(Sq // 512):
                qb = qcount % 2
                qcount += 1
                qc0 = q0 + qt * 512
                R.dma('sp', qz[qb][0:64, 0, :], d['QT'][h * 128:h * 128 + 64, qc0:qc0 + 512], 'qz%d' % qb,
                      [], [('qz', qb)])
                R.dma('sp', qz[qb][64:128, 1, :], d['QT'][h * 128 + 64:h * 128 + 128, qc0:qc0 + 512], 'qz%d' % qb,
                      [], [('qz', qb)])

                def smm(j):
                    sb = (j % 2) * 2
                    for c in range(2):
                        R.mm(k.ps[sb + c], KhT[hb][:, j * 128:(j + 1) * 128], qz[qb][:, c, :], True, True,
                             [('KhT', hb), ('qz', qb)], [('ps', sb + c)])

                smm(0)
                inited = {'D': False, 'P': False}
                for j in range(nk):
                    if j + 1 < nk:
                        smm(j + 1)
                    sb = (j % 2) * 2
                    eb = ecount % 6
                    ecount += 1
                    e2 = et[eb]
                    R.act(fl(e2), k.pd[j % 2], AF.Exp, [('ps', sb), ('ps', sb + 1), 'negM'],
                          [('et', eb)], bias=negM, scale=0.125)
                    for c in range(2):
                        R.mm(k.ps[4 + c], Vh[hb][:, j, :], e2[:, c, :], j == 0, j == nk - 1,
                             [('Vh', hb), ('et', eb)], [('ps', 4 + c)])
                    R.mm(k.ps[6], k.ones_bf, e2[:, 0, :], j == 0, j == nk - 1, [('et', eb)], [('ps', 6)])
                    who = 'P' if j % 3 == 2 else 'D'
                    eng, acc = ('pool', accP) if who == 'P' else ('dve', accD)
                    if not inited[who]:
                        R.cp(eng, acc, e2[:, 1, :], [('et', eb)], ['acc' + who])
                        inited[who] = True
                    else:
                        R.tt(eng, acc, acc, e2[:, 1, :], ALU.add, [('et', eb), 'acc' + who], ['acc' + who])
                for c in range(3):
                    R.cp('dve', nsb[:, c, :], k.ps[4 + c], [('ps', 4 + c)], [('nsb', c)])
                if inited['P']:
                    R.tt('pool', accb, accD, accP, ALU.add, ['accD', 'accP'], ['accb'])
                else:
                    R.cp('pool', accb, accD, ['accD'], ['accb'])
                R.mm(k.ps[7], k.ones_bf, accb, True, True, ['accb'], [('ps', 7)])
                R.act(lnl[:, 0, :], nsb[:, 2, :], AF.Ln, [('nsb', 2)], [('lnl', 0)])
                R.act(lnl[:, 1, :], k.ps[7], AF.Ln, [('ps', 7)], [('lnl', 1)])
                R.act(fl(rr), fl(lnl), AF.Exp, [('lnl', 0), ('lnl', 1)], [('rr', 0), ('rr', 1)], scale=-1.0)
                R.tt('pool', fl(tq), fl(nsb[:, 0:2, :]), fl(rr), ALU.mult, [('nsb', 0), ('nsb', 1), ('rr', 0), ('rr', 1)],
                     ['tq'])
                R.stt(o, tq[:, 1, :], k.lamneg, tq[:, 0, :], ALU.mult, ALU.add, ['tq', 'lamneg'], ['o'])
                R.tt('pool', sq, o, o, ALU.mult, ['o'], ['osq'])
                R.mm(k.ps[7], k.ones_bf, sq, True, True, ['osq'], [('ps', 7)])
                R.act(rs, k.ps[7], AF.Ln, [('ps', 7)], ['ors'], bias=k.eps5, scale=1.0 / 128)
                R.act(rstd, rs, AF.Exp, ['ors'], ['orstd'], scale=-0.5)
                ob = qcount % 2
                R.tt('pool', oT[ob], o, rstd, ALU.mult, ['o', 'orstd'], [('oT', ob)])
                R.dma('pool', d['OT'][h * 128:(h + 1) * 128, qc0:qc0 + 512], oT[ob], 'oT%d' % ob, [('oT', ob)], [])
    ops_ = R.q
    R = k.R = R_real
    list_schedule(R, ops_)
    R.barrier()


def phase_C1(k):
    R, ar, cfg, d = k.R, k.ar, k.cfg, k.d
    ar.reset()
    Wo = ar.alloc([8, 1024], BF16)
    k.stg = [ar.alloc([1024], F32) for _ in range(4)]
    K.stgw = k.stgw = 1024
    sc = ar.alloc([1], F32)
    R.ts1('dve', sc, k.sublnw, 0.8, ALU.mult, ['sublnw'], ['sc'])
    load_w(k, Wo, d['w_o'], 8, 1024, 'Wo', scale=lambda kc: sc)
    oT = [ar.alloc([8, 512], BF16) for _ in range(2)]
    xT = [ar.alloc([8, 512], F32) for _ in range(2)]
    XT = fm(d['XT'])
    OT = fm(d['OT'])

    def loads(i):
        b = i % 2
        R.dma('sp', oT[b], OT[:, :, i * 512:(i + 1) * 512], 'c1o%d' % b, [], [('oT', b)])
        R.dma('sp', xT[b], XT[:, :, i * 512:(i + 1) * 512], 'c1x%d' % b, [], [('xT', b, kc) for kc in range(8)])

    R_real = R
    R = k.R = Defer()
    loads(0)
    for i in range(cfg.NT_OWN):
        b = i % 2
        if i + 1 < cfg.NT_OWN:
            loads(i + 1)
        for dc in range(8):
            pb = dc % 4
            for h in range(8):
                R.mm(k.ps[pb], Wo[:, h, dc * 128:(dc + 1) * 128], oT[b][:, h, :], h == 0, h == 7,
                     [('oT', b)] + wkeys('Wo', h, dc * 128, (dc + 1) * 128), [('ps', pb)])
            R.tt('dve', xT[b][:, dc, :], k.ps[pb], xT[b][:, dc, :], ALU.add, [('ps', pb), ('xT', b, dc)],
                 [('xT', b, dc)])
        R.dma('pool', XT[:, :, i * 512:(i + 1) * 512], xT[b], 'c1x%d' % b, [('xT', b, kc) for kc in range(8)], [])
    ops_ = R.q
    R = k.R = R_real
    list_schedule(R, ops_)
    R.barrier()


def phase_FFN(k, layer):
    R, ar, cfg, d = k.R, k.ar, k.cfg, k.d
    ar.reset()
    Win = ar.alloc([8, 2 * FH], BF16)
    Wout = ar.alloc([FJ, 1024], BF16)
    hidraw = ar.alloc([FJ * 256], F32)
    k.stg = [hidraw[:, j * 1024:(j + 1) * 1024] for j in range(5)]
    K.stgw = k.stgw = 1024
    vn = V_NFFN0 if layer == 0 else V_NFFN1
    load_w(k, Win, d['w_ffn_in'][layer], 8, 2 * FH, 'Win', scale=lambda kc: vcol(k, vn, kc))
    load_w(k, Wout, d['w_ffn_out'][layer], FJ, 1024, 'Wout')
    R.barrier()
    R_real = R
    R = k.R = Defer()
    xT = ar.alloc([8, 512], F32)
    xn = ar.alloc([8, 512], BF16)
    hid = hidraw.bitcast(BF16).rearrange("p (a b) -> p a b", b=512)
    ytok = hidraw[:, 0:4096].rearrange("p (n c) -> p n c", c=1024)
    sq = [ar.alloc([512], BF16) for _ in range(2)]
    rs = ar.alloc([512], F32)
    rstd = ar.alloc([512], F32)
    sg = [ar.alloc([512], F32) for _ in range(2)]
    lastc = ar.alloc([8], F32)
    xh = xn
    yT = xT
    XT = fm(d['XT'])
    XH = fm(d['XH'])
    if layer == 0:
        for c in (0, cfg.TP + 1, cfg.TP + 2):
            R.dma('pool', XH[:, :, c:c + 1], k.zero_bf.unsqueeze(2), 'xhz', ['zero_bf'], [], slow=True)
    xk = [('xT', kc) for kc in range(8)]
    for i in range(cfg.NT_OWN):
        R.dma('sp', xT, XT[:, :, i * 512:(i + 1) * 512], 'fx', [], xk)
        rms_rstd(k, xT, 'xT', 8, 1.0 / D, 1e-6, 7, sq, rs, rstd, 'F')
        for kc in range(8):
            R.tt('pool' if kc % 2 else 'dve', xn[:, kc, :], xT[:, kc, :], rstd, ALU.mult,
                 [('xT', kc), 'Frstd'], [('xn', kc)])
        for j in range(FJ):
            pg, pu = (0, 1) if j % 2 == 0 else (2, 3)
            for kc in range(8):
                R.mm(k.ps[pg], Win[:, kc, j * 128:(j + 1) * 128], xn[:, kc, :], kc == 0, kc == 7,
                     [('xn', kc)] + wkeys('Win', kc, j * 128, (j + 1) * 128), [('ps', pg)])
            for kc in range(8):
                c0 = FH + j * 128
                R.mm(k.ps[pu], Win[:, kc, c0:c0 + 128], xn[:, kc, :], kc == 0, kc == 7,
                     [('xn', kc)] + wkeys('Win', kc, c0, c0 + 128), [('ps', pu)])
            R.act(sg[j % 2], k.ps[pg], AF.Silu, [('ps', pg)], [('sg', j % 2)])
            R.tt('dve', hid[:, j, :], k.ps[pu], sg[j % 2], ALU.mult, [('ps', pu), ('sg', j % 2)], [('hid', j)])
        for dc in range(8):
            pb = 4 + dc % 2
            for j in range(FJ):
                R.mm(k.ps[pb], Wout[:, j, dc * 128:(dc + 1) * 128], hid[:, j, :], j == 0, j == FJ - 1,
                     [('hid', j)] + wkeys('Wout', j, dc * 128, (dc + 1) * 128), [('ps', pb)])
            R.tt('dve', xT[:, dc, :], k.ps[pb], xT[:, dc, :], ALU.add, [('ps', pb), ('xT', dc)], [('xT', dc)])
        rms_rstd(k, xT, 'xT', 8, 1.0 / D, 1e-6, 6, sq, rs, rstd, 'G')
        if layer == 0:
            R.dma('pool', XT[:, :, i * 512:(i + 1) * 512], xT, 'fxs', xk, [])
            for kc in range(8):
                R.tt('pool' if kc % 2 else 'dve', xh[:, kc, :], xT[:, kc, :], rstd, ALU.mult,
                     [('xT', kc), 'Grstd'], [('xn', kc)])
            xhk = [('xn', kc) for kc in range(8)]
            base = 1 + i * 512 if i < cfg.NTP else cfg.TP + 3 + (i - cfg.NTP) * 512
            R.dma('pool', XH[:, :, base:base + 512], xh, 'fxh', xhk, [])
            if i == cfg.NT_OWN - 1:
                R.cp('dve', lastc, xh[:, :, 511], xhk, ['lastc'])
                R.dma('pool', d['ex1_in'], lastc, 'ex1', ['lastc'], ['ex1_in'])
        else:
            for kc in range(8):
                R.stt(yT[:, kc, :], xT[:, kc, :], vcol(k, V_NFIN, kc), rstd, ALU.mult, ALU.mult,
                      [('xT', kc), 'Grstd'], [('xT', kc)])
            for n in range(4):
                for half in range(2):
                    pb = half
                    for q4 in range(4):
                        kc = half * 4 + q4
                        R.tr(k.ps[pb][:, q4 * 128:(q4 + 1) * 128], yT[:, kc, n * 128:(n + 1) * 128], k.ident_f,
                             [('xT', kc)], [('ps', pb)])
                    R.cp('act' if half else 'dve', ytok[:, n, half * 512:(half + 1) * 512], k.ps[pb],
                         [('ps', pb)], [('hid', jj) for jj in range(FJ)])
            R.dma('pool', tm(d['y_out'])[:, 4 * i:4 * i + 4, :], ytok, 'yst',
                  [('hid', jj) for jj in range(FJ)], [])
    ops_ = R.q
    R = k.R = R_real
    list_schedule(R, ops_)
    R.barrier()


CDEC = math.exp(-0.5)
PAIRS = [[0, 1], [2, 3], [4, 5], [6, 7]]


def add_cc(R, fn, r, w, name):
    deps = {}
    for kk_ in list(r) + list(w):
        t = R.lastw.get(kk_)
        if t and deps.get(t[0], 0) < t[1]:
            deps[t[0]] = t[1]
    wd = R.waited['pool']
    for s_, v in deps.items():
        if s_[0] == 'd':
            v = R.dcnt[s_]
        if wd.get(s_, 0) >= v:
            continue
        wd[s_] = v
        R.semrefs.add(s_)
        R.lists['pool'].append(('w', s_, v))
    ref = ('d', 'cc_' + name)
    R.dcnt[ref] = R.dcnt.get(ref, 0) + 1
    tok = (ref, R.dcnt[ref])
    R.lists['pool'].append(('i', fn, ref, 0))
    R.semrefs.add(ref)
    for kk_ in w:
        R.lastw[kk_] = tok
        R.readers[kk_] = {}


def phase_X1(k):
    R, ar, cfg, d = k.R, k.ar, k.cfg, k.d
    ar.reset()
    g = ar.alloc([2, 8], F32)
    t = ar.alloc([8], F32)
    pb = ar.alloc([8], BF16)
    add_cc(R, lambda e: e.collective_compute("AllGather", ALU.bypass, replica_groups=PAIRS,
                                             ins=[d['ex1_in']], outs=[d['ex1_out']]), [], ['ex1o'], 'cc1')
    R.dma('sp', g, d['ex1_out'].rearrange("(s p) c -> p s c", p=128), 'x1g', ['ex1o'], ['x1g'])
    R.scp('dve', t, g[:, 0, :], k.sel[:, 0:1], ['x1g'], ['x1t'])
    R.stt(pb, g[:, 1, :], k.sel[:, 1:2], t, ALU.mult, ALU.add, ['x1g', 'x1t'], ['x1p'])
    R.dma('sp', fm(d['XH'])[:, :, cfg.XHW - 1:cfg.XHW], pb.unsqueeze(2), 'x1s', ['x1p'], [], slow=True)
    R.barrier()


def phase_D(k):
    R, ar, cfg, d = k.R, k.ar, k.cfg, k.d
    ar.reset()
    TD = 256
    NTD = cfg.NOWN // TD
    Wrkv = ar.alloc([8, 3072], BF16)
    W1s = ar.alloc([8, 128], BF16)
    A1s = ar.alloc([8, 128], BF16)
    G1 = ar.alloc([8, 256], BF16)
    W2p = ar.alloc([2, 1024], BF16)
    A2p = ar.alloc([2, 1024], BF16)
    G2 = ar.alloc([2, 1024], BF16)
    k.stg = [ar.alloc([1024], F32) for _ in range(2)]
    K.stgw = k.stgw = 1024
    nm = lambda kc: vcol(k, V_NMIX1, kc)
    for j in range(3):
        load_w(k, Wrkv[:, :, j * 1024:(j + 1) * 1024], d['w_rkv'][j], 8, 1024, 'Wrkv%d' % j, scale=nm)
    load_w(k, W1s, d['w1s'], 8, 128, 'W1s', scale=nm)
    load_w(k, A1s, d['a1s'], 8, 128, 'A1s', scale=nm)
    load_w(k, G1, d['g1p'], 8, 256, 'G1', scale=nm)
    load_w(k, W2p, d['w2p'], 2, 1024, 'W2p')
    load_w(k, A2p, d['a2p'], 2, 1024, 'A2p')
    load_w(k, G2, d['g2p'], 2, 1024, 'G2')
    smask = ar.alloc([512], F32)
    R.dma('sp', smask, d['smask'], 'dsm', [], ['smask'])
    sm = smask[:, 0:TD]
    xh = [ar.alloc([8, TD + 2], BF16) for _ in range(2)]
    xm = [[ar.alloc([8, TD], BF16) for _ in range(6)] for _ in range(2)]
    hw = [ar.alloc([TD], BF16) for _ in range(2)]
    ha = [ar.alloc([TD], BF16) for _ in range(2)]
    hg = [ar.alloc([2, TD], BF16) for _ in range(2)]
    mtmp = ar.alloc([TD], F32)
    mxx = ar.alloc([TD], F32)
    f32n = ['rT', 'kT', 'sig0', 'sig1', 'a0', 'a1', 'tk', 'ssm', 'lnv', 'rin', 'kk', 'tK0', 'tK1', 'kd0', 'kd1',
            'b0', 'b1', 'cs', 'pre', 'cum', 'cume', 'E1', 'E2', 'E3']
    T = [{n: ar.alloc([TD], F32) for n in f32n} for _ in range(2)]
    bfn = ['vTb', 'sqk', 'cr', 'gT', 'cv', 'o_r', 'o_kk', 'o_b', 'o_k']
    B = [{n: ar.alloc([TD], BF16) for n in bfn} for _ in range(2)]
    tmj = [{n: ar.alloc([2, 128], BF16) for n in ('bh', 'kh', 'vt')} for _ in range(2)]
    pc = [ar.alloc([2], F32) for _ in range(2)]
    XH = fm(d['XH'])
    wr = lambda j, kc: [('Wrkv%d' % j, kc, 0)]
    cnt = [0]
    NSUB = TD // 128

    scnt = [0, 0]

    def bank(st=None):
        if st is None:
            cnt[0] += 1
            return cnt[0] % 6
        scnt[st] += 1
        return 3 * st + scnt[st] % 3

    def loads(i):
        t0 = i * TD
        base = 1 + t0 if t0 < cfg.TP else cfg.TP + 3 + (t0 - cfg.TP)
        R.dma('sp', xh[i % 2], XH[:, :, base - 1:base + TD + 1], 'dxh%d' % (i % 2), [], [('xh', i % 2)])

    def to_tm(R, src, skey, name, par, dst, i, dc):
        pb = 6 + par
        for n in range(NSUB):
            R.tr(k.psb[pb][:, n * 128:(n + 1) * 128], src[:, n * 128:(n + 1) * 128], k.ident_bf, [skey], [('ps', pb)])
        R.cp('dve', tmj[par][name], k.psb[pb][:, 0:TD].rearrange("p (n c) -> p n c", c=128), [('ps', pb)],
             [(name, par)])
        R.dma('sp', tm(dst)[:, NSUB * i:NSUB * i + NSUB, dc * 128:(dc + 1) * 128], tmj[par][name],
              'dt_%s%d' % (name, par), [(name, par)], [])

    def mix_kc(R, i, kc):
        b = i % 2
        xb = xh[b]
        cur = xb[:, kc, 1:TD + 1]
        R.tt('pool', mtmp, xb[:, kc, 0:TD], xb[:, kc, 2:TD + 2], ALU.add, [('xh', b)], ['mtmp'])
        R.stt(mxx, mtmp, 0.5, cur, ALU.mult, ALU.subtract, ['mtmp', ('xh', b)], ['mxx'])
        for m in range(6):
            R.stt(xm[b][m][:, kc, :], mxx, vcol(k, V_MU + m, kc), cur, ALU.mult, ALU.add,
                  ['mxx', ('xh', b)], [('xm', b, m, kc)])

    def hidden(i):
        b = i % 2
        pb = bank()
        for kc in range(8):
            R.mm(k.ps[pb][:, 0:TD], W1s[:, kc, :], xm[b][3][:, kc, :], kc == 0, kc == 7,
                 [('xm', b, 3, kc), ('W1s', kc, 0)], [('ps', pb)])
        R.act(hw[b], k.ps[pb][:, 0:TD], AF.Tanh, [('ps', pb)], [('hw', b)])
        pb = bank()
        for kc in range(8):
            R.mm(k.ps[pb][:, 0:TD], A1s[:, kc, :], xm[b][4][:, kc, :], kc == 0, kc == 7,
                 [('xm', b, 4, kc), ('A1s', kc, 0)], [('ps', pb)])
        R.cp('act', ha[b], k.ps[pb][:, 0:TD], [('ps', pb)], [('ha', b)])
        for hf in range(2):
            pb = bank()
            for kc in range(8):
                R.mm(k.ps[pb][:, 0:TD], G1[:, kc, hf * 128:(hf + 1) * 128], xm[b][5][:, kc, :], kc == 0, kc == 7,
                     [('xm', b, 5, kc), ('G1', kc, 0)], [('ps', pb)])
            R.act(hg[b][:, hf, :], k.ps[pb][:, 0:TD], AF.Sigmoid, [('ps', pb)], [('hg', b, hf)])

    def front(R, i, dc, par):
        b = i % 2
        t, bb = T[par], B[par]
        K_ = lambda n: (n, par)
        dsl = slice(dc * 128, (dc + 1) * 128)
        cols = slice(i * TD, (i + 1) * TD)
        pr, pk, pv = bank(par), bank(par), bank(par)
        for j, pp in ((0, pr), (1, pk), (2, pv)):
            for kc in range(8):
                R.mm(k.ps[pp][:, 0:TD], Wrkv[:, kc, j * 1024 + dc * 128:j * 1024 + (dc + 1) * 128], xm[b][j][:, kc, :],
                     kc == 0, kc == 7, [('xm', b, j, kc)] + wr(j, kc), [('ps', pp)])
        R.cp('act', t['rT'], k.ps[pr][:, 0:TD], [('ps', pr)], [K_('rT')])
        R.cp('act', t['kT'], k.ps[pk][:, 0:TD], [('ps', pk)], [K_('kT')])
        R.cp('act', bb['vTb'], k.ps[pv][:, 0:TD], [('ps', pv)], [K_('vTb')])
        for z in range(2):
            pw = bank(par)
            R.mm(k.ps[pw][:, 0:TD], W2p[:, z, dsl], hw[b], True, True, [('hw', b), ('W2p', z, 0)], [('ps', pw)])
            R.act(t['sig%d' % z], k.ps[pw][:, 0:TD], AF.Sigmoid, [('ps', pw)], [K_('sig%d' % z)],
                  bias=vcol(k, V_W0 + z, dc))
            pa = bank(par)
            R.mm(k.ps[pa][:, 0:TD], A2p[:, z, dsl], ha[b], True, True, [('ha', b), ('A2p', z, 0)], [('ps', pa)])
            R.act(t['a%d' % z], k.ps[pa][:, 0:TD], AF.Sigmoid, [('ps', pa)], [K_('a%d' % z)],
                  bias=vcol(k, V_A0 + z, dc))
        pg = bank(par)
        for hf in range(2):
            R.mm(k.ps[pg][:, 0:TD], G2[:, hf, dsl], hg[b][:, hf, :], hf == 0, hf == 1,
                 [('hg', b, hf), ('G2', hf, 0)], [('ps', pg)])
        R.cp('act', bb['gT'], k.ps[pg][:, 0:TD], [('ps', pg)], [K_('gT')])
        R.dma('sp', d['GT'][dsl, cols], bb['gT'], 'd_g%d' % par, [K_('gT')], [])
        R.scp('act', t['tk'], t['kT'], vcol(k, V_KK, dc), [K_('kT')], [K_('tk')])
        for z in range(2):
            R.add('act', (lambda o_, i_, s_, b_: (lambda e: e.activation(out=o_, in_=i_, func=AF.Identity,
                                                                         bias=b_, scale=s_)))(
                t['tK%d' % z], t['a%d' % z], vcol(k, V_KA, dc), vcol(k, V_OMKA, dc)),
                [K_('a%d' % z)], [K_('tK%d' % z)])

    def back(R, i, dc, par):
        t, bb = T[par], B[par]
        K_ = lambda n: (n, par)
        dsl = slice(dc * 128, (dc + 1) * 128)
        cols = slice(i * TD, (i + 1) * TD)
        R.tt('pool', bb['sqk'], t['tk'], t['tk'], ALU.mult, [K_('tk')], [K_('sqk')])
        pn = bank(par)
        R.mm(k.ps[pn][:, 0:TD], k.onesbd_bf, bb['sqk'], True, True, [K_('sqk')], [('ps', pn)])
        R.ts1('dve', t['ssm'], k.ps[pn][:, 0:TD], 1e-24, ALU.max, [('ps', pn)], [K_('ssm')])
        R.act(t['lnv'], t['ssm'], AF.Ln, [K_('ssm')], [K_('lnv')])
        R.act(t['rin'], t['lnv'], AF.Exp, [K_('lnv')], [K_('rin')], scale=-0.5)
        R.tt('pool', t['kk'], t['tk'], t['rin'], ALU.mult, [K_('tk'), K_('rin')], [K_('kk')])
        for z in range(2):
            R.tt('pool', t['kd%d' % z], t['kT'], t['tK%d' % z], ALU.mult, [K_('kT'), K_('tK%d' % z)], [K_('kd%d' % z)])
            R.tt('pool', t['b%d' % z], t['kk'], t['a%d' % z], ALU.mult, [K_('kk'), K_('a%d' % z)], [K_('b%d' % z)])
        R.tt('pool', t['cs'], t['kd0'], t['kd1'], ALU.add, [K_('kd0'), K_('kd1')], [K_('cs')])
        R.stt(bb['cr'], t['cs'], vcol(k, V_RK, dc), t['rT'], ALU.mult, ALU.mult, [K_('cs'), K_('rT')], [K_('cr')])
        pc_ = bank(par)
        R.mm(k.ps[pc_][:, 0:TD], k.onesbd_bf, bb['cr'], True, True, [K_('cr')], [('ps', pc_)])
        R.tt('dve', bb['cv'], k.ps[pc_][:, 0:TD], bb['vTb'], ALU.mult, [('ps', pc_), K_('vTb')], [K_('cv')])
        R.dma('sp', d['CVT'][dsl, cols], bb['cv'], 'd_cv%d' % par, [K_('cv')], [])
        to_tm(R, bb['vTb'], K_('vTb'), 'vt', par, d['VT'], i, dc)
        for z in range(2):
            sg_ = t['sig%d' % z]
            sk = K_('sig%d' % z)
            cum3 = t['cum'].rearrange("p (n t) -> p n t", t=128)
            if z == 0:
                R.scan(t['cum'], sm, sg_, [sk, 'smask'], [K_('cum')])
                tot = cum3[:, :, 127:128]
            else:
                R.scan(t['pre'], sm, sg_, [sk, 'smask'], [K_('pre')])
                pre3 = t['pre'].rearrange("p (n t) -> p n t", t=128)
                R.tt('dve', t['cum'], sg_, t['pre'], ALU.subtract, [sk, K_('pre')], [K_('cum')])
                R.tt('dve', cum3, cum3, pre3[:, :, 127:128].to_broadcast([128, NSUB, 128]), ALU.add,
                     [K_('cum'), K_('pre')], [K_('cum')])
                tot = cum3[:, :, 0:1]
            R.tt('pool', t['cume'], t['cum'], sg_, ALU.subtract, [K_('cum'), sk], [K_('cume')])
            R.act(t['E1'], t['cum'], AF.Exp, [K_('cum')], [K_('E1')], scale=-CDEC)
            R.act(t['E2'], t['cume'], AF.Exp, [K_('cume')], [K_('E2')], scale=-CDEC)
            R.act(t['E3'], t['cum'], AF.Exp, [K_('cum')], [K_('E3')], scale=CDEC)
            R.act(pc[par].unsqueeze(2), tot, AF.Exp, [K_('cum')], [K_('pc')], scale=-CDEC)
            R.dma('sp', d['PC%d' % z][dsl, i * NSUB:i * NSUB + NSUB], pc[par], 'd_pc%d' % par, [K_('pc')], [])
            R.tt('pool', bb['o_r'], t['rT'], t['E1'], ALU.mult, [K_('rT'), K_('E1')], [K_('o_r')])
            R.dma('sp', d['RT%d' % z][dsl, cols], bb['o_r'], 'd_r%d' % par, [K_('o_r')], [])
            R.tt('pool', bb['o_kk'], t['kk'], t['E2'], ALU.mult, [K_('kk'), K_('E2')], [K_('o_kk')])
            R.dma('sp', d['KKT%d' % z][dsl, cols], bb['o_kk'], 'd_kk%d' % par, [K_('o_kk')], [])
            R.tt('dve', bb['o_b'], t['b%d' % z], t['E3'], ALU.mult, [K_('b%d' % z), K_('E3')], [K_('o_b')])
            R.dma('sp', d['BT%d' % z][dsl, cols], bb['o_b'], 'd_b%d' % par, [K_('o_b')], [])
            to_tm(R, bb['o_b'], K_('o_b'), 'bh', par, d['BH%d' % z], i, dc)
            R.tt('dve', bb['o_k'], t['kd%d' % z], t['E3'], ALU.mult, [K_('kd%d' % z), K_('E3')], [K_('o_k')])
            R.dma('sp', d['KT2%d' % z][dsl, cols], bb['o_k'], 'd_k%d' % par, [K_('o_k')], [])
            to_tm(R, bb['o_k'], K_('o_k'), 'kh', par, d['KH%d' % z], i, dc)

    RD = Defer()

    def loads_d(i):
        t0 = i * TD
        base = 1 + t0 if t0 < cfg.TP else cfg.TP + 3 + (t0 - cfg.TP)
        RD.dma('sp', xh[i % 2], XH[:, :, base - 1:base + TD + 1], 'dxh%d' % (i % 2), [], [('xh', i % 2)])

    def hidden_d(i):
        nonlocal R
        R_save = R
        R = RD
        try:
            hidden(i)
        finally:
            R = R_save

    loads_d(0)
    for i in range(NTD):
        if i + 1 < NTD:
            loads_d(i + 1)
        for kc in range(8):
            mix_kc(RD, i, kc)
        hidden_d(i)
        for dc in range(8):
            front(RD, i, dc, dc % 2)
            back(RD, i, dc, dc % 2)
    list_schedule(R, RD.q)
    R.barrier()


def phase_E(k):
    R, ar, cfg, d = k.R, k.ar, k.cfg, k.d
    ar.reset()
    masks = ar.alloc([7, 512], BF16)
    NS = 3
    P = [dict(kkP=ar.alloc([8, 128], BF16), rP=ar.alloc([8, 128], BF16), bP=ar.alloc([8, 128], BF16),
              kP=ar.alloc([8, 128], BF16), KR=ar.alloc([8, 4, 128], BF16), bB=ar.alloc([8, 2, 128], BF16),
              bh=ar.alloc([1024], BF16), kh=ar.alloc([1024], BF16), v=ar.alloc([1024], BF16),
              pc=ar.alloc([8], F32)) for _ in range(NS)]
    ATb = [ar.alloc([8, 512], BF16) for _ in range(2)]
    ATk = [ar.alloc([8, 512], BF16) for _ in range(2)]
    Mb = [[[ar.alloc([4, 128], BF16) for _ in range(2)] for _ in range(4)] for _ in range(2)]
    Nb = [[[ar.alloc([4, 128], BF16) for _ in range(2)] for _ in range(4)] for _ in range(2)]
    Zb = [[[ar.alloc([4, 128], BF16) for _ in range(2)] for _ in range(4)] for _ in range(2)]
    Hf = ar.alloc([8, 128], F32)
    Hb = ar.alloc([8, 128], BF16)
    Ht = ar.alloc([8, 64], F32)
    Gn = ar.alloc([8, 128], BF16)
    Ub = ar.alloc([8, 128], BF16)
    Ys = [ar.alloc([1024], F32) for _ in range(2)]
    xg = ar.alloc([2, 1024], F32)
    xgf = xg.rearrange("p a b -> p (a b)")
    for r0 in (0, 4):
        n_ = min(4, 7 - r0)
        stv = xgf[:, 0:n_ * 512].rearrange("p (a b) -> p a b", b=512)
        R.dma('sp', stv, d['masks'][:, r0:r0 + n_, :], 'em', [], ['x2g'])
        R.cp('dve', masks[:, r0:r0 + n_, :], stv, ['x2g'], ['masks'])
    for s_ in range(NS):
        R.memset('dve', P[s_]['KR'], 0.0, [('KR', s_)])
        R.memset('dve', P[s_]['bB'], 0.0, [('bB', s_)])
    cnt = [0]
    cur = [[0, 0, 0, 0], [0, 0, 0, 0]]
    fl = lambda a: a.rearrange("p a b -> p (a b)")

    def bank():
        cnt[0] += 1
        return cnt[0] % 8

    def loads(z, tok0, s_):
        p = P[s_]
        c = slice(tok0, tok0 + 128)
        sfx = '%d' % s_
        R.dma('sp', p['kkP'], fm(d['KKT%d' % z])[:, :, c], 'e_kk' + sfx, [], [('kkP', s_)])
        R.dma('sp', p['bP'], fm(d['BT%d' % z])[:, :, c], 'e_b' + sfx, [], [('bP', s_)])
        R.dma('sp', p['kP'], fm(d['KT2%d' % z])[:, :, c], 'e_k' + sfx, [], [('kP', s_)])
        for hh in range(2):
            rows = slice(hh * 64, hh * 64 + 64)
            R.dma('sp', p['KR'][rows, :, hh, :], fm(d['KKT%d' % z])[rows, :, c], 'e_KR' + sfx, [], [('KR', s_)])
            R.dma('sp', p['KR'][rows, :, 2 + hh, :], fm(d['RT%d' % z])[rows, :, c], 'e_KR' + sfx, [], [('KR', s_)])
            R.dma('sp', p['bB'][rows, :, hh, :], fm(d['BT%d' % z])[rows, :, c], 'e_bB' + sfx, [], [('bB', s_)])
        R.dma('sp', p['rP'], fm(d['RT%d' % z])[:, :, c], 'e_r' + sfx, [], [('rP', s_)])
        R.dma('sp', p['bh'], d['BH%d' % z][c, :], 'e_bh' + sfx, [], [('bh', s_)])
        R.dma('sp', p['kh'], d['KH%d' % z][c, :], 'e_kh' + sfx, [], [('kh', s_)])
        R.dma('sp', p['v'], d['VT'][c, :], 'e_v' + sfx, [], [('v', s_)])
        n = tok0 // 128
        R.dma('sp', p['pc'].unsqueeze(2), fm(d['PC%d' % z])[:, :, n:n + 1], 'e_pc' + sfx, [], [('pc', s_)], slow=True)

    def pre_stages(z, s_, par):
        p = P[s_]
        atb, atk = ATb[par], ATk[par]
        M_, N_, Z_ = Mb[par], Nb[par], Zb[par]
        cu = cur[par]
        mB, mK, mM, mI = masks[:, 3 * z, :], masks[:, 3 * z + 1, :], masks[:, 3 * z + 2, :], masks[:, 6, :]

        def level(lev):
            info = []
            for g in range(4):
                c0, c1 = cu[g], 1 - cu[g]
                Mp, Np = M_[g][c0], N_[g][c0]
                bn = None
                if lev < 6:
                    bn = bank()
                    for x in range(4):
                        R.mm(k.ps[bn][:, x * 128:(x + 1) * 128], Mp[:, x, :], Np[:, x, :], True, True,
                             [('M', par, g, c0), ('N', par, g, c0)], [('ps', bn)])
                bm = bank()
                for x in range(4):
                    R.mm(k.ps[bm][:, x * 128:(x + 1) * 128], Np[:, x, :], Mp[:, x, :], True, True,
                         [('M', par, g, c0), ('N', par, g, c0)], [('ps', bm)])
                info.append((c0, c1, bn, bm))
            for g in range(4):
                c0, c1, bn, bm = info[g]
                R.cp('act', fl(M_[g][c1]), k.ps[bm], [('ps', bm)], [('M', par, g, c1)])
                if bn is not None:
                    R.cp('dve' if g == 0 else 'act', fl(N_[g][c1]), k.ps[bn], [('ps', bn)], [('N', par, g, c1)])
            bzs = []
            for g in range(4):
                c0, c1, bn, bm = info[g]
                bz = bank()
                bzs.append(bz)
                for x in range(4):
                    R.mm(k.ps[bz][:, x * 128:(x + 1) * 128], M_[g][c1][:, x, :], Z_[g][c0][:, x, :], True, True,
                         [('M', par, g, c1), ('Z', par, g, c0)], [('ps', bz)])
            for g in range(4):
                c0, c1, bn, bm = info[g]
                R.tt('dve', fl(Z_[g][c1]), k.ps[bzs[g]], fl(Z_[g][c0]), ALU.add,
                     [('ps', bzs[g]), ('Z', par, g, c0)], [('Z', par, g, c1)])
                cu[g] = c1

        def stA():
            for dc in range(8):
                rhs = fl(p['KR'][:, dc, :, :])
                b1 = bank()
                R.mm(k.ps[b1], p['bP'][:, dc, :], rhs, True, True, [('bP', s_), ('KR', s_)], [('ps', b1)])
                b2 = bank()
                R.mm(k.ps[b2], p['kP'][:, dc, :], rhs, True, True, [('kP', s_), ('KR', s_)], [('ps', b2)])
                R.tt('dve', atb[:, dc, :], k.ps[b1], mB, ALU.mult, [('ps', b1), 'masks'], [('ATb', par, dc)])
                R.tt('dve', atk[:, dc, :], k.ps[b2], mK, ALU.mult, [('ps', b2), 'masks'], [('ATk', par, dc)])
            for g in range(4):
                cu[g] = 0
                b1 = bank()
                for dd in range(2):
                    dc = 2 * g + dd
                    R.mm(k.ps[b1][:, dd * 256:(dd + 1) * 256], p['kkP'][:, dc, :], fl(p['bB'][:, dc, :, :]), True, True,
                         [('kkP', s_), ('bB', s_)], [('ps', b1)])
                R.tt('dve', fl(M_[g][0]), k.ps[b1], mM, ALU.mult, [('ps', b1), 'masks'], [('M', par, g, 0)])
                n0 = atb[:, 2 * g:2 * g + 2, 0:256]
                nk_ = [('ATb', par, 2 * g), ('ATb', par, 2 * g + 1)]
                R.cp('pool', N_[g][0].rearrange("p (a c) b -> p a (c b)", c=2), n0, nk_, [('N', par, g, 0)])
                R.tt('pool', Z_[g][0].rearrange("p (a c) b -> p a (c b)", c=2), n0,
                     mI.rearrange("p (a b) -> p a b", b=256), ALU.add, nk_ + ['masks'], [('Z', par, g, 0)])
            level(1)
            level(2)

        def stB():
            level(3)

        def stC():
            level(4)

        def stD():
            level(5)
            level(6)

        return [stA, stB, stC, stD]

    step = [0]

    def seq_stages(z, tok0, s_, par):
        p = P[s_]
        atb, atk = ATb[par], ATk[par]
        TT = lambda h: Zb[par][h // 4][cur[par][h // 4]][:, h % 4, :]
        tkey = lambda h: ('Z', par, h // 4, cur[par][h // 4])
        st = {}

        def s1():
            for half in range(2):
                dcs = range(4 * half, 4 * half + 4)
                bg = bank()
                for dl, dc in enumerate(dcs):
                    o = dl * 128
                    R.mm(k.ps[bg][:, o:o + 128], p['kkP'][:, dc, :], Hb[:, dc, :], True, False,
                         [('kkP', s_), 'Hb'], [('ps', bg)])
                    for hh in range(2):
                        R.mm(k.ps[bg][:, o + hh * 64:o + hh * 64 + 64], atk[:, dc, hh * 128:(hh + 1) * 128],
                             p['v'][:, dc * 128 + hh * 64:dc * 128 + hh * 64 + 64], False, hh == 1,
                             [('ATk', par, dc), ('v', s_)], [('ps', bg)])
                R.act(fl(Gn[:, 4 * half:4 * half + 4, :]), k.ps[bg], AF.Copy, [('ps', bg)], [('Gn', half)], scale=-1.0)

        def s2():
            for half in range(2):
                dcs = range(4 * half, 4 * half + 4)
                bu = bank()
                for dl, dc in enumerate(dcs):
                    for hh in range(2):
                        h = 2 * dc + hh
                        o = dl * 128 + hh * 64
                        R.mm(k.ps[bu][:, o:o + 64], TT(h), Gn[:, dc, hh * 64:hh * 64 + 64], True, True,
                             [tkey(h), ('Gn', half)], [('ps', bu)])
                R.cp('dve', fl(Ub[:, 4 * half:4 * half + 4, :]), k.ps[bu], [('ps', bu)], [('Ub', half)])

        def s3():
            ys = Ys[step[0] % 2]
            yk = ('Ys', step[0] % 2)
            ysem = 'e_y%d' % (step[0] % 2)
            step[0] += 1
            hbanks = []
            for half in range(2):
                dcs = range(4 * half, 4 * half + 4)
                by = bank()
                for dl, dc in enumerate(dcs):
                    o = dl * 128
                    R.mm(k.ps[by][:, o:o + 128], p['rP'][:, dc, :], Hb[:, dc, :], True, False,
                         [('rP', s_), 'Hb'], [('ps', by)])
                    for hh in range(2):
                        oo = o + hh * 64
                        R.mm(k.ps[by][:, oo:oo + 64], atb[:, dc, 256 + hh * 128:256 + (hh + 1) * 128],
                             Ub[:, dc, hh * 64:hh * 64 + 64], False, False, [('ATb', par, dc), ('Ub', half)],
                             [('ps', by)])
                        R.mm(k.ps[by][:, oo:oo + 64], atk[:, dc, 256 + hh * 128:256 + (hh + 1) * 128],
                             p['v'][:, dc * 128 + hh * 64:dc * 128 + hh * 64 + 64], False, hh == 1,
                             [('ATk', par, dc), ('v', s_)], [('ps', by)])
                R.cp('act', ys[:, half * 512:(half + 1) * 512], k.ps[by], [('ps', by)], [yk])
                bh_ = bank()
                hbanks.append(bh_)
                for dl, dc in enumerate(dcs):
                    o = dl * 128
                    R.mm(k.ps[bh_][:, o:o + 128], p['bh'][:, dc * 128:(dc + 1) * 128], Ub[:, dc, :], True, False,
                         [('bh', s_), ('Ub', half)], [('ps', bh_)])
                    R.mm(k.ps[bh_][:, o:o + 128], p['kh'][:, dc * 128:(dc + 1) * 128],
                         p['v'][:, dc * 128:(dc + 1) * 128], False, True, [('kh', s_), ('v', s_)], [('ps', bh_)])
            for half in range(2):
                ps3 = k.ps[hbanks[half]].rearrange("p (a b) -> p a b", b=128)
                for hh in range(2):
                    rows = slice(hh * 64, hh * 64 + 64)
                    csl = slice(hh * 64, hh * 64 + 64)
                    hblk = Hf[rows, 4 * half:4 * half + 4, csl]
                    tblk = Ht[rows, 4 * half:4 * half + 4, :]
                    R.tt('dve', tblk, hblk, ps3[rows, :, csl], ALU.add, [('Hf', half, hh), ('ps', hbanks[half])],
                         [('Ht', half, hh)])
                    R.tt('pool', hblk, tblk,
                         p['pc'][rows, 4 * half:4 * half + 4].unsqueeze(2).to_broadcast([64, 4, 64]),
                         ALU.mult, [('Ht', half, hh), ('pc', s_)], [('Hf', half, hh)])
            hk = [('Hf', a, b) for a in range(2) for b in range(2)]
            R.cp('act', Hb[0:64, :, 0:64], Hf[0:64, :, 0:64], hk, ['Hb'])
            R.cp('act', Hb[64:128, :, 64:128], Hf[64:128, :, 64:128], hk, ['Hb'])
            R.dma('pool', d['Y%d' % z][tok0:tok0 + 128, :], ys, ysem, [yk], [])

        return [s1, s2, s3]

    slot = [0]
    parc = [0]
    HK = [('Hf', a, b) for a in range(2) for b in range(2)]

    def run(seq, z, init):
        nonlocal R
        t0, T_ = (0, cfg.TP) if seq == 0 else (cfg.TP, cfg.TS)
        nch = T_ // 128
        order = list(range(nch)) if z == 0 else list(range(nch - 1, -1, -1))
        toks = [t0 + n * 128 for n in order]
        if init is None:
            R.memset('dve', Hf, 0.0, HK)
            R.memset('pool', Hb, 0.0, ['Hb'])
        else:
            init()
        s0 = slot[0]
        p0 = parc[0]
        slot[0] += nch
        parc[0] += nch
        R_real = R
        R = Defer()
        try:
            emit_run(z, toks, nch, s0, p0)
        finally:
            ops_ = R.q
            R = R_real
        list_schedule(R, ops_)

    def emit_run(z, toks, nch, s0, p0):
        loads(z, toks[0], s0 % NS)
        if nch > 1:
            loads(z, toks[1], (s0 + 1) % NS)
        for stg_ in pre_stages(z, s0 % NS, p0 % 2):
            stg_()
        for ci in range(nch):
            if ci + 2 < nch:
                loads(z, toks[ci + 2], (s0 + ci + 2) % NS)
            sq_ = seq_stages(z, toks[ci], (s0 + ci) % NS, (p0 + ci) % 2)
            if ci + 1 < nch:
                nx = pre_stages(z, (s0 + ci + 1) % NS, (p0 + ci + 1) % 2)
            else:
                nx = [lambda: None] * 4
            nx[0]()
            sq_[0]()
            nx[1]()
            sq_[1]()
            nx[2]()
            sq_[2]()
            nx[3]()

    def publish():
        R.dma('pool', d['ex2_in'], fl(Hf), 'x2p', HK, ['ex2_in'])
        add_cc(R, lambda e: e.collective_compute("AllGather", ALU.bypass, replica_groups=PAIRS,
                                                 ins=[d['ex2_in']], outs=[d['ex2_out']]), ['ex2_in'], ['ex2o'], 'cc2')

    def init_from_partner():
        R.dma('sp', xg, d['ex2_out'].rearrange("(s p) c -> p s c", p=128), 'x2g', ['ex2o'], ['x2g'])
        hfl = fl(Hf)
        R.scp('dve', hfl, xg[:, 0, :], k.sel[:, 0:1], ['x2g'], HK)
        R.stt(hfl, xg[:, 1, :], k.sel[:, 1:2], hfl, ALU.mult, ALU.add, ['x2g'] + HK, HK)
        R.cp('act', Hb, Hf, HK, ['Hb'])

    run(1, 0, None)
    publish()
    run(0, 0, None)
    run(0, 1, None)
    run(1, 1, init_from_partner)
    R.barrier()


def phase_F(k):
    R, ar, cfg, d = k.R, k.ar, k.cfg, k.d
    ar.reset()
    Wo = ar.alloc([8, 1024], BF16)
    k.stg = [ar.alloc([1024], F32) for _ in range(4)]
    K.stgw = k.stgw = 1024
    load_w(k, Wo, d['w_out'], 8, 1024, 'WoR')
    y0s = [ar.alloc([4, 1024], F32) for _ in range(2)]
    y1s = [ar.alloc([4, 1024], F32) for _ in range(2)]
    ynb = ar.alloc([4, 1024], BF16)
    cvs = [ar.alloc([8, 512], BF16) for _ in range(2)]
    gts = [ar.alloc([8, 512], BF16) for _ in range(2)]
    zT = ar.alloc([8, 512], BF16)
    xTs = [ar.alloc([8, 512], F32) for _ in range(2)]
    s1 = ar.alloc([64], F32)
    s2 = ar.alloc([64], F32)
    mean = ar.alloc([64], F32)
    msq = ar.alloc([64], F32)
    var = ar.alloc([64], F32)
    rstd = ar.alloc([64], F32)
    yb = [ar.alloc([512], F32) for _ in range(2)]
    XT = fm(d['XT'])
    def floads(i):
        b = i % 2
        cols = slice(i * 512, (i + 1) * 512)
        R.dma('sp', y0s[b], tm(d['Y0'])[:, 4 * i:4 * i + 4, :], 'f_y0%d' % b, [], [('y0', b)])
        R.dma('sp', y1s[b], tm(d['Y1'])[:, 4 * i:4 * i + 4, :], 'f_y1%d' % b, [], [('y1', b)])
        R.dma('sp', cvs[b], fm(d['CVT'])[:, :, cols], 'f_cv%d' % b, [], [('cv', b)])
        R.dma('sp', gts[b], fm(d['GT'])[:, :, cols], 'f_g%d' % b, [], [('gt', b)])
        R.dma('sp', xTs[b], XT[:, :, cols], 'f_x%d' % b, [], [('xT', b, dc) for dc in range(8)])

    R_real = R
    R = k.R = Defer()
    floads(0)
    for i in range(cfg.NT_OWN):
        cols = slice(i * 512, (i + 1) * 512)
        pb_ = i % 2
        if i + 1 < cfg.NT_OWN:
            floads(i + 1)
        y0, y1, cv, gt, xT = y0s[pb_], y1s[pb_], cvs[pb_], gts[pb_], xTs[pb_]
        y0f = y0.rearrange("p a b -> p (a b)")
        y1f = y1.rearrange("p a b -> p (a b)")
        R.tt('pool', y0f, y0f, y1f, ALU.add, [('y0', pb_), ('y1', pb_)], [('y0', pb_)])
        y3 = y0.rearrange("p a (h n) -> p (a h) n", n=64)
        q3 = y1.rearrange("p a (h n) -> p (a h) n", n=64)
        R.red(s1, y3, ALU.add, [('y0', pb_)], ['s1'])
        R.tt('pool', y1f, y0f, y0f, ALU.mult, [('y0', pb_), ('y1', pb_)], [('y1', pb_)])
        R.red(s2, q3, ALU.add, [('y1', pb_)], ['s2'])
        R.ts1('dve', mean, s1, 1.0 / 64, ALU.mult, ['s1'], ['mean'])
        R.tt('dve', msq, mean, mean, ALU.mult, ['mean'], ['msq'])
        R.stt(var, s2, 1.0 / 64, msq, ALU.mult, ALU.subtract, ['s2', 'msq'], ['var'])
        R.act(var, var, AF.Sqrt, ['var'], ['var'], bias=64e-5, scale=1.0)
        R.recip(rstd, var, ['var'], ['rstd'])
        R.tt('dve', y3, y3, mean.unsqueeze(2).to_broadcast([128, 64, 64]), ALU.subtract, [('y0', pb_), 'mean'], [('y0', pb_)])
        R.tt('dve', ynb.rearrange("p a (h n) -> p (a h) n", n=64), y3,
             rstd.unsqueeze(2).to_broadcast([128, 64, 64]), ALU.mult, [('y0', pb_), 'rstd'], ['ynb'])
        for dc in range(8):
            pb = dc % 2
            for n in range(4):
                R.tr(k.psb[pb][:, n * 128:(n + 1) * 128], ynb[:, n, dc * 128:(dc + 1) * 128], k.ident_bf,
                     ['ynb'], [('ps', pb)])
            R.act(yb[dc % 2], k.psb[pb][:, 0:512], AF.Identity, [('ps', pb)], [('yb', dc % 2)],
                  bias=vcol(k, V_LNB, dc), scale=vcol(k, V_LNW, dc))
            R.tt('dve', yb[dc % 2], yb[dc % 2], cv[:, dc, :], ALU.add, [('yb', dc % 2), ('cv', pb_)], [('yb', dc % 2)])
            R.tt('pool', zT[:, dc, :], yb[dc % 2], gt[:, dc, :], ALU.mult, [('yb', dc % 2), ('gt', pb_)], [('zT', dc)])
        for dco in range(8):
            pb = 2 + dco % 4
            for dc in range(8):
                R.mm(k.ps[pb], Wo[:, dc, dco * 128:(dco + 1) * 128], zT[:, dc, :], dc == 0, dc == 7,
                     [('zT', dc), ('WoR', dc, 0)], [('ps', pb)])
            R.tt('dve', xT[:, dco, :], k.ps[pb], xT[:, dco, :], ALU.add, [('ps', pb), ('xT', pb_, dco)], [('xT', pb_, dco)])
        R.dma('pool', XT[:, :, cols], xT, 'f_x%d' % pb_, [('xT', pb_, dc) for dc in range(8)], [])
    ops_ = R.q
    R = k.R = R_real
    list_schedule(R, ops_)
    R.barrier()


def EXTRA_PHASES(k):
    return [('X1', lambda: phase_X1(k)), ('D', lambda: phase_D(k)), ('E', lambda: phase_E(k)),
            ('F', lambda: phase_F(k)), ('G', lambda: phase_FFN(k, 1))]


def build(cfg, upto='all', dbg=()):
    nc = bass.Bass("TRN2", target_bir_lowering=False)
    k = K()
    k.cfg = cfg
    k.nc = nc
    d = {}

    def inp(name, shape, dt=F32):
        d[name] = nc.dram_tensor(name, list(shape), dt, kind="ExternalInput").ap()

    def scr(name, shape, dt):
        kind = "ExternalOutput" if name in dbg else "Internal"
        d[name] = nc.dram_tensor(name, list(shape), dt, kind=kind).ap()

    inp('x_in', [cfg.NALL, D])
    inp('ropeC', [128, cfg.NALL])
    inp('ropeS', [128, cfg.NALL])
    inp('cmat', [128, 3, 128])
    inp('vecs', [128, NV - 1, 8])
    inp('smask', [128, 512])
    inp('masks', [128, 7, 512])
    inp('sel', [128, 2])
    inp('sublnw', [128, 1])
    inp('lamt', [64, 4])
    inp('w_qkv', [D, 3072])
    inp('w_qks', [D, 2048])
    inp('w_o', [D, D])
    inp('w_ffn_in', [2, D, 2 * FH])
    inp('w_ffn_out', [2, FH, D])
    inp('w_rkv', [3, D, D])
    inp('w1s', [D, 128])
    inp('w2p', [256, D])
    inp('a1s', [D, 128])
    inp('a2p', [256, D])
    inp('g1p', [D, 256])
    inp('g2p', [256, D])
    inp('w_out', [D, D])
    d['y_out'] = nc.dram_tensor('y_out', [cfg.NOWN, D], F32, kind="ExternalOutput").ap()
    scr('XT', [D, cfg.NOWN], F32)
    scr('QT', [D, cfg.NOWN], BF16)
    scr('KT', [D, cfg.NALL], BF16)
    scr('VV', [cfg.NALL, D], BF16)
    scr('OT', [D, cfg.NOWN], BF16)
    scr('XH', [D, cfg.XHW], BF16)
    NCH = cfg.NOWN // 128
    for z in range(2):
        for nm_ in ('RT', 'KKT', 'BT', 'KT2'):
            scr('%s%d' % (nm_, z), [D, cfg.NOWN], BF16)
        for nm_ in ('BH', 'KH'):
            scr('%s%d' % (nm_, z), [cfg.NOWN, D], BF16)
        scr('PC%d' % z, [D, NCH], F32)
        scr('Y%d' % z, [cfg.NOWN, D], F32)
    scr('VT', [cfg.NOWN, D], BF16)
    scr('GT', [D, cfg.NOWN], BF16)
    scr('CVT', [D, cfg.NOWN], BF16)
    d['ex2_in'] = nc.dram_tensor('ex2_in', [128, 1024], F32, kind="Internal").ap()
    d['ex2_out'] = nc.dram_tensor('ex2_out', [256, 1024], F32, kind="Internal").ap()
    d['ex1_in'] = nc.dram_tensor('ex1_in', [128, 8], F32, kind="Internal").ap()
    d['ex1_out'] = nc.dram_tensor('ex1_out', [256, 8], F32, kind="Internal").ap()
    k.d = d

    with ExitStack() as stack:
        sb = nc.alloc_sbuf_tensor("sball", [128, 206 * 1024 // 4], F32)
        k.ar = Arena(sb, 206 * 1024)
        k.pd = [nc.alloc_psum_tensor("pd%d" % i, [128, 1024], F32)[:, :] for i in range(4)]
        k.ps = [k.pd[i // 2][:, (i % 2) * 512:(i % 2) * 512 + 512] for i in range(8)]
        k.psb = [p.bitcast(BF16) for p in k.ps]
        R = Rec(nc)
        k.R = R
        phases = [('consts', lambda: phase_consts(k)), ('A', lambda: phase_A(k)), ('B', lambda: phase_B(k)),
                  ('C1', lambda: phase_C1(k)), ('C2', lambda: phase_FFN(k, 0))]
        extra = globals().get('EXTRA_PHASES')
        if extra:
            phases += extra(k)
        for name, fn in phases:
            fn()
            if name == upto:
                break
        R.barrier()
        sems = {}
        for i, ref in enumerate(sorted(R.semrefs, key=str)):
            sems[ref] = stack.enter_context(nc.semaphore("s%d" % i))
        k.n_instr = {e: len(R.lists[e]) for e in ENGS}
        block = stack.enter_context(nc.Block())

        def emit(eng):
            def body(e):
                for it in R.lists[eng]:
                    if it[0] == 'w':
                        e.wait_ge(sems[it[1]], it[2])
                    else:
                        ins = it[1](e)
                        if it[3] == 1:
                            ins.then_inc(sems[it[2]], 1)
                        elif it[3] == 16:
                            ins.then_inc(sems[it[2]], 16)
                        else:
                            ins.then_inc(sems[it[2]])
            return body

        block.tensor(emit('pe'))
        block.scalar(emit('act'))
        block.vector(emit('dve'))
        block.gpsimd(emit('pool'))
        block.sync(emit('sp'))
    return nc, k


def _fmvec(v):
    return np.ascontiguousarray(np.asarray(v, np.float32).reshape(8, 128).T)


def _tri_masks():
    s = np.arange(128)[:, None]
    t = np.arange(128)[None, :]
    su = (s < t).astype(np.float32)
    iu = (s <= t).astype(np.float32)
    sl = (s > t).astype(np.float32)
    il = (s >= t).astype(np.float32)
    I = np.eye(128, dtype=np.float32)
    m = np.zeros((128, 7, 512), np.float32)
    m[:, 0] = np.concatenate([-su, -su, iu, iu], 1)
    m[:, 1] = np.concatenate([su, su, iu, iu], 1)
    m[:, 2] = np.concatenate([-sl] * 4, 1)
    m[:, 3] = np.concatenate([-sl, -sl, il, il], 1)
    m[:, 4] = np.concatenate([sl, sl, il, il], 1)
    m[:, 5] = np.concatenate([-su] * 4, 1)
    m[:, 6] = np.concatenate([I] * 4, 1)
    return m


def _padz(a, b):
    o = np.zeros((256, D), np.float32)
    o[0:64] = a
    o[128 + 64:256] = b
    return o


def prep_inputs(inp, cfg, ncores=8):
    f32 = np.float32
    TP, TS = cfg.TP, cfg.TS
    inv_freq = (f32(ROPE_THETA) ** (-np.arange(0, 16, 2, dtype=f32) / f32(16))).astype(f32)
    wqkv = np.asarray(inp['attn_w_qkv'][0], f32)
    wqks = np.zeros((D, 2048), f32)
    for which in range(2):
        for blk in range(16):
            b0 = which * 1024 + blk * 64
            wqks[:, b0:b0 + 8] = wqkv[:, b0 + 8:b0 + 16]
            wqks[:, b0 + 8:b0 + 16] = wqkv[:, b0:b0 + 8]
    cm = np.zeros((128, 3, 128), f32)
    cm[:, 0] = np.eye(128, dtype=f32)
    cm[:, 1] = 1.0
    cm[0:64, 2, 0:64] = 1.0
    cm[64:128, 2, 64:128] = 1.0
    smask = np.ones((128, 512), f32)
    smask[:, 0::128] = 0.0
    masks = _tri_masks()
    lamt = np.stack([np.asarray(inp['attn_lambda_q1'][0], f32), np.asarray(inp['attn_lambda_k1'][0], f32),
                     np.asarray(inp['attn_lambda_q2'][0], f32), np.asarray(inp['attn_lambda_k2'][0], f32)], 1)
    sublnw = np.asarray(inp['attn_subln_w'][0], f32).reshape(128, 1)
    g1p = np.zeros((D, 256), f32)
    g1p[:, :160] = np.asarray(inp['rwkv_g1'][0], f32)
    g2p = np.zeros((256, D), f32)
    g2p[:160] = np.asarray(inp['rwkv_g2'][0], f32)
    shared = {
        'cmat': cm, 'smask': smask, 'masks': masks, 'lamt': np.ascontiguousarray(lamt), 'sublnw': sublnw,
        'w_qkv': wqkv, 'w_qks': wqks, 'w_o': np.asarray(inp['attn_w_o'][0], f32),
        'w_ffn_in': np.asarray(inp['ffn_w_in'], f32), 'w_ffn_out': np.asarray(inp['ffn_w_out'], f32),
        'w_rkv': np.asarray(inp['rwkv_w_rkv'][0], f32),
        'g1p': g1p, 'g2p': g2p,
        'w_out': np.asarray(inp['rwkv_w_out'][0], f32),
    }
    maps = []
    for c in range(ncores):
        s, q = c // 2, c % 2
        xp = np.asarray(inp['x_prompt'][c], f32)
        xs = np.asarray(inp['x_sample'][s], f32)
        pos_p = np.arange(TP)
        if q == 0:
            xo, xf = xs[:TS], xs[TS:]
            pos_o, pos_f = np.arange(TS), np.arange(TS, 2 * TS)
        else:
            xp = xp[::-1]
            pos_p = pos_p[::-1]
            xo, xf = xs[TS:][::-1], xs[:TS]
            pos_o, pos_f = np.arange(TS, 2 * TS)[::-1], np.arange(TS)
        x_in = np.ascontiguousarray(np.concatenate([xp, xo, xf], 0))
        pos = np.concatenate([pos_p, pos_o, pos_f]).astype(f32)
        ang = (pos[:, None] * inv_freq[None, :]).astype(f32)
        cos, sin = np.cos(ang).astype(f32), np.sin(ang).astype(f32)
        C = np.ones((128, cfg.NALL), f32)
        S = np.zeros((128, cfg.NALL), f32)
        for half in range(2):
            o = half * 64
            C[o:o + 8] = cos.T
            C[o + 8:o + 16] = cos.T
            S[o:o + 8] = -sin.T
            S[o + 8:o + 16] = sin.T
        p1, p2 = q, 1 - q
        vl = [inp['norm_mix'][0], inp['norm_mix'][1], inp['norm_ffn'][0], inp['norm_ffn'][1], inp['norm_final']]
        vl += [inp['rwkv_mu'][0][m] for m in range(6)]
        vl += [inp['rwkv_w0'][0][p1], inp['rwkv_w0'][0][p2], inp['rwkv_a0'][0][p1], inp['rwkv_a0'][0][p2]]
        vl += [inp['rwkv_k_k'][0], inp['rwkv_k_a'][0], inp['rwkv_r_k'][0], inp['rwkv_ln_w'][0], inp['rwkv_ln_b'][0]]
        vecs = np.ascontiguousarray(np.stack([_fmvec(v) for v in vl], 1))
        sel = np.zeros((128, 2), f32)
        sel[:, 1 - q] = 1.0
        w1, w2 = np.asarray(inp['rwkv_w1'][0], f32), np.asarray(inp['rwkv_w2'][0], f32)
        a1, a2 = np.asarray(inp['rwkv_a1'][0], f32), np.asarray(inp['rwkv_a2'][0], f32)
        m = dict(shared)
        m.update({
            'x_in': x_in, 'ropeC': C, 'ropeS': S, 'vecs': vecs, 'sel': sel,
            'w1s': np.ascontiguousarray(np.concatenate([w1[p1], w1[p2]], 1)),
            'w2p': _padz(w2[p1], w2[p2]),
            'a1s': np.ascontiguousarray(np.concatenate([a1[p1], a1[p2]], 1)),
            'a2p': _padz(a2[p1], a2[p2]),
        })
        maps.append(m)
    return maps


_CACHE = {}


def kernel(**inputs):
    cfg = Cfg(inputs['x_prompt'].shape[1], inputs['x_sample'].shape[1] // 2)
    key = (cfg.TP, cfg.TS)
    if key not in _CACHE:
        _CACHE[key] = build(cfg)
    nc, k = _CACHE[key]
    maps = prep_inputs(inputs, cfg)
    res = run_bass_kernel_spmd(nc, maps, core_ids=list(range(8)))
    B, S = inputs['x_prompt'].shape[0], inputs['x_prompt'].shape[1]
    DB, DS = inputs['x_sample'].shape[0], inputs['x_sample'].shape[1]
    yp = np.zeros((B, S, D), np.float32)
    ys = np.zeros((DB, DS, D), np.float32)
    for c in range(8):
        y = res.results[c]['y_out']
        s, q = c // 2, c % 2
        a, b = y[:cfg.TP], y[cfg.TP:]
        if q == 0:
            yp[c] = a
            ys[s, :cfg.TS] = b
        else:
            yp[c] = a[::-1]
            ys[s, cfg.TS:] = b[::-1]
    return yp, ys
```
